# Optimizing a Trainium2 kernel written in Bass

```python
import jax
import jax.numpy as jnp
from jax import lax
import numpy as np

D_MODEL = 1024
BATCH = 16
SEQ = 256
DEPTH = 4
DEC_BATCH = 2
DEC_SEQ = 4096
PAST_LEN = 256

GRID_W = 64
N_EVEN = (DEPTH + 1) // 2
N_ODD = DEPTH // 2
EPS = 1e-6
ROPE_BASE = 10000.0
Q_BLOCK = 128
F32 = jnp.float32

MLA_HEADS = 8
MLA_NOPE = 64
MLA_ROPE = 32
MLA_QK = MLA_NOPE + MLA_ROPE
MLA_V = 64
Q_LORA = 384
KV_LORA = 256

GDN_HEADS = 8
GDN_DK = 64
GDN_DV = 64
GDN_CONV = 3
GDN_CHUNK = 64

MLSTM_HEADS = 8
MLSTM_DK = 64
MLSTM_DV = 128
MLSTM_CHUNK = 64

D_FF = -(-8 * D_MODEL // (3 * 256)) * 256

EVEN_SIZES = (Q_LORA, KV_LORA, MLA_ROPE, GDN_HEADS * GDN_DK, GDN_HEADS * GDN_DK,
              GDN_HEADS * GDN_DV, GDN_HEADS * GDN_DV, 2 * GDN_HEADS, 2 * GDN_HEADS)
EVEN_IN = sum(EVEN_SIZES)
EVEN_MIX = MLA_HEADS * MLA_V + GDN_HEADS * GDN_DV
GDN_CONV_CH = GDN_HEADS * (2 * GDN_DK + GDN_DV)
ODD_SIZES = (MLSTM_HEADS * MLSTM_DK, MLSTM_HEADS * MLSTM_DK, MLSTM_HEADS * MLSTM_DV,
             MLSTM_HEADS * MLSTM_DV, 4 * MLSTM_HEADS)
ODD_IN = sum(ODD_SIZES)
ODD_MIX = MLSTM_HEADS * MLSTM_DV

kernel_name = 'hybrid_mla_gdn_mlstm_diffusion_step'


def split_last(x, sizes):
    return jnp.split(x, [int(i) for i in np.cumsum(sizes)[:-1]], axis=-1)


def rms_norm(x, w):
    xf = x.astype(F32)
    y = xf * lax.rsqrt(jnp.mean(xf * xf, axis=-1, keepdims=True) + EPS)
    return (y * w.astype(F32)).astype(x.dtype)


def l2_norm(x):
    xf = x.astype(F32)
    return (xf * lax.rsqrt(jnp.sum(xf * xf, axis=-1, keepdims=True) + EPS)).astype(x.dtype)


def flip_t(a):
    return jnp.flip(a, axis=1)


def axial_rope_table(n_tokens, dtype):
    rows = n_tokens // GRID_W
    row = jnp.repeat(jnp.arange(rows, dtype=F32), GRID_W)
    col = jnp.tile(jnp.arange(GRID_W, dtype=F32), rows)
    per_axis = MLA_ROPE // 2
    inv = 1.0 / (ROPE_BASE ** (jnp.arange(0, per_axis, 2, dtype=F32) / per_axis))
    ar = row[:, None] * inv
    ac = col[:, None] * inv
    ang = jnp.concatenate([ar, ar, ac, ac], axis=-1)
    return jnp.cos(ang).astype(dtype), jnp.sin(ang).astype(dtype)


def apply_axial_rope(x, cos, sin):
    a = x.reshape(*x.shape[:-1], 2, 2, MLA_ROPE // 4)
    rot = jnp.concatenate([-a[..., 1:, :], a[..., :1, :]], axis=-2).reshape(x.shape)
    return x * cos[:, None, :] + rot * sin[:, None, :]


def rope_tail(x, cos, sin):
    return jnp.concatenate([x[..., :MLA_NOPE], apply_axial_rope(x[..., MLA_NOPE:], cos, sin)], axis=-1)


def adaln(cond, w, b):
    mod = jax.nn.silu(cond) @ w + b
    return jnp.split(mod[:, None, :], 6, axis=-1)


def swiglu(h, w_in, w_out):
    gate, up = jnp.split(h @ w_in, 2, axis=-1)
    return (jax.nn.silu(gate) * up) @ w_out


def short_conv(x, w):
    ch = x.shape[-1]
    return lax.conv_general_dilated(x, w[:, None, :].astype(x.dtype), window_strides=(1,),
                                    padding=[(GDN_CONV // 2, GDN_CONV // 2)],
                                    dimension_numbers=('NWC', 'WIO', 'NWC'), feature_group_count=ch)


def block_attention(q, k, v):
    B, Tq, H, Dk = q.shape
    Dv = v.shape[-1]
    nb = Tq // Q_BLOCK
    scale = Dk ** -0.5
    qb = jnp.moveaxis(q.reshape(B, nb, Q_BLOCK, H, Dk), 1, 0)

    def one_block(q_blk):
        s = jnp.einsum('bqhd,bkhd->bhqk', q_blk, k).astype(F32) * scale
        p = jax.nn.softmax(s, axis=-1).astype(v.dtype)
        return jnp.einsum('bhqk,bkhd->bqhd', p, v)

    o = lax.map(one_block, qb)
    return jnp.moveaxis(o, 0, 1).reshape(B, Tq, H, Dv)


def mla_expand(ckv, krope, w_ukv, k_norm):
    B, T, _ = ckv.shape
    kv = (ckv @ w_ukv).reshape(B, T, MLA_HEADS, MLA_NOPE + MLA_V)
    k_nope, v = kv[..., :MLA_NOPE], kv[..., MLA_NOPE:]
    k = jnp.concatenate([k_nope, jnp.broadcast_to(krope[:, :, None, :], (B, T, MLA_HEADS, MLA_ROPE))], axis=-1)
    return rms_norm(k, k_norm), v


def chunk_major(a, n_chunks, length):
    B, _, H = a.shape[:3]
    a = a.astype(F32).reshape(B, n_chunks, length, H, *a.shape[3:])
    return jnp.moveaxis(jnp.moveaxis(a, 3, 2), 1, 0)


def gated_delta_scan(q, k, v, g, beta, s0):
    B, T, H, DK = q.shape
    DV = v.shape[-1]
    L = GDN_CHUNK
    nc = T // L
    qc, kc, vc, gc_in, bc = (chunk_major(a, nc, L) for a in (q, k, v, g, beta))
    qc = qc * (DK ** -0.5)
    gc = jnp.cumsum(gc_in, axis=-1)
    tri = jnp.tril(jnp.ones((L, L), dtype=bool))
    strict = jnp.tril(jnp.ones((L, L), dtype=bool), -1)
    decay = jnp.exp(jnp.where(tri, gc[..., :, None] - gc[..., None, :], -jnp.inf))
    kb = kc * bc[..., None]
    a_mat = jnp.where(strict, jnp.einsum('cbhid,cbhjd->cbhij', kb, kc) * decay, 0.0)
    eye = jnp.eye(L, dtype=F32)
    t_mat = lax.linalg.triangular_solve(a_mat + eye, jnp.broadcast_to(eye, a_mat.shape),
                                        left_side=True, lower=True)
    u = jnp.einsum('cbhij,cbhjv->cbhiv', t_mat, vc * bc[..., None])
    w = jnp.einsum('cbhij,cbhjd->cbhid', t_mat, kb * jnp.exp(gc)[..., None])
    attn = jnp.einsum('cbhid,cbhjd->cbhij', qc, kc) * decay
    q_dec = qc * jnp.exp(gc)[..., None]
    k_dec = kc * jnp.exp(gc[..., -1:] - gc)[..., None]
    g_last = jnp.exp(gc[..., -1])

    def step(s, inp):
        u_i, w_i, a_i, qd_i, kd_i, gl_i = inp
        v_new = u_i - jnp.einsum('bhld,bhdv->bhlv', w_i, s)
        o = jnp.einsum('bhld,bhdv->bhlv', qd_i, s) + jnp.einsum('bhls,bhsv->bhlv', a_i, v_new)
        s = gl_i[..., None, None] * s + jnp.einsum('bhld,bhlv->bhdv', kd_i, v_new)
        return s, o

    s, o = lax.scan(step, s0.astype(F32), (u, w, attn, q_dec, k_dec, g_last))
    o = jnp.moveaxis(jnp.moveaxis(o, 0, 1), 2, 3).reshape(B, T, H, DV)
    return o.astype(v.dtype), s


def mlstm_scan(q, k, v, log_i, log_f, c0, n0, m0):
    B, T, H, DK = q.shape
    DV = v.shape[-1]
    L = MLSTM_CHUNK
    nc = T // L
    qc, kc, vc, ic, fc = (chunk_major(a, nc, L) for a in (q, k, v, log_i, log_f))
    tri = jnp.tril(jnp.ones((L, L), dtype=bool))
    b = jnp.cumsum(fc, axis=-1)
    d_log = jnp.where(tri, b[..., :, None] - b[..., None, :] + ic[..., None, :], -jnp.inf)
    d_max = jnp.max(d_log, axis=-1)
    qk = jnp.einsum('cbhld,cbhsd->cbhls', qc, kc)
    g_log = b[..., -1:] - b + ic
    g_max = jnp.max(g_log, axis=-1)

    def step(carry, inp):
        cm, nv, m = carry
        q_i, k_i, v_i, b_i, dl_i, dm_i, qk_i, gl_i, gm_i = inp
        inter = b_i + m[..., None]
        m_t = jnp.maximum(inter, dm_i)
        wts = jnp.exp(dl_i - m_t[..., None]) * qk_i
        a = jnp.exp(inter - m_t)
        num = jnp.einsum('bhls,bhsv->bhlv', wts, v_i) + a[..., None] * jnp.einsum('bhld,bhdv->bhlv', q_i, cm)
        den = jnp.sum(wts, axis=-1) + a * jnp.einsum('bhld,bhd->bhl', q_i, nv)
        h = num / jnp.maximum(jnp.abs(den), jnp.exp(-m_t))[..., None]
        b_last = b_i[..., -1]
        m_new = jnp.maximum(b_last + m, gm_i)
        ws = jnp.exp(gl_i - m_new[..., None])
        dec = jnp.exp(b_last + m - m_new)
        cm = dec[..., None, None] * cm + jnp.einsum('bhs,bhsd,bhsv->bhdv', ws, k_i, v_i)
        nv = dec[..., None] * nv + jnp.einsum('bhs,bhsd->bhd', ws, k_i)
        return (cm, nv, m_new), h

    init = (c0.astype(F32), n0.astype(F32), m0.astype(F32))
    (cm, nv, m), h = lax.scan(step, init, (qc, kc, vc, b, d_log, d_max, qk, g_log, g_max))
    h = jnp.moveaxis(jnp.moveaxis(h, 0, 1), 2, 3).reshape(B, T, H, DV)
    return h.astype(v.dtype), (cm, nv, m)


def even_mixer(h, w_in, w_out, q_a_norm, kv_a_norm, w_uq, w_ukv, q_norm, k_norm,
               conv_w, a_log, dt_bias, out_norm, rope, ctx):
    B, T, _ = h.shape
    cq, ckv, krope, gq, gk, gv, gz, ga, gb = split_last(h @ w_in, EVEN_SIZES)
    ckv = rms_norm(ckv, kv_a_norm)
    q = rms_norm((rms_norm(cq, q_a_norm) @ w_uq).reshape(B, T, MLA_HEADS, MLA_QK), q_norm)
    k, v = mla_expand(ckv, krope, w_ukv, k_norm)
    if rope is not None:
        cos, sin = rope
        q = rope_tail(q, cos, sin)
        k = rope_tail(k, cos, sin)
    if ctx is not None:
        ckv_ctx, krope_ctx, s0 = ctx
        k_ctx, v_ctx = mla_expand(ckv_ctx, krope_ctx, w_ukv, k_norm)
        k = jnp.concatenate([k_ctx, k], axis=1)
        v = jnp.concatenate([v_ctx, v], axis=1)
    else:
        s0 = jnp.zeros((B, 2, GDN_HEADS, GDN_DK, GDN_DV), F32)
    attn_out = block_attention(q, k, v).reshape(B, T, MLA_HEADS * MLA_V)
    qkv = jax.nn.silu(short_conv(jnp.concatenate([gq, gk, gv], axis=-1), conv_w))
    gq, gk, gv = split_last(qkv, (GDN_HEADS * GDN_DK, GDN_HEADS * GDN_DK, GDN_HEADS * GDN_DV))
    gq = l2_norm(gq.reshape(B, T, GDN_HEADS, GDN_DK))
    gk = l2_norm(gk.reshape(B, T, GDN_HEADS, GDN_DK))
    gv = gv.reshape(B, T, GDN_HEADS, GDN_DV)
    log_decay = -jnp.exp(a_log.astype(F32)) * jax.nn.softplus(
        ga.reshape(B, T, 2, GDN_HEADS).astype(F32) + dt_bias.astype(F32))
    beta = jax.nn.sigmoid(gb.reshape(B, T, 2, GDN_HEADS).astype(F32))
    o_f, s_f = gated_delta_scan(gq, gk, gv, log_decay[:, :, 0], beta[:, :, 0], s0[:, 0])
    o_b, s_b = gated_delta_scan(flip_t(gq), flip_t(gk), flip_t(gv), flip_t(log_decay[:, :, 1]),
                                flip_t(beta[:, :, 1]), s0[:, 1])
    delta_out = rms_norm(o_f + flip_t(o_b), out_norm) * jax.nn.silu(gz.reshape(B, T, GDN_HEADS, GDN_DV))
    out = jnp.concatenate([attn_out, delta_out.reshape(B, T, GDN_HEADS * GDN_DV)], axis=-1) @ w_out
    if ctx is None:
        return out, (ckv, krope, jnp.stack([s_f, s_b], axis=1).astype(h.dtype))
    return out, None


def odd_mixer(h, w_in, w_out, gate_bias, out_norm, ctx):
    B, T, _ = h.shape
    q, k, v, o, gates = split_last(h @ w_in, ODD_SIZES)
    q = q.reshape(B, T, MLSTM_HEADS, MLSTM_DK)
    k = k.reshape(B, T, MLSTM_HEADS, MLSTM_DK) * (MLSTM_DK ** -0.5)
    v = v.reshape(B, T, MLSTM_HEADS, MLSTM_DV)
    gates = gates.reshape(B, T, 4, MLSTM_HEADS).astype(F32) + gate_bias.astype(F32)
    log_i = gates[:, :, 0:2]
    log_f = jax.nn.log_sigmoid(gates[:, :, 2:4])
    if ctx is None:
        c0 = jnp.zeros((B, 2, MLSTM_HEADS, MLSTM_DK, MLSTM_DV), F32)
        n0 = jnp.zeros((B, 2, MLSTM_HEADS, MLSTM_DK), F32)
        m0 = jnp.zeros((B, 2, MLSTM_HEADS), F32)
    else:
        c0, n0, m0 = ctx
    h_f, st_f = mlstm_scan(q, k, v, log_i[:, :, 0], log_f[:, :, 0], c0[:, 0], n0[:, 0], m0[:, 0])
    h_b, st_b = mlstm_scan(flip_t(q), flip_t(k), flip_t(v), flip_t(log_i[:, :, 1]), flip_t(log_f[:, :, 1]),
                           c0[:, 1], n0[:, 1], m0[:, 1])
    mem = rms_norm(h_f + flip_t(h_b), out_norm.reshape(MLSTM_HEADS, MLSTM_DV))
    out = (mem * jax.nn.sigmoid(o.reshape(B, T, MLSTM_HEADS, MLSTM_DV))).reshape(B, T, ODD_MIX) @ w_out
    if ctx is None:
        new_c = jnp.stack([st_f[0], st_b[0]], axis=1).astype(h.dtype)
        new_n = jnp.stack([st_f[1], st_b[1]], axis=1).astype(h.dtype)
        new_m = jnp.stack([st_f[2], st_b[2]], axis=1).astype(h.dtype)
        return out, (new_c, new_n, new_m)
    return out, None


def setup_inputs(seed: int = 0) -> dict:
    key = jax.random.key(seed)
    it = iter(jax.random.split(key, 64))

    def nrm(shape, scale):
        return jax.random.normal(next(it), shape, F32) * scale

    def gain(shape):
        return 1.0 + nrm(shape, 0.02)

    a_log = jnp.log(jax.random.uniform(next(it), (N_EVEN, 2, GDN_HEADS), F32, 1.0, 16.0))
    dt = jnp.exp(jax.random.uniform(next(it), (N_EVEN, 2, GDN_HEADS), F32, float(np.log(1e-3)), float(np.log(1e-1))))
    dt_bias = dt + jnp.log(-jnp.expm1(-dt))
    i_bias = nrm((N_ODD, 2, MLSTM_HEADS), 0.1)
    f_bias = 3.0 + jax.random.uniform(next(it), (N_ODD, 2, MLSTM_HEADS), F32, 0.0, 3.0)
    return {
        'x_prompt': nrm((BATCH, SEQ, D_MODEL), 1.0),
        'x_sample': nrm((DEC_BATCH, DEC_SEQ, D_MODEL), 1.0),
        'c': nrm((DEC_BATCH, D_MODEL), 1.0),
        'c_ctx': nrm((D_MODEL,), 1.0),
        'cache_mla_ckv': nrm((DEC_BATCH, N_EVEN, PAST_LEN, KV_LORA), 1.0),
        'cache_mla_krope': nrm((DEC_BATCH, N_EVEN, PAST_LEN, MLA_ROPE), 1.0),
        'state_gdn': nrm((DEC_BATCH, N_EVEN, 2, GDN_HEADS, GDN_DK, GDN_DV), 0.3),
        'state_mlstm_C': nrm((DEC_BATCH, N_ODD, 2, MLSTM_HEADS, MLSTM_DK, MLSTM_DV), 0.3),
        'state_mlstm_n': nrm((DEC_BATCH, N_ODD, 2, MLSTM_HEADS, MLSTM_DK), 0.3),
        'state_mlstm_m': nrm((DEC_BATCH, N_ODD, 2, MLSTM_HEADS), 0.5),
        'norm_mix': gain((DEPTH, D_MODEL)),
        'norm_ffn': gain((DEPTH, D_MODEL)),
        'w_ada': nrm((DEPTH, D_MODEL, 6 * D_MODEL), 0.5 * D_MODEL ** -0.5),
        'b_ada': nrm((DEPTH, 6 * D_MODEL), 0.02),
        'w_ffn_in': nrm((DEPTH, D_MODEL, 2 * D_FF), D_MODEL ** -0.5),
        'w_ffn_out': nrm((DEPTH, D_FF, D_MODEL), D_FF ** -0.5),
        'w_even_in': nrm((N_EVEN, D_MODEL, EVEN_IN), D_MODEL ** -0.5),
        'w_even_out': nrm((N_EVEN, EVEN_MIX, D_MODEL), EVEN_MIX ** -0.5),
        'mla_q_a_norm': gain((N_EVEN, Q_LORA)),
        'mla_kv_a_norm': gain((N_EVEN, KV_LORA)),
        'w_mla_uq': nrm((N_EVEN, Q_LORA, MLA_HEADS * MLA_QK), Q_LORA ** -0.5),
        'w_mla_ukv': nrm((N_EVEN, KV_LORA, MLA_HEADS * (MLA_NOPE + MLA_V)), KV_LORA ** -0.5),
        'mla_q_norm': gain((N_EVEN, MLA_QK)),
        'mla_k_norm': gain((N_EVEN, MLA_QK)),
        'gdn_conv': nrm((N_EVEN, GDN_CONV, GDN_CONV_CH), GDN_CONV ** -0.5),
        'gdn_a_log': a_log,
        'gdn_dt_bias': dt_bias,
        'gdn_out_norm': gain((N_EVEN, GDN_DV)),
        'w_odd_in': nrm((N_ODD, D_MODEL, ODD_IN), D_MODEL ** -0.5),
        'w_odd_out': nrm((N_ODD, ODD_MIX, D_MODEL), ODD_MIX ** -0.5),
        'mlstm_gate_bias': jnp.concatenate([i_bias, f_bias], axis=1),
        'mlstm_out_norm': gain((N_ODD, MLSTM_HEADS * MLSTM_DV)),
    }


def reference(x_prompt, x_sample, c, c_ctx,
              cache_mla_ckv, cache_mla_krope, state_gdn, state_mlstm_C, state_mlstm_n, state_mlstm_m,
              norm_mix, norm_ffn, w_ada, b_ada, w_ffn_in, w_ffn_out,
              w_even_in, w_even_out, mla_q_a_norm, mla_kv_a_norm, w_mla_uq, w_mla_ukv,
              mla_q_norm, mla_k_norm, gdn_conv, gdn_a_log, gdn_dt_bias, gdn_out_norm,
              w_odd_in, w_odd_out, mlstm_gate_bias, mlstm_out_norm):
    rope_lat = axial_rope_table(x_sample.shape[1], x_sample.dtype)

    def layer(x, cond, l, rope, ctx):
        j = l // 2
        sh1, sc1, g1, sh2, sc2, g2 = adaln(cond, w_ada[l], b_ada[l])
        h = rms_norm(x, norm_mix[l]) * (1.0 + sc1) + sh1
        if l % 2 == 0:
            out, ctx_new = even_mixer(h, w_even_in[j], w_even_out[j], mla_q_a_norm[j], mla_kv_a_norm[j],
                                      w_mla_uq[j], w_mla_ukv[j], mla_q_norm[j], mla_k_norm[j],
                                      gdn_conv[j], gdn_a_log[j], gdn_dt_bias[j], gdn_out_norm[j], rope, ctx)
        else:
            out, ctx_new = odd_mixer(h, w_odd_in[j], w_odd_out[j], mlstm_gate_bias[j], mlstm_out_norm[j], ctx)
        x = x + g1 * out
        h = rms_norm(x, norm_ffn[l]) * (1.0 + sc2) + sh2
        x = x + g2 * swiglu(h, w_ffn_in[l], w_ffn_out[l])
        return x, ctx_new

    xp, xs = x_prompt, x_sample
    cond_ctx = c_ctx[None, :]
    ckv_new, krope_new, gdn_new, mc_new, mn_new, mm_new = [], [], [], [], [], []
    for l in range(DEPTH):
        j = l // 2
        xp, ctx_p = layer(xp, cond_ctx, l, None, None)
        if l % 2 == 0:
            ckv_new.append(ctx_p[0])
            krope_new.append(ctx_p[1])
            gdn_new.append(ctx_p[2])
            ctx_s = (cache_mla_ckv[:, j], cache_mla_krope[:, j], state_gdn[:, j])
        else:
            mc_new.append(ctx_p[0])
            mn_new.append(ctx_p[1])
            mm_new.append(ctx_p[2])
            ctx_s = (state_mlstm_C[:, j], state_mlstm_n[:, j], state_mlstm_m[:, j])
        xs, _ = layer(xs, c, l, rope_lat, ctx_s)
    return (xp, xs, jnp.stack(ckv_new, axis=1), jnp.stack(krope_new, axis=1), jnp.stack(gdn_new, axis=1),
            jnp.stack(mc_new, axis=1), jnp.stack(mn_new, axis=1), jnp.stack(mm_new, axis=1))
```

```python
import numpy as np
from contextlib import ExitStack
import concourse.bass as bass
import concourse.mybir as mybir
from concourse.bass_utils import run_bass_kernel_spmd

F32 = mybir.dt.float32
BF16 = mybir.dt.bfloat16
AF = mybir.ActivationFunctionType
ALU = mybir.AluOpType
AX = mybir.AxisListType

NCORES = 8
D = 1024
DEPTH = 4
DFF = 2816
NJ = 22
NTOK = 1536
NT = 3
EPS = 1e-6
MIXERS = True
NLAYERS = DEPTH
DEBUG = False
ONLY_ADA = False
FFN = True
LAYERS = None


class R:
    __slots__ = ("w", "rd")

    def __init__(self):
        self.w = None
        self.rd = {}


class K:
    NDMA = 24

    def __init__(self, nc, stack):
        self.nc = nc
        self.b = {"pe": nc.tensor, "act": nc.scalar, "dve": nc.vector, "pool": nc.gpsimd, "sp": nc.sync}
        self.sem, self.cnt, self.seen = {}, {}, {}
        for e in self.b:
            self.sem[e] = stack.enter_context(nc.semaphore("s_" + e))
            self.cnt[e] = 0
            self.seen[e] = {}
        self.dsem = [stack.enter_context(nc.semaphore("d%d" % i)) for i in range(self.NDMA)]
        self.dval = [0] * self.NDMA
        self.dnext = 0
        self.ccsem = stack.enter_context(nc.semaphore("cc"))
        self.ccval = 0
        self.stack = stack
        self.n_ins = 0
        self.uid = 0

    def sb(self, shape, dt=F32, st=None, name=None):
        self.uid += 1
        return (st or self.stack).enter_context(self.nc.sbuf_tensor("%s%d" % (name or "t", self.uid), list(shape), dt))

    def ps(self, shape, dt=F32, st=None, name=None):
        self.uid += 1
        return (st or self.stack).enter_context(self.nc.psum_tensor("%s%d" % (name or "p", self.uid), list(shape), dt))

    def _semobj(self, key):
        if isinstance(key, str):
            return self.ccsem if key == "cc" else self.sem[key]
        return self.dsem[key]

    def _wait(self, eng, deps):
        seen = self.seen[eng]
        for key, val in deps.items():
            if eng == "pe" and key == "pe":
                continue
            if seen.get(key, 0) < val:
                self.b[eng].wait_ge(self._semobj(key), val)
                seen[key] = val

    @staticmethod
    def _deps(reads, writes):
        deps = {}

        def add(kv):
            if kv is not None and deps.get(kv[0], 0) < kv[1]:
                deps[kv[0]] = kv[1]
        for r in reads:
            add(r.w)
        for w in writes:
            add(w.w)
            for kv in w.rd.items():
                add(kv)
        return deps

    def I(self, eng, reads, writes, emit, inc=True):
        self._wait(eng, self._deps(reads, writes))
        ins = emit(self.b[eng])
        self.n_ins += 1
        val = self.cnt[eng] + 1
        if inc:
            ins.then_inc(self.sem[eng], 1)
            self.cnt[eng] = val
        for w in writes:
            w.w = (eng, val)
            w.rd = {}
        for r in reads:
            if r.rd.get(eng, 0) < val:
                r.rd[eng] = val
        return ins

    def dma(self, q, out, in_, reads, writes, **kw):
        deps = self._deps(reads, writes)
        s = self.dnext
        self.dnext = (s + 1) % self.NDMA
        if self.dval[s] > 0:
            deps[s] = max(deps.get(s, 0), self.dval[s])
        self._wait(q, deps)
        ins = self.b[q].dma_start(out=out, in_=in_, **kw)
        self.dval[s] += 16
        ins.then_inc(self.dsem[s], 16)
        self.n_ins += 1
        for w in writes:
            w.w = (s, self.dval[s])
            w.rd = {}
        for r in reads:
            r.rd[s] = self.dval[s]
        return ins

    def cc(self, kind, op, groups, in_ap, out_ap, reads, writes):
        deps = self._deps(reads, writes)
        if self.ccval > 0:
            deps["cc"] = self.ccval
        self._wait("pool", deps)
        ins = self.b["pool"].collective_compute(kind, op, replica_groups=groups, ins=[in_ap], outs=[out_ap])
        self.ccval += 1
        ins.then_inc(self.ccsem, 1)
        for w in writes:
            w.w = ("cc", self.ccval)
            w.rd = {}
        for r in reads:
            r.rd["cc"] = self.ccval
        return ins

    def barrier(self):
        deps = {e: c for e, c in self.cnt.items() if c > 0}
        for s, v in enumerate(self.dval):
            if v > 0:
                deps[s] = v
        if self.ccval > 0:
            deps["cc"] = self.ccval
        for e in self.b:
            self._wait(e, dict(deps))

    def finish(self, regions):
        deps = {}
        for r in regions:
            if r.w is not None:
                deps[r.w[0]] = max(deps.get(r.w[0], 0), r.w[1])
        self._wait("sp", deps)


class Rot:
    def __init__(self, k, n, shape, dt, st=None, psum=False):
        self.t = [(k.ps if psum else k.sb)(shape, dt, st=st) for _ in range(n)]
        self.r = [R() for _ in range(n)]
        self.i = 0

    def next(self):
        i = self.i
        self.i = (i + 1) % len(self.t)
        return self.t[i], self.r[i]


def build_program():
    nc = bass.Bass("TRN2", target_bir_lowering=False)

    def din(name, shape):
        return nc.dram_tensor(name, list(shape), F32, kind="ExternalInput").ap()

    def dout(name, shape):
        return nc.dram_tensor(name, list(shape), F32, kind="ExternalOutput").ap()

    xT_d = din("xT", [128, 8, NTOK])
    cond_d = din("condT", [128, 8, 2])
    wada_d = din("w_ada", [DEPTH, 128, 8, 6 * D])
    bada_d = din("b_adaT", [128, DEPTH, 48])
    nm_d = din("nmT", [128, DEPTH, 8])
    nf_d = din("nfT", [128, DEPTH, 8])
    wfi_d = din("w_ffn_in", [DEPTH, 128, NJ, 8, 256])
    wfo_d = din("w_ffn_out", [DEPTH, 128, NJ, D])
    yT_d = dout("yT", [128, 8, NTOK])
    OD = {
        "p": dict(H=8, NTk=512, nseq=2, nchs=4,
                  w_in=din("wo_in_p", [2, 128, 8, 388 * 8]), w_out=din("wo_out_p", [2, 128, 8, D]),
                  gb=din("ogb_p", [2, 64, 32]), onorm=din("onorm_p", [2, 128, 8])),
        "s": dict(H=2, NTk=4096, nseq=1, nchs=64,
                  w_in=din("wo_in_s", [2, 128, 8, 388 * 2]), w_out=din("wo_out_s", [2, 128, 2, D]),
                  gb=din("ogb_s", [2, 64, 8]), onorm=din("onorm_s", [2, 128, 2]),
                  C0=din("oC0_s", [2, 64, 2, 2, 1, 129]), m0=din("om0_s", [2, 1, 2, 2, 1])),
    }
    oC_out = dout("oC_out", [2, 64, 2, 8, 2, 129])
    om_out = dout("om_out", [2, 1, 2, 8, 2])

    def ev_decl(sfx, H, extra):
        WE = 832 + 260 * H
        d_ = dict(H=H, w_in=din("we_in_" + sfx, [2, 128, 8, WE]), w_uq=din("we_uq_" + sfx, [2, 128, 3, H, 192]),
                  w_uk=din("we_uk_" + sfx, [2, 128, 2, H, 64]), w_uv=din("we_uv_" + sfx, [2, 128, 2, H * 64]),
                  w_out=din("we_out_" + sfx, [2, 64, 2, H, D]), conv=din("we_conv_" + sfx, [2, 64, 3, H, 3]),
                  alog=din("we_alog_" + sfx, [2, 64, 2 * H]), dtb=din("we_dtb_" + sfx, [2, 64, 2 * H]))
        d_.update(extra)
        return d_
    EVN = dict(qan=din("we_qan", [2, 128, 3]), kvan=din("we_kvan", [2, 128, 2]), qn=din("we_qn", [2, 96, 2]),
               kn=din("we_kn", [2, 96, 2]), onorm=din("we_onorm", [2, 64, 1]))
    EV = {
        "p": ev_decl("p", 8, dict(NTk=512, nseq=2, nchs=4)),
        "s": ev_decl("s", 2, dict(NTk=4096, nseq=1, nchs=64, ckv_ctx=din("we_ckvctx", [2, 128, 2, 256]),
                                  kr_ctx=din("we_krctx", [2, 96, 256]), S0=din("we_S0", [2, 64, 2, 2, 1, 64]),
                                  cos=din("rope_cos", [96, 4096]), sin=din("rope_sin", [96, 4096]))),
    }
    ckv_out = dout("ckv_out", [2, 128, 2, 512])
    kr_out = dout("kr_out", [2, 32, 512])
    gdn_out = dout("gdn_out", [2, 64, 2, 8, 2, 64])
    xpre_p = nc.dram_tensor("xpre_p", [24, 64, 2, 258], F32)
    xpre_s = nc.dram_tensor("xpre_s", [6, 64, 1, 4098], F32)
    ag_in = nc.dram_tensor("ag_in", [4, 256, 512], F32)
    ag_out = nc.dram_tensor("ag_out", [4, 1024, 512], F32)
    rs_in = nc.dram_tensor("rs_in", [4096, 1024], F32)
    rs_out = nc.dram_tensor("rs_out", [1024, 1024], F32)
    GROUPS = [[0, 1, 2, 3], [4, 5, 6, 7]]
    dbg = {}

    def dump(k, name, ap, shape, reads):
        if not DEBUG:
            return
        d = dout("dbg_" + name, shape)
        r = R()
        k.dma("sp", d, ap, reads, [r])
        dbg[name] = r

    with ExitStack() as st:
        k = K(nc, st)
        xT = k.sb([128, 8, NTOK], F32, name="xT")
        rx = [R() for _ in range(NT)]
        for t in range(NT):
            k.dma("sp", xT[:, :, t * 512:(t + 1) * 512], xT_d[:, :, t * 512:(t + 1) * 512], [], [rx[t]])
        ones_bf = k.sb([128, 128], BF16, name="ones")
        r_const = R()
        k.I("pool", [], [r_const], lambda e: e.memset(ones_bf[:], 1.0))
        modA = k.sb([128, DEPTH, 2, 8, 2], F32, name="modA")
        mod = k.sb([128, DEPTH, 48, 2], F32, name="mod")
        r_mod = R()
        nm = k.sb([128, DEPTH, 8], F32, name="nm")
        nf = k.sb([128, DEPTH, 8], F32, name="nf")
        bada = k.sb([128, DEPTH, 48], F32, name="bada")
        r_small = R()
        k.dma("sp", nm[:], nm_d, [], [r_small])
        k.dma("sp", nf[:], nf_d, [], [r_small])
        k.dma("sp", bada[:], bada_d, [], [r_small])

        with ExitStack() as ph:
            cond = k.sb([128, 8, 2], F32, st=ph)
            r_c = R()
            k.dma("sp", cond[:], cond_d, [], [r_c])
            k.I("act", [r_c], [r_c], lambda e: e.activation(out=cond[:], in_=cond[:], func=AF.Silu))
            dump(k, "cond", cond[:], [128, 8, 2], [r_c])
            wst = Rot(k, 2, [128, 8, 512], F32, st=ph)
            pmod = Rot(k, 2, [128, 48, 2], F32, st=ph, psum=True)
            for l in range(DEPTH):
                pm, rpm = pmod.next()
                for blk in range(12):
                    wt, rw = wst.next()
                    k.dma("pool" if blk % 2 else "sp", wt[:], wada_d[l, :, :, blk * 512:(blk + 1) * 512], [], [rw])
                    for mm in range(4):
                        m = blk * 4 + mm
                        for kc in range(8):
                            k.I("pe", [rw, r_c], [rpm], lambda e: e.matmul(
                                pm[:, m, :], lhsT=wt[:, kc, mm * 128:(mm + 1) * 128], rhs=cond[:, kc, :],
                                start=(kc == 0), stop=(kc == 7)), inc=(kc == 7 and mm == 3))
                k.I("dve", [rpm, r_small], [r_mod], lambda e: e.tensor_tensor(
                    out=mod[:, l], in0=pm[:], in1=bada[:, l, :].unsqueeze(2).to_broadcast([128, 48, 2]), op=ALU.add))
                for which, (nw, part) in enumerate(((nm, 1), (nf, 4))):
                    k.I("dve", [r_mod, r_small], [r_mod], lambda e: e.scalar_tensor_tensor(
                        out=modA[:, l, which], in0=mod[:, l, part * 8:(part + 1) * 8, :], scalar=1.0,
                        in1=nw[:, l, :].unsqueeze(2).to_broadcast([128, 8, 2]), op0=ALU.add, op1=ALU.mult))
            dump(k, "mod", mod[:], [128, DEPTH, 48, 2], [r_mod])
            dump(k, "modA", modA[:], [128, DEPTH, 2, 8, 2], [r_mod])
            k.barrier()

        def mvec(l, part, c, cond_i):
            return mod[:, l, part * 8 + c, cond_i:cond_i + 1]

        def norm_mod(ph, l, which, t, out_fn, rh, ps_ss, rps):
            cond_i = 0 if t == 0 else 1
            part_b = 0 if which == 0 else 3
            sl = slice(t * 512, (t + 1) * 512)
            sq, rsq = ph["sq"].next()
            k.I("act", [rx[t]], [rsq], lambda e: e.activation(out=sq[:], in_=xT[:, :, sl], func=AF.Square))
            for c in range(8):
                k.I("pe", [rsq, r_const], [rps], lambda e: e.matmul(
                    ps_ss[:], lhsT=ones_bf[:], rhs=sq[:, c, :], start=(c == 0), stop=(c == 7)), inc=(c == 7))
            rs, rrs = ph["rs"].next()
            k.I("act", [rps], [rrs], lambda e: e.activation(out=rs[:], in_=ps_ss[:], func=AF.Sqrt, bias=eps_t[:], scale=1.0 / D))
            k.I("dve", [rrs], [rrs], lambda e: e.reciprocal(out=rs[:], in_=rs[:]))
            for c in range(8):
                tmp, rtmp = ph["tmp"].next()
                k.I("dve", [rx[t], rrs], [rtmp], lambda e: e.tensor_tensor(out=tmp[:], in0=xT[:, c, sl], in1=rs[:], op=ALU.mult))
                k.I("act", [rtmp, r_mod], [rh], lambda e: e.activation(
                    out=out_fn(c), in_=tmp[:], func=AF.Identity,
                    bias=mvec(l, part_b, c, cond_i), scale=modA[:, l, which, c, cond_i:cond_i + 1]))

        eps_t = k.sb([128, 1], F32, name="eps")
        k.I("pool", [], [r_const], lambda e: e.memset(eps_t[:], EPS))

        def ffn(l):
            with ExitStack() as ph_st:
                ph = {"sq": Rot(k, 1, [128, 8, 512], BF16, st=ph_st),
                      "rs": Rot(k, 2, [128, 512], F32, st=ph_st),
                      "tmp": Rot(k, 2, [128, 512], F32, st=ph_st)}
                hT = k.sb([128, 8, NTOK], BF16, st=ph_st)
                rh = [R() for _ in range(NT)]
                ps_ss = k.ps([128, 512], F32, st=ph_st)
                rps = R()
                for t in range(NT):
                    norm_mod(ph, l, 1, t, (lambda c, t=t: hT[:, c, t * 512:(t + 1) * 512]), rh[t], ps_ss, rps)
                if l == 0 and DEBUG:
                    hf = k.sb([128, 8, 512], F32, st=ph_st)
                    rhf = R()
                    k.I("dve", rh, [rhf], lambda e: e.tensor_copy(out=hf[:], in_=hT[:, :, 0:512]))
                    dump(k, "h", hf[:], [128, 8, 512], [rhf])
                aT = k.sb([128, 11, NTOK], BF16, st=ph_st)
                ra = [R() for _ in range(NT)]
                wst = Rot(k, 2, [128, 8, 256], F32, st=ph_st)
                wbf = Rot(k, 2, [128, 8, 256], BF16, st=ph_st)
                wost = Rot(k, 2, [128, 11, 128], F32, st=ph_st)
                wobf = Rot(k, 2, [128, 11, 128], BF16, st=ph_st)
                psg = Rot(k, 2, [128, 512], F32, st=ph_st, psum=True)
                psu = Rot(k, 2, [128, 512], F32, st=ph_st, psum=True)
                pso = Rot(k, 2, [128, 512], F32, st=ph_st, psum=True)
                sgs = Rot(k, 2, [128, 512], BF16, st=ph_st)
                for half in range(2):
                    for jj in range(11):
                        j = half * 11 + jj
                        ws, rws = wst.next()
                        k.dma("sp", ws[:], wfi_d[l, :, j], [], [rws])
                        wb, rwb = wbf.next()
                        k.I("pool", [rws], [rwb], lambda e: e.tensor_copy(out=wb[:], in_=ws[:]))
                        for t in range(NT):
                            sl = slice(t * 512, (t + 1) * 512)
                            pg, rpg = psg.next()
                            pu, rpu = psu.next()
                            for c in range(8):
                                k.I("pe", [rwb, rh[t]], [rpg], lambda e: e.matmul(
                                    pg[:], lhsT=wb[:, c, 0:128], rhs=hT[:, c, sl], start=(c == 0), stop=(c == 7)), inc=(c == 7))
                            for c in range(8):
                                k.I("pe", [rwb, rh[t]], [rpu], lambda e: e.matmul(
                                    pu[:], lhsT=wb[:, c, 128:256], rhs=hT[:, c, sl], start=(c == 0), stop=(c == 7)), inc=(c == 7))
                            sg, rsg = sgs.next()
                            k.I("act", [rpg], [rsg], lambda e: e.activation(out=sg[:], in_=pg[:], func=AF.Silu))
                            k.I("dve", [rsg, rpu], [ra[t]], lambda e: e.tensor_tensor(out=aT[:, jj, sl], in0=sg[:], in1=pu[:], op=ALU.mult))
                    for c in range(8):
                        ws, rws = wost.next()
                        k.dma("sp", ws[:], wfo_d[l, :, half * 11:(half + 1) * 11, c * 128:(c + 1) * 128], [], [rws])
                        wb, rwb = wobf.next()
                        k.I("pool", [rws], [rwb], lambda e: e.tensor_copy(out=wb[:], in_=ws[:]))
                        for t in range(NT):
                            sl = slice(t * 512, (t + 1) * 512)
                            cond_i = 0 if t == 0 else 1
                            po, rpo = pso.next()
                            for jj in range(11):
                                k.I("pe", [rwb, ra[t]], [rpo], lambda e: e.matmul(
                                    po[:], lhsT=wb[:, jj, :], rhs=aT[:, jj, sl], start=(jj == 0), stop=(jj == 10)), inc=(jj == 10))
                            k.I("dve", [rpo, r_mod, rx[t]], [rx[t]], lambda e: e.scalar_tensor_tensor(
                                out=xT[:, c, sl], in0=po[:], scalar=mvec(l, 5, c, cond_i), in1=xT[:, c, sl],
                                op0=ALU.mult, op1=ALU.add))
                k.barrier()


        ident = k.sb([128, 128], F32, name="ident")
        maskU = k.sb([64, 64], F32, name="maskU")
        maskL = k.sb([64, 64], F32, name="maskL")
        ones_f = k.sb([64, 64], F32, name="ones_f")
        k.I("pool", [], [r_const], lambda e: e.memset(ident[:], 0.0))
        k.I("pool", [r_const], [r_const], lambda e: e.affine_select(
            out=ident[:], in_=ident[:], pattern=[[-1, 128]], compare_op=ALU.not_equal, fill=1.0, base=0, channel_multiplier=1))
        k.I("pool", [], [r_const], lambda e: e.memset(ones_f[:], 1.0))
        k.I("pool", [], [r_const], lambda e: e.memset(maskU[:], 1.0))
        k.I("pool", [r_const], [r_const], lambda e: e.affine_select(
            out=maskU[:], in_=maskU[:], pattern=[[1, 64]], compare_op=ALU.is_ge, fill=0.0, base=0, channel_multiplier=-1))
        k.I("pool", [], [r_const], lambda e: e.memset(maskL[:], 1.0))
        k.I("pool", [r_const], [r_const], lambda e: e.affine_select(
            out=maskL[:], in_=maskL[:], pattern=[[-1, 64]], compare_op=ALU.is_ge, fill=0.0, base=0, channel_multiplier=1))
        dump(k, "maskU", maskU[:], [64, 64], [r_const])

        def mm(rd, wr, out, lhsT, rhs, start=True, stop=True, inc=True):
            return k.I("pe", rd, wr, lambda e: e.matmul(out, lhsT=lhsT, rhs=rhs, start=start, stop=stop), inc=inc)

        def act(rd, wr, out, in_, func, **kw):
            return k.I("act", rd, wr, lambda e: e.activation(out=out, in_=in_, func=func, **kw))

        def tt(rd, wr, out, in0, in1, op, eng="dve"):
            return k.I(eng, rd, wr, lambda e: e.tensor_tensor(out=out, in0=in0, in1=in1, op=op))

        def stt(rd, wr, out, in0, scalar, in1, op0, op1):
            return k.I("dve", rd, wr, lambda e: e.scalar_tensor_tensor(out=out, in0=in0, scalar=scalar, in1=in1, op0=op0, op1=op1))

        def tsm(rd, wr, out, in0, scalar, eng="dve"):
            return k.I(eng, rd, wr, lambda e: e.tensor_scalar_mul(out=out, in0=in0, scalar1=scalar))

        def cp(rd, wr, out, in_, eng="dve"):
            if eng == "act":
                return k.I("act", rd, wr, lambda e: e.copy(out=out, in_=in_))
            return k.I(eng, rd, wr, lambda e: e.tensor_copy(out=out, in_=in_))

        def load_w(st_, dram_ap, shape, q="sp"):
            ws = k.sb(shape, F32, st=st_)
            r1 = R()
            k.dma(q, ws[:], dram_ap, [], [r1])
            wb = k.sb(shape, BF16, st=st_)
            r2 = R()
            k.I("pool", [r1], [r2], lambda e: e.tensor_copy(out=wb[:], in_=ws[:]))
            return wb, r2

        def odd_job(l, key, make_get_h, st_mem, make_sink):
            jb = OD[key]
            j = l // 2
            H, NTk, nseq, nchs = jb["H"], jb["NTk"], jb["nseq"], jb["nchs"]
            nch = nseq * nchs
            U = 2 * H
            ntile = NTk // 512
            NC_ = 2 * nch * H
            Q0, K0, V0, O0, G0 = 0, 64 * H, 128 * H, 256 * H, 384 * H
            WTOT = 388 * H
            memT = k.sb([128, H, NTk], F32, st=st_mem)
            r_mem = R()
            k.I("pool", [], [r_mem], lambda e: e.memset(memT[:], 0.0))
            Cst = k.sb([64, 2, H, nseq, 129], F32, st=st_mem)
            r_C = R()
            m0row = k.sb([1, 2, H, nseq], F32, st=st_mem)
            mfin = k.sb([1, 2, H, nseq], F32, st=st_mem)
            r_m0 = R()
            if "C0" in jb:
                k.dma("sp", Cst[:], jb["C0"][j], [], [r_C])
                k.dma("sp", m0row[:], jb["m0"][j], [], [r_m0])
            else:
                k.I("pool", [], [r_C], lambda e: e.memset(Cst[:], 0.0))
                k.I("pool", [], [r_m0], lambda e: e.memset(m0row[:], 0.0))
            st_main = ExitStack()
            qT = k.sb([64, H, NTk], BF16, st=st_main)
            kT = k.sb([64, H, NTk], BF16, st=st_main)
            r_q, r_k = R(), R()
            Ktm = k.sb([64, nch, H * 64], BF16, st=st_main)
            Vtm = k.sb([64, nch, H, 129], BF16, st=st_main)
            Gtm = k.sb([64, nch, 4 * H], F32, st=st_main)
            r_Ktm, r_V, r_G = R(), R(), R()
            k.I("pool", [], [r_V], lambda e: e.memset(Vtm[:, :, :, 128:129], 1.0))
            with ExitStack() as st_proj:
                get_h = make_get_h(st_proj)
                pfm = Rot(k, 2, [64, 512], F32, st=st_proj, psum=True)
                ptm = Rot(k, 2, [64, 256], F32, st=st_proj, psum=True)
                for b0 in range(0, WTOT, 256):
                    b1 = min(b0 + 256, WTOT)
                    if b0 >= O0 and b1 <= G0:
                        continue
                    with ExitStack() as st_w:
                        wb, rwb = load_w(st_w, jb["w_in"][j, :, :, b0:b1], [128, 8, b1 - b0])
                        for tt_ in range(ntile):
                            hts, rhts = get_h(tt_)
                            tsl = slice(tt_ * 512, (tt_ + 1) * 512)
                            for h in range(H):
                                for (base, dst, rdst, isk) in ((Q0, qT, r_q, False), (K0, kT, r_k, True)):
                                    c0 = base + 64 * h
                                    if not (b0 <= c0 < b1):
                                        continue
                                    pp, rpp = pfm.next()
                                    for kc in range(8):
                                        mm([rwb, rhts], [rpp], pp[:], wb[:, kc, c0 - b0:c0 - b0 + 64], hts[:, kc, :],
                                           start=(kc == 0), stop=(kc == 7), inc=(kc == 7))
                                    if isk:
                                        k.I("act", [rpp], [rdst], lambda e: e.mul(out=dst[:, h, tsl], in_=pp[:], mul=0.125))
                                    else:
                                        cp([rpp], [rdst], dst[:, h, tsl], pp[:], eng="act")
                            for (base, width) in ((K0, 64 * H), (V0, 128 * H), (G0, 4 * H)):
                                lo, hi = max(base, b0), min(base + width, b1)
                                if lo >= hi:
                                    continue
                                for cc in range(8):
                                    c = tt_ * 8 + cc
                                    pp, rpp = ptm.next()
                                    for kc in range(8):
                                        mm([rwb, rhts], [rpp], pp[:, 0:hi - lo], hts[:, kc, cc * 64:(cc + 1) * 64], wb[:, kc, lo - b0:hi - b0],
                                           start=(kc == 0), stop=(kc == 7), inc=(kc == 7))
                                    if base == K0:
                                        tsm([rpp], [r_Ktm], Ktm[:, c, lo - K0:hi - K0], pp[:, 0:hi - lo], 0.125)
                                    elif base == V0:
                                        h0, h1 = (lo - V0) // 128, (hi - V0) // 128
                                        cp([rpp], [r_V], Vtm[:, c, h0:h1, 0:128],
                                           pp[:, 0:hi - lo].rearrange("p (h d) -> p h d", d=128), eng="act")
                                    else:
                                        cp([rpp], [r_G], Gtm[:, c, lo - G0:hi - G0], pp[:, 0:hi - lo])
                        k.barrier()
                k.barrier()
            G = k.sb([64, 4, nch, H], F32, st=st_main)
            gb = k.sb([64, 4 * H], F32, st=st_main)
            r_g = R()
            k.dma("sp", gb[:], jb["gb"][j], [], [r_g])
            tt([r_G, r_g], [r_g], G[:], Gtm[:].rearrange("p c (t h) -> p t c h", t=4),
               gb[:].rearrange("p (t h) -> p t h", t=4).unsqueeze(2).to_broadcast([64, 4, nch, H]), ALU.add)
            Lg = k.sb([64, 2, nch, H], F32, st=st_main)
            act([r_g], [r_g], Lg[:], G[:, 2:4], AF.Exp, scale=-1.0)
            act([r_g], [r_g], Lg[:], Lg[:], AF.Ln, bias=1.0)
            Cc = k.sb([64, 2, nch, H], F32, st=st_main)
            CU = k.sb([64, 2, nch, H], F32, st=st_main)
            rows = k.sb([1, 7, 2, nch, H], F32, st=st_main)
            rowp = k.sb([1, 8, 2, H, nseq, nchs], F32, st=st_main)
            BC = k.sb([64, 5, 2, nch, H], F32, st=st_main)
            wsr = k.sb([64, 2, nch, H], F32, st=st_main)
            ws0 = k.sb([64, 2, nch, H], F32, st=st_main)
            flo = k.sb([64, 2, nch, H], F32, st=st_main)
            Mcol = k.sb([128, 1], F32, st=st_main)
            r_row, r_bc, r_tok = R(), R(), R()
            with ExitStack() as st_g:
                pg = k.ps([64, 2, nch, H], F32, st=st_g)
                r_pg = R()
                nfl = nch * H
                mm([r_g, r_const], [r_pg], pg[:, 0].rearrange("p c h -> p (c h)"), maskU[:], Lg[:, 0].rearrange("p c h -> p (c h)"))
                mm([r_g, r_const], [r_pg], pg[:, 1].rearrange("p c h -> p (c h)"), maskL[:], Lg[:, 1].rearrange("p c h -> p (c h)"))
                tt([r_pg, r_g], [r_tok], Cc[:], G[:, 0:2], pg[:], ALU.add)
                cp([r_pg], [r_tok], CU[:], pg[:])
                ptr = k.ps([128, 64], F32, st=st_g)
                r_ptr = R()
                prow = k.ps([1, 512], F32, st=st_g)
                r_prow = R()
                Cflat = Cc[:].rearrange("p d c h -> p (d c h)")
                for g0 in range(0, NC_, 128):
                    k.I("pe", [r_tok, r_const], [r_ptr], lambda e: e.transpose(ptr[:], Cflat[:, g0:g0 + 128], ident[0:64, 0:64]))
                    k.I("dve", [r_ptr], [r_tok], lambda e: e.reduce_max(out=Mcol[:], in_=ptr[:], axis=AX.X))
                    mm([r_tok, r_const], [r_prow], prow[:, g0:g0 + 128], Mcol[:], ident[:])
                cp([r_prow], [r_row], rows[:, 0].rearrange("o d c h -> o (d c h)"), prow[:, 0:NC_])
                mm([r_g, r_const], [r_prow], prow[:, 0:NC_], ones_f[:, 0:1], Lg[:].rearrange("p d c h -> p (d c h)"))
                k.I("act", [r_prow], [r_row], lambda e: e.mul(out=rows[:, 1].rearrange("o d c h -> o (d c h)"), in_=prow[:, 0:NC_], mul=-1.0))

                def to_proc(dst_q, src_q):
                    for d in range(2):
                        src = rows[:, src_q, d].rearrange("o (s c) h -> o h s c", s=nseq)
                        if d == 1:
                            src = src[:, :, :, ::-1]
                        cp([r_row], [r_row], rowp[:, dst_q, d], src)

                def to_nat(dst_q, src_q):
                    for d in range(2):
                        dst = rows[:, dst_q, d].rearrange("o (s c) h -> o h s c", s=nseq)
                        src = rowp[:, src_q, d]
                        if d == 1:
                            src = src[:, :, :, ::-1]
                        cp([r_row], [r_row], dst, src)
                Mp, Bp, Gp, D0, D1, MS, MB, AA = range(8)
                MX = D1
                to_proc(Mp, 0)
                to_proc(Bp, 1)
                rp = lambda q: rowp[:, q]
                tt([r_row], [r_row], rp(Gp), rp(Bp), rp(Mp), ALU.add)
                cp([r_row], [r_row], rp(D0), rp(Bp))
                k.I("pool", [r_row], [r_row], lambda e: e.memset(rowp[:, D0, :, :, :, 0:1], -1e30))
                cp([r_row], [r_row], rp(D1), rp(Gp))
                tt([r_row, r_m0], [r_row], rowp[:, MS, :, :, :, 0], rowp[:, Bp, :, :, :, 0], m0row[:], ALU.add)
                tt([r_row], [r_row], rowp[:, D1, :, :, :, 0], rowp[:, MS, :, :, :, 0], rowp[:, Gp, :, :, :, 0], ALU.max)
                fl = lambda q: rowp[:, q].rearrange("o d h s c -> o (d h s c)")
                k.I("dve", [r_row], [r_row], lambda e: e.tensor_tensor_scan(
                    out=fl(MS), data0=fl(D0), data1=fl(D1), initial=0.0, op0=ALU.add, op1=ALU.max))
                cp([r_row, r_m0], [r_row], rowp[:, MB, :, :, :, 0], m0row[:])
                if nchs > 1:
                    cp([r_row], [r_row], rowp[:, MB, :, :, :, 1:nchs], rowp[:, MS, :, :, :, 0:nchs - 1])
                tt([r_row], [r_row], rp(MX), rp(MB), rp(Mp), ALU.max)
                tt([r_row], [r_row], rp(AA), rp(MB), rp(MX), ALU.subtract)
                act([r_row], [r_row], rp(AA), rp(AA), AF.Exp)
                to_nat(2, MX)
                to_nat(3, AA)
                DEC = AA
                tt([r_row], [r_row], rp(DEC), rp(Bp), rp(MB), ALU.add)
                tt([r_row], [r_row], rp(DEC), rp(DEC), rp(MS), ALU.subtract)
                act([r_row], [r_row], rp(DEC), rp(DEC), AF.Exp)
                tt([r_row], [r_row], rp(D0), rp(Gp), rp(MS), ALU.subtract)
                act([r_row], [r_row], rp(D0), rp(D0), AF.Exp)
                to_nat(4, DEC)
                to_nat(5, D0)
                cp([r_row], [r_row], rows[:, 6], rows[:, 0])
                src = rows[:, 2:7].rearrange("o q d c h -> o (q d c h)")
                dstf = BC[:].rearrange("p q d c h -> p (q d c h)")
                pb = Rot(k, 2, [64, 512], F32, st=st_g, psum=True)
                for n0 in range(0, 5 * NC_, 512):
                    n1 = min(n0 + 512, 5 * NC_)
                    pp, rpp = pb.next()
                    mm([r_row, r_const], [rpp], pp[:, 0:n1 - n0], ones_f[0:1, :], src[:, n0:n1])
                    cp([rpp], [r_bc], dstf[:, n0:n1], pp[:, 0:n1 - n0])
                tt([r_tok, r_bc], [r_tok], wsr[:], Cc[:], BC[:, 0], ALU.subtract)
                act([r_tok], [r_tok], wsr[:], wsr[:], AF.Exp)
                tt([r_tok, r_bc], [r_tok], ws0[:], Cc[:], BC[:, 4], ALU.subtract)
                act([r_tok], [r_tok], ws0[:], ws0[:], AF.Exp)
                tt([r_tok, r_bc], [r_tok], flo[:], CU[:], BC[:, 0], ALU.subtract)
                act([r_tok], [r_tok], flo[:], flo[:], AF.Exp)
                if key == "p":
                    cp([r_row], [r_row], mfin[:], rowp[:, MS, :, :, :, nchs - 1])
                    k.dma("sp", om_out[j], mfin[:], [r_row], [r_fin])
                k.barrier()
            with ExitStack() as st_l:
                pq = Rot(k, 2, [64, 64], F32, st=st_l, psum=True)
                px = Rot(k, 2, [64, 129], F32, st=st_l, psum=True)
                pu = Rot(k, 2, [64, 129], F32, st=st_l, psum=True)
                pt = Rot(k, 2, [128, 64], F32, st=st_l, psum=True)
                PTs = Rot(k, 2, [64, 64], BF16, st=st_l)
                qss = Rot(k, 2, [64, 64], F32, st=st_l)
                kwss = Rot(k, 2, [64, 64], BF16, st=st_l)
                dns = Rot(k, 2, [64, 1], F32, st=st_l)
                hts_ = Rot(k, 2, [64, 128], F32, st=st_l)
                tmps = Rot(k, 2, [64, 129], F32, st=st_l)
                for cc in range(nchs):
                    for s_ in range(nseq):
                        for d in range(2):
                            c = s_ * nchs + (cc if d == 0 else nchs - 1 - cc)
                            tok = slice(c * 64, (c + 1) * 64)
                            msk = maskU if d == 0 else maskL
                            for h in range(H):
                                col = lambda t_: t_[:, d, c, h:h + 1]
                                p1, rp1 = pq.next()
                                mm([r_k, r_q], [rp1], p1[:], kT[:, h, tok], qT[:, h, tok])
                                PT, rPT = PTs.next()
                                stt([rp1, r_tok, r_const], [rPT], PT[:], p1[:], col(wsr), msk[:], ALU.mult, ALU.mult)
                                qs, rqs = qss.next()
                                k.I("act", [r_q, r_bc], [rqs], lambda e: e.mul(out=qs[:], in_=qT[:, h, tok], mul=BC[:, 1, d, c, h:h + 1]))
                                p2, rp2 = px.next()
                                mm([rPT, r_V], [rp2], p2[:], PT[:], Vtm[:, c, h, :], start=True, stop=False, inc=False)
                                mm([rqs, r_C], [rp2], p2[:], qs[:], Cst[:, d, h, s_, :], start=False, stop=True)
                                kws, rkws = kwss.next()
                                tsm([r_Ktm, r_tok], [rkws], kws[:], Ktm[:, c, h * 64:(h + 1) * 64], col(ws0), eng="pool")
                                p3, rp3 = pu.next()
                                mm([rkws, r_V], [rp3], p3[:], kws[:], Vtm[:, c, h, :])
                                dn, rdn = dns.next()
                                act([rp2], [rdn], dn[:], p2[:, 128:129], AF.Abs)
                                tt([rdn, r_tok], [rdn], dn[:], dn[:], col(flo), ALU.max)
                                k.I("dve", [rdn], [rdn], lambda e: e.reciprocal(out=dn[:], in_=dn[:]))
                                hh, rhh = hts_.next()
                                tsm([rp2, rdn], [rhh], hh[:], p2[:, 0:128], dn[:, 0:1])
                                p4, rp4 = pt.next()
                                k.I("pe", [rhh, r_const], [rp4], lambda e: e.transpose(p4[:], hh[:], ident[0:64, 0:64]))
                                tt([rp4, r_mem], [r_mem], memT[:, h, tok], memT[:, h, tok], p4[:], ALU.add)
                                tmp, rtmp = tmps.next()
                                k.I("act", [rp3, r_bc], [rtmp], lambda e: e.mul(out=tmp[:], in_=p3[:], mul=BC[:, 3, d, c, h:h + 1]))
                                stt([r_C, r_bc, rtmp], [r_C], Cst[:, d, h, s_, :], Cst[:, d, h, s_, :], BC[:, 2, d, c, h:h + 1], tmp[:], ALU.mult, ALU.add)
                if key == "p":
                    k.dma("sp", oC_out[j], Cst[:], [r_C], [r_fin])
                k.barrier()
            st_main.close()
            with ExitStack() as st_f:
                get_h = make_get_h(st_f)
                out_sink = make_sink(st_f)
                onw = k.sb([128, H], F32, st=st_f)
                r_on = R()
                k.dma("sp", onw[:], jb["onorm"][j], [], [r_on])
                wo_in, r_woin = load_w(st_f, jb["w_in"][j, :, :, O0:O0 + 128 * H] if H == 2 else None, [128, 8, 256]) if H == 2 else (None, None)
                ps1 = Rot(k, 2, [128, 512], F32, st=st_f, psum=True)
                ps2 = Rot(k, 2, [128, 512], F32, st=st_f, psum=True)
                ps3 = Rot(k, 2, [128, 512], F32, st=st_f, psum=True)
                sqs = Rot(k, 2, [128, 512], BF16, st=st_f)
                rss = Rot(k, 2, [128, 512], F32, st=st_f)
                sgs = Rot(k, 2, [128, 512], F32, st=st_f)
                mixT = k.sb([128, H, 512], BF16, st=st_f)
                r_mix = R()
                for tt_ in range(ntile):
                    hts, rhts = get_h(tt_)
                    tsl = slice(tt_ * 512, (tt_ + 1) * 512)
                    for h in range(H):
                        if H == 2:
                            wo, rwo, c0 = wo_in, r_woin, h * 128
                            stw = None
                        else:
                            stw = ExitStack()
                            wo, rwo = load_w(stw, jb["w_in"][j, :, :, O0 + h * 128:O0 + (h + 1) * 128], [128, 8, 128])
                            c0 = 0
                        sq, rsq = sqs.next()
                        act([r_mem], [rsq], sq[:], memT[:, h, tsl], AF.Square)
                        p1, rp1 = ps1.next()
                        mm([rsq, r_const], [rp1], p1[:], ones_bf[:], sq[:])
                        rs, rrs = rss.next()
                        act([rp1], [rrs], rs[:], p1[:], AF.Sqrt, bias=eps_t[:], scale=1.0 / 128)
                        k.I("dve", [rrs], [rrs], lambda e: e.reciprocal(out=rs[:], in_=rs[:]))
                        p2, rp2 = ps2.next()
                        for kc in range(8):
                            mm([rwo, rhts], [rp2], p2[:], wo[:, kc, c0:c0 + 128], hts[:, kc, :], start=(kc == 0), stop=(kc == 7), inc=(kc == 7))
                        sg, rsg = sgs.next()
                        act([rp2], [rsg], sg[:], p2[:], AF.Sigmoid)
                        tt([r_mem, rrs], [rrs], rs[:], memT[:, h, tsl], rs[:], ALU.mult)
                        stt([rrs, r_on, rsg], [r_mix], mixT[:, h, :], rs[:], onw[:, h:h + 1], sg[:], ALU.mult, ALU.mult)
                        if stw is not None:
                            k.barrier()
                            stw.close()
                    for c in range(8):
                        with ExitStack() as stw:
                            wout, rwout = load_w(stw, jb["w_out"][j, :, :, c * 128:(c + 1) * 128], [128, H, 128])
                            p3, rp3 = ps3.next()
                            for h in range(H):
                                mm([rwout, r_mix], [rp3], p3[:], wout[:, h, :], mixT[:, h, :], start=(h == 0), stop=(h == H - 1), inc=(h == H - 1))
                            out_sink(tt_, c, p3, rp3)
                            k.barrier()
                k.barrier()


        maskUs = k.sb([64, 64], F32, name="maskUs")
        maskLs = k.sb([64, 64], F32, name="maskLs")
        ident_bf = k.sb([64, 64], BF16, name="ident_bf")
        ones128 = k.sb([128, 64], F32, name="ones128")
        k.I("pool", [r_const], [r_const], lambda e: e.tensor_tensor(out=maskUs[:], in0=maskU[:], in1=ident[0:64, 0:64], op=ALU.subtract))
        k.I("pool", [r_const], [r_const], lambda e: e.tensor_tensor(out=maskLs[:], in0=maskL[:], in1=ident[0:64, 0:64], op=ALU.subtract))
        k.I("pool", [r_const], [r_const], lambda e: e.tensor_copy(out=ident_bf[:], in_=ident[0:64, 0:64]))
        k.I("pool", [], [r_const], lambda e: e.memset(ones128[:], 1.0))
        zpad = k.sb([64, 24 * 2], F32, name="zpad")
        k.I("pool", [], [r_const], lambda e: e.memset(zpad[:], 0.0))
        r_xpre = R()
        for (xp_, n_, ns_, L_) in ((xpre_p, 24, 2, 256), (xpre_s, 6, 1, 4096)):
            for col in (0, L_ + 1):
                for i_ in range(n_):
                    k.dma("sp", xp_.ap()[i_, :, :, col:col + 1], zpad[:, 0:ns_].unsqueeze(2), [r_const], [r_xpre],
                          allow_slow_non_contiguous=True)

        def wres(st_, dram_ap, shape):
            wb = k.sb(shape, BF16, st=st_)
            r2 = R()
            with ExitStack() as tmp:
                ws = k.sb(shape, F32, st=tmp)
                r1 = R()
                k.dma("sp", ws[:], dram_ap, [], [r1])
                k.I("pool", [r1], [r2], lambda e: e.tensor_copy(out=wb[:], in_=ws[:]))
                k.barrier()
            return wb, r2

        def rstd_of(rd, wr, out, ps, n, rows=128):
            act(rd, wr, out, ps, AF.Sqrt, bias=eps_t[0:rows, :], scale=1.0 / n)
            k.I("dve", wr, wr, lambda e: e.reciprocal(out=out, in_=out))

        def even_job(l, key, make_get_h, st_mem, make_sink):
            jb = EV[key]
            j = l // 2
            H, NTk, nseq, nchs = jb["H"], jb["NTk"], jb["nseq"], jb["nchs"]
            nch = nseq * nchs
            ntile = NTk // 512
            samp = key == "s"
            NCTX = 256 if samp else 0
            nkt = (NTk + NCTX) // 128
            Ls = nchs * 64
            CQ, CKV, KR, KRR, GQ = 0, 384, 640, 736, 832
            GK, GV, GZ, GA = GQ + 64 * H, GQ + 128 * H, GQ + 192 * H, GQ + 256 * H
            WE = GA + 4 * H
            xpre = (xpre_s if samp else xpre_p).ap()
            SC = 96 ** -0.5
            mixA = k.sb([64, H, NTk], BF16, st=st_mem)
            Sst = k.sb([64, 2, H, nseq, 64], F32, st=st_mem)
            r_mixA, r_oT, r_S, r_par = R(), R(), R(), R()
            if samp:
                k.dma("sp", Sst[:], jb["S0"][j], [], [r_S])
            else:
                k.I("pool", [], [r_S], lambda e: e.memset(Sst[:], 0.0))
            qan = k.sb([128, 3], F32, st=st_mem)
            kvan = k.sb([128, 2], F32, st=st_mem)
            qn = k.sb([96, 2], F32, st=st_mem)
            kn = k.sb([96, 2], F32, st=st_mem)
            onorm = k.sb([64, 1], F32, st=st_mem)
            convw = k.sb([64, 3, H, 3], F32, st=st_mem)
            alog = k.sb([64, 2 * H], F32, st=st_mem)
            dtb = k.sb([64, 2 * H], F32, st=st_mem)
            for t_, d_ in ((qan, EVN["qan"]), (kvan, EVN["kvan"]), (qn, EVN["qn"]), (kn, EVN["kn"]), (onorm, EVN["onorm"]),
                           (convw, jb["conv"]), (alog, jb["alog"]), (dtb, jb["dtb"])):
                k.dma("sp", t_[:], d_[j], [], [r_par])
            act([r_par], [r_par], alog[:], alog[:], AF.Exp)
            k.I("act", [r_par], [r_par], lambda e: e.mul(out=alog[:], in_=alog[:], mul=-1.0))

            with ExitStack() as st_a:
                get_h = make_get_h(st_a)
                qT = k.sb([96, H, NTk], BF16, st=st_a)
                kT = k.sb([96, H, NTk + NCTX], BF16, st=st_a)
                Vt = k.sb([128, nkt, H, 65], BF16, st=st_a)
                r_qT, r_kT, r_Vt = R(), R(), R()
                k.I("pool", [], [r_Vt], lambda e: e.memset(Vt[:, :, :, 64:65], 1.0))
                wsh, r_wsh = wres(st_a, jb["w_in"][j, :, :, 0:832], [128, 8, 832])
                uq, r_uq = wres(st_a, jb["w_uq"][j], [128, 3, H, 192])
                uk, r_uk = wres(st_a, jb["w_uk"][j], [128, 2, H, 64])
                uv, r_uv = wres(st_a, jb["w_uv"][j], [128, 2, H * 64])
                with ExitStack() as st_t:
                    pA = Rot(k, 2, [128, 512], F32, st=st_t, psum=True)
                    pSm = Rot(k, 2, [128, 512], F32, st=st_t, psum=True)
                    pVv = Rot(k, 2, [128, 512], F32, st=st_t, psum=True)
                    cqf = k.sb([128, 3, 512], F32, st=st_t)
                    cqn = k.sb([128, 3, 512], BF16, st=st_t)
                    ckf = k.sb([128, 2, 512], F32, st=st_t)
                    ckb = k.sb([128, 2, 512], BF16, st=st_t)
                    krf = k.sb([96, 512], F32, st=st_t)
                    krrf = k.sb([96, 512], F32, st=st_t)
                    kpre = k.sb([96, 512], F32, st=st_t)
                    sqb = k.sb([128, 3, 512], BF16, st=st_t)
                    rsb = k.sb([128, 512], F32, st=st_t)
                    tmpA = Rot(k, 2, [128, 512], F32, st=st_t)
                    tmpB = Rot(k, 2, [128, 512], F32, st=st_t)
                    cosb = k.sb([96, 512], F32, st=st_t)
                    sinb = k.sb([96, 512], F32, st=st_t)
                    r_cq, r_ck, r_kr, r_kp, r_sq, r_rs, r_cs = R(), R(), R(), R(), R(), R(), R()

                    def kv_side(n, ksl, kt0, rope):
                        cp([r_kr], [r_kp], kpre[64:96, 0:n], krf[64:96, 0:n], eng="act")
                        for h in range(H):
                            pp, rpp = pA.next()
                            for kc in range(2):
                                mm([r_uk, r_ck], [rpp], pp[0:64, 0:n], uk[:, kc, h, :], ckb[:, kc, 0:n], start=(kc == 0), stop=(kc == 1), inc=(kc == 1))
                            cp([rpp], [r_kp], kpre[0:64, 0:n], pp[0:64, 0:n], eng="act")
                            act([r_kp], [r_sq], sqb[0:96, 0, 0:n], kpre[:, 0:n], AF.Square)
                            ps_, rps_ = pSm.next()
                            mm([r_sq, r_const], [rps_], ps_[0:96, 0:n], ones_bf[0:96, 0:96], sqb[0:96, 0, 0:n])
                            rstd_of([rps_], [r_rs], rsb[0:96, 0:n], ps_[0:96, 0:n], 96, rows=96)
                            t1, rt1 = tmpA.next()
                            tt([r_kp, r_rs], [rt1], t1[0:96, 0:n], kpre[:, 0:n], rsb[0:96, 0:n], ALU.mult)
                            if not rope:
                                k.I("act", [rt1, r_par], [r_kT], lambda e: e.mul(out=kT[:, h, ksl], in_=t1[0:96, 0:n], mul=kn[:, 0:1]))
                            else:
                                k.I("act", [rt1, r_par], [r_kT], lambda e: e.mul(out=kT[0:64, h, ksl], in_=t1[0:64, 0:n], mul=kn[0:64, 0:1]))
                                k.I("act", [rt1, r_par], [rt1], lambda e: e.mul(out=t1[64:96, 0:n], in_=t1[64:96, 0:n], mul=kn[64:96, 0:1]))
                                tt([rt1, r_cs], [rt1], t1[64:96, 0:n], t1[64:96, 0:n], cosb[64:96, 0:n], ALU.mult)
                                t2, rt2 = tmpB.next()
                                tt([r_kr, r_rs], [rt2], t2[64:96, 0:n], krrf[64:96, 0:n], rsb[64:96, 0:n], ALU.mult)
                                k.I("act", [rt2, r_par], [rt2], lambda e: e.mul(out=t2[64:96, 0:n], in_=t2[64:96, 0:n], mul=kn[64:96, 1:2]))
                                tt([rt2, r_cs], [rt2], t2[64:96, 0:n], t2[64:96, 0:n], sinb[64:96, 0:n], ALU.mult)
                                tt([rt1, rt2], [r_kT], kT[64:96, h, ksl], t1[64:96, 0:n], t2[64:96, 0:n], ALU.add)
                        for q_ in range(n // 128):
                            pp, rpp = pVv.next()
                            for kc in range(2):
                                mm([r_uv, r_ck], [rpp], pp[:, 0:H * 64], ckb[:, kc, q_ * 128:(q_ + 1) * 128], uv[:, kc, :], start=(kc == 0), stop=(kc == 1), inc=(kc == 1))
                            cp([rpp], [r_Vt], Vt[:, kt0 + q_, :, 0:64], pp[:, 0:H * 64].rearrange("p (h d) -> p h d", d=64), eng="act")

                    if samp:
                        k.dma("sp", ckf[:, :, 0:256], jb["ckv_ctx"][j], [], [r_ck])
                        cp([r_ck], [r_ck], ckb[:, :, 0:256], ckf[:, :, 0:256])
                        k.dma("sp", krf[:, 0:256], jb["kr_ctx"][j], [], [r_kr])
                        kv_side(256, slice(0, 256), 0, False)
                    for tt_ in range(ntile):
                        hts, rhts = get_h(tt_)
                        tsl = slice(tt_ * 512, (tt_ + 1) * 512)
                        ksl = slice(NCTX + tt_ * 512, NCTX + (tt_ + 1) * 512)
                        if samp:
                            k.dma("sp", cosb[64:96, :], jb["cos"][64:96, tsl], [], [r_cs])
                            k.dma("sp", sinb[64:96, :], jb["sin"][64:96, tsl], [], [r_cs])
                        for c in range(3):
                            pp, rpp = pA.next()
                            for kc in range(8):
                                mm([r_wsh, rhts], [rpp], pp[:], wsh[:, kc, CQ + c * 128:CQ + (c + 1) * 128], hts[:, kc, :], start=(kc == 0), stop=(kc == 7), inc=(kc == 7))
                            cp([rpp], [r_cq], cqf[:, c, :], pp[:], eng="act")
                        act([r_cq], [r_sq], sqb[:], cqf[:], AF.Square)
                        ps_, rps_ = pSm.next()
                        for c in range(3):
                            mm([r_sq, r_const], [rps_], ps_[:], ones_bf[:], sqb[:, c, :], start=(c == 0), stop=(c == 2), inc=(c == 2))
                        rstd_of([rps_], [r_rs], rsb[:], ps_[:], 384)
                        for c in range(3):
                            t1, rt1 = tmpA.next()
                            tt([r_cq, r_rs], [rt1], t1[:], cqf[:, c, :], rsb[:], ALU.mult)
                            k.I("act", [rt1, r_par], [r_cq], lambda e: e.mul(out=cqn[:, c, :], in_=t1[:], mul=qan[:, c:c + 1]))
                        for c in range(2):
                            pp, rpp = pA.next()
                            for kc in range(8):
                                mm([r_wsh, rhts], [rpp], pp[:], wsh[:, kc, CKV + c * 128:CKV + (c + 1) * 128], hts[:, kc, :], start=(kc == 0), stop=(kc == 7), inc=(kc == 7))
                            cp([rpp], [r_ck], ckf[:, c, :], pp[:], eng="act")
                        act([r_ck], [r_sq], sqb[:, 0:2, :], ckf[:], AF.Square)
                        ps_, rps_ = pSm.next()
                        for c in range(2):
                            mm([r_sq, r_const], [rps_], ps_[:], ones_bf[:], sqb[:, c, :], start=(c == 0), stop=(c == 1), inc=(c == 1))
                        rstd_of([rps_], [r_rs], rsb[:], ps_[:], 256)
                        for c in range(2):
                            tt([r_ck, r_rs], [r_ck], ckf[:, c, :], ckf[:, c, :], rsb[:], ALU.mult)
                            k.I("act", [r_ck, r_par], [r_ck], lambda e: e.mul(out=ckf[:, c, :], in_=ckf[:, c, :], mul=kvan[:, c:c + 1]))
                        cp([r_ck], [r_ck], ckb[:], ckf[:])
                        if not samp:
                            k.dma("sp", ckv_out[j], ckf[:], [r_ck], [r_fin])
                        for (c0, dst) in ((KR, krf), (KRR, krrf)):
                            if dst is krrf and not samp:
                                continue
                            pp, rpp = pA.next()
                            for kc in range(8):
                                mm([r_wsh, rhts], [rpp], pp[0:96, :], wsh[:, kc, c0:c0 + 96], hts[:, kc, :], start=(kc == 0), stop=(kc == 7), inc=(kc == 7))
                            cp([rpp], [r_kr], dst[64:96, :], pp[64:96, :], eng="act")
                        if not samp:
                            k.dma("sp", kr_out[j], krf[64:96, :], [r_kr], [r_fin])
                        for h in range(H):
                            pp, rpp = pA.next()
                            for kc in range(3):
                                mm([r_uq, r_cq], [rpp], pp[0:96, :], uq[:, kc, h, 0:96], cqn[:, kc, :], start=(kc == 0), stop=(kc == 2), inc=(kc == 2))
                            t0, rt0 = tmpB.next()
                            cp([rpp], [rt0], t0[0:96, :], pp[0:96, :], eng="act")
                            act([rt0], [r_sq], sqb[0:96, 0, :], t0[0:96, :], AF.Square)
                            ps_, rps_ = pSm.next()
                            mm([r_sq, r_const], [rps_], ps_[0:96, :], ones_bf[0:96, 0:96], sqb[0:96, 0, :])
                            rstd_of([rps_], [r_rs], rsb[0:96, :], ps_[0:96, :], 96, rows=96)
                            t1, rt1 = tmpA.next()
                            tt([rt0, r_rs], [rt1], t1[0:96, :], t0[0:96, :], rsb[0:96, :], ALU.mult)
                            if not samp:
                                k.I("act", [rt1, r_par], [r_qT], lambda e: e.mul(out=qT[:, h, tsl], in_=t1[0:96, :], mul=qn[:, 0:1]))
                            else:
                                k.I("act", [rt1, r_par], [r_qT], lambda e: e.mul(out=qT[0:64, h, tsl], in_=t1[0:64, :], mul=qn[0:64, 0:1]))
                                k.I("act", [rt1, r_par], [rt1], lambda e: e.mul(out=t1[64:96, :], in_=t1[64:96, :], mul=qn[64:96, 0:1]))
                                tt([rt1, r_cs], [rt1], t1[64:96, :], t1[64:96, :], cosb[64:96, :], ALU.mult)
                                pp2, rpp2 = pA.next()
                                for kc in range(3):
                                    mm([r_uq, r_cq], [rpp2], pp2[0:96, :], uq[:, kc, h, 96:192], cqn[:, kc, :], start=(kc == 0), stop=(kc == 2), inc=(kc == 2))
                                t2, rt2 = tmpB.next()
                                tt([rpp2, r_rs], [rt2], t2[64:96, :], pp2[64:96, :], rsb[64:96, :], ALU.mult)
                                k.I("act", [rt2, r_par], [rt2], lambda e: e.mul(out=t2[64:96, :], in_=t2[64:96, :], mul=qn[64:96, 1:2]))
                                tt([rt2, r_cs], [rt2], t2[64:96, :], t2[64:96, :], sinb[64:96, :], ALU.mult)
                                tt([rt1, rt2], [r_qT], qT[64:96, h, tsl], t1[64:96, :], t2[64:96, :], ALU.add)
                        kv_side(512, ksl, NCTX // 128 + tt_ * 4, samp)
                    k.barrier()
                with ExitStack() as st_t:
                    pSc = Rot(k, 2, [128, 512], F32, st=st_t, psum=True)
                    pO = Rot(k, 2, [128, 512], F32, st=st_t, psum=True)
                    pB = Rot(k, 1, [64, 512], F32, st=st_t, psum=True)
                    Pb = Rot(k, 3, [128, 512], BF16, st=st_t)
                    rdb = Rot(k, 2, [128, 512], F32, st=st_t)
                    bcb = Rot(k, 2, [64, 512], F32, st=st_t)
                    if samp:
                        blocks = [(slice(b * 512, (b + 1) * 512), 512, list(range(nkt))) for b in range(ntile)]
                    else:
                        blocks = [(slice(s_ * 256, (s_ + 1) * 256), 256, [2 * s_, 2 * s_ + 1]) for s_ in range(nseq)]
                    for h in range(H):
                        for (qsl, n, kts) in blocks:
                            po, rpo = pO.next()
                            for i_, kt in enumerate(kts):
                                ps_, rps_ = pSc.next()
                                mm([r_kT, r_qT], [rps_], ps_[:, 0:n], kT[:, h, kt * 128:(kt + 1) * 128], qT[:, h, qsl])
                                P, rP = Pb.next()
                                act([rps_], [rP], P[:, 0:n], ps_[:, 0:n], AF.Exp, scale=SC)
                                mm([r_Vt, rP], [rpo], po[0:65, 0:n], Vt[:, kt, h, :], P[:, 0:n], start=(i_ == 0), stop=(i_ == len(kts) - 1), inc=(i_ == len(kts) - 1))
                            rd, rrd = rdb.next()
                            k.I("dve", [rpo], [rrd], lambda e: e.reciprocal(out=rd[64:65, 0:n], in_=po[64:65, 0:n]))
                            pb_, rpb = pB.next()
                            mm([rrd, r_const], [rpb], pb_[:, 0:n], ones128[64:65, :], rd[64:65, 0:n])
                            bc, rbc = bcb.next()
                            cp([rpb], [rbc], bc[:, 0:n], pb_[:, 0:n], eng="act")
                            tt([rpo, rbc], [r_mixA], mixA[:, h, qsl], po[0:64, 0:n], bc[:, 0:n], ALU.mult)
                    k.barrier()
            oT = k.sb([64, H, NTk], F32, st=st_mem)
            k.I("pool", [], [r_oT], lambda e: e.memset(oT[:], 0.0))
            with ExitStack() as st_b:
                qg = k.sb([64, H, NTk], BF16, st=st_b)
                kg = k.sb([64, H, NTk], BF16, st=st_b)
                Vtm = k.sb([64, nch, H, 64], BF16, st=st_b)
                Gtm = k.sb([64, nch, 4 * H], F32, st=st_b)
                r_qg, r_kg, r_Ktm, r_Vtm, r_G = R(), R(), R(), R(), R()
                with ExitStack() as st_p:
                    get_h = make_get_h(st_p)
                    pfm = Rot(k, 2, [64, 512], F32, st=st_p, psum=True)
                    ptm = Rot(k, 2, [64, 64], F32, st=st_p, psum=True)
                    xsb = Rot(k, 2, [64, 512], F32, st=st_p)
                    for b0 in range(GQ, WE, 256):
                        b1 = min(b0 + 256, WE)
                        if b0 >= GZ and b1 <= GA:
                            continue
                        with ExitStack() as st_w:
                            wb, rwb = load_w(st_w, jb["w_in"][j, :, :, b0:b1], [128, 8, b1 - b0])
                            for tt_ in range(ntile):
                                hts, rhts = get_h(tt_)
                                for T_ in range(3):
                                    for h in range(H):
                                        c0 = GQ + (T_ * H + h) * 64
                                        if not (b0 <= c0 < b1):
                                            continue
                                        pp, rpp = pfm.next()
                                        for kc in range(8):
                                            mm([rwb, rhts], [rpp], pp[:], wb[:, kc, c0 - b0:c0 - b0 + 64], hts[:, kc, :], start=(kc == 0), stop=(kc == 7), inc=(kc == 7))
                                        xs, rxs = xsb.next()
                                        cp([rpp], [rxs], xs[:], pp[:], eng="act")
                                        if samp:
                                            k.dma("sp", xpre[T_ * H + h][:, 0, 1 + tt_ * 512:1 + (tt_ + 1) * 512], xs[:], [rxs], [r_xpre])
                                        else:
                                            k.dma("sp", xpre[T_ * H + h][:, :, 1:257], xs[:].rearrange("p (s t) -> p s t", s=2), [rxs], [r_xpre])
                                lo, hi = max(GA, b0), min(GA + 4 * H, b1)
                                if lo < hi:
                                    for cc in range(8):
                                        c = tt_ * 8 + cc
                                        pp, rpp = ptm.next()
                                        for kc in range(8):
                                            mm([rwb, rhts], [rpp], pp[:, 0:hi - lo], hts[:, kc, cc * 64:(cc + 1) * 64], wb[:, kc, lo - b0:hi - b0], start=(kc == 0), stop=(kc == 7), inc=(kc == 7))
                                        cp([rpp], [r_G], Gtm[:, c, lo - GA:hi - GA], pp[:, 0:hi - lo])
                            k.barrier()
                    k.barrier()
                with ExitStack() as st_c:
                    xin = k.sb([64, nseq, Ls + 2], F32, st=st_c)
                    yb = k.sb([64, nseq, Ls], F32, st=st_c)
                    r_xin, r_y = R(), R()
                    pss = Rot(k, 2, [64, 512], F32, st=st_c, psum=True)
                    ptr = Rot(k, 2, [64, 8, 64], F32, st=st_c, psum=True)
                    sqc = Rot(k, 2, [64, 512], BF16, st=st_c)
                    rsc = Rot(k, 2, [64, 512], F32, st=st_c)
                    ynf = Rot(k, 2, [64, 512], F32, st=st_c)
                    yfl = yb[:].rearrange("p s t -> p (s t)")
                    for T_ in range(3):
                        for h in range(H):
                            k.dma("sp", xin[:], xpre[T_ * H + h], [r_xpre], [r_xin])
                            k.I("dve", [r_xin, r_par], [r_y], lambda e: e.tensor_scalar_mul(out=yb[:], in0=xin[:, :, 1:Ls + 1], scalar1=convw[:, T_, h, 1:2]))
                            stt([r_xin, r_par, r_y], [r_y], yb[:], xin[:, :, 0:Ls], convw[:, T_, h, 0:1], yb[:], ALU.mult, ALU.add)
                            stt([r_xin, r_par, r_y], [r_y], yb[:], xin[:, :, 2:Ls + 2], convw[:, T_, h, 2:3], yb[:], ALU.mult, ALU.add)
                            act([r_y], [r_y], yb[:], yb[:], AF.Silu)
                            for tt_ in range(ntile):
                                tsl = slice(tt_ * 512, (tt_ + 1) * 512)
                                if T_ < 2:
                                    sq, rsq = sqc.next()
                                    act([r_y], [rsq], sq[:], yfl[:, tsl], AF.Square)
                                    ps_, rps_ = pss.next()
                                    mm([rsq, r_const], [rps_], ps_[:], ones_bf[0:64, 0:64], sq[:])
                                    rs, rrs = rsc.next()
                                    rstd_of([rps_], [rrs], rs[:], ps_[:], 1.0, rows=64)
                                    yn, ryn = ynf.next()
                                    tt([r_y, rrs], [ryn], yn[:], yfl[:, tsl], rs[:], ALU.mult)
                                    if T_ == 0:
                                        k.I("act", [ryn], [r_qg], lambda e: e.mul(out=qg[:, h, tsl], in_=yn[:], mul=0.125))
                                        continue
                                    cp([ryn], [r_kg], kg[:, h, tsl], yn[:], eng="act")
                                    continue
                                else:
                                    src_t, rsrc = None, r_y
                                    dst_t, rdst = Vtm, r_Vtm
                                pt_, rpt = ptr.next()
                                for cc in range(8):
                                    src = (src_t[:, cc * 64:(cc + 1) * 64] if src_t is not None else yfl[:, tt_ * 512 + cc * 64:tt_ * 512 + (cc + 1) * 64])
                                    k.I("pe", [rsrc, r_const], [rpt], lambda e: e.transpose(pt_[:, cc, :], src, ident[0:64, 0:64]), inc=(cc == 7))
                                cp([rpt], [rdst], dst_t[:, tt_ * 8:(tt_ + 1) * 8, h, :], pt_[:])
                    k.barrier()
                Gg = k.sb([64, 4, nch, H], F32, st=st_b)
                GCs = k.sb([64, 2, nch, H], F32, st=st_b)
                BET = k.sb([64, 2, nch, H], F32, st=st_b)
                KBG = k.sb([64, 2, nch, H], F32, st=st_b)
                KDE = k.sb([64, 2, nch, H], F32, st=st_b)
                GLb = k.sb([64, 2, nch, H], F32, st=st_b)
                r_gt = R()
                with ExitStack() as st_g:
                    pg = k.ps([64, 2, nch, H], F32, st=st_g)
                    pt2 = k.ps([64, 2, nch, H], F32, st=st_g)
                    r_pg, r_pt2 = R(), R()
                    cp([r_G], [r_gt], Gg[:], Gtm[:].rearrange("p c (t h) -> p t c h", t=4))
                    tt([r_gt, r_par], [r_gt], Gg[:, 0:2], Gg[:, 0:2], dtb[:].rearrange("p (d h) -> p d h", d=2).unsqueeze(2).to_broadcast([64, 2, nch, H]), ALU.add)
                    act([r_gt], [r_gt], Gg[:, 0:2], Gg[:, 0:2], AF.Exp)
                    act([r_gt], [r_gt], Gg[:, 0:2], Gg[:, 0:2], AF.Ln, bias=1.0)
                    tt([r_gt, r_par], [r_gt], Gg[:, 0:2], Gg[:, 0:2], alog[:].rearrange("p (d h) -> p d h", d=2).unsqueeze(2).to_broadcast([64, 2, nch, H]), ALU.mult)
                    act([r_gt], [r_gt], BET[:], Gg[:, 2:4], AF.Sigmoid)
                    fl3 = lambda t_: t_.rearrange("p c h -> p (c h)")
                    mm([r_gt, r_const], [r_pg], fl3(pg[:, 0]), maskU[:], fl3(Gg[:, 0]))
                    mm([r_gt, r_const], [r_pg], fl3(pg[:, 1]), maskL[:], fl3(Gg[:, 1]))
                    cp([r_pg], [r_gt], GCs[:], pg[:])
                    mm([r_gt, r_const], [r_pt2], pt2[:].rearrange("p d c h -> p (d c h)"), ones_f[:], Gg[:, 0:2].rearrange("p d c h -> p (d c h)"))
                    act([r_pt2], [r_gt], GLb[:], pt2[:], AF.Exp)
                    tt([r_pt2, r_gt], [r_gt], KDE[:], pt2[:], GCs[:], ALU.subtract)
                    act([r_gt], [r_gt], KDE[:], KDE[:], AF.Exp)
                    act([r_gt], [r_gt], KBG[:], GCs[:], AF.Exp)
                    tt([r_gt], [r_gt], KBG[:], KBG[:], BET[:], ALU.mult)
                    k.barrier()
                with ExitStack() as st_l:
                    pR = Rot(k, 2, [64, 8, 64], F32, st=st_l, psum=True)
                    pM = Rot(k, 4, [64, 8, 64], F32, st=st_l, psum=True)
                    pKt = Rot(k, 1, [64, 8, 64], BF16, st=st_l, psum=True)
                    pSt = Rot(k, 1, [64, 3, 64], F32, st=st_l, psum=True)
                    B3 = lambda: k.sb([64, 8, 64], F32, st=st_l)
                    Dg, Zb, DmT, DmTs, EgR, qd, NTb, Nb, Xb, attnT, Ma, Mb, MTa, MTb, vb, kbg, kd, Ub, WTb = [B3() for _ in range(19)]
                    kbT = k.sb([64, 8, 64], BF16, st=st_l)
                    vns = Rot(k, 2, [64, 64], F32, st=st_l)
                    r_sp = R()
                    r_vn = R()
                    nsp = nch // 8
                    fl = lambda t_: t_[:].rearrange("p c i -> p (c i)")
                    bcI = ident[0:64, 0:64].unsqueeze(1).to_broadcast([64, 8, 64])
                    for si in range(nsp):
                        for d in range(2):
                            sp_ = si if d == 0 else nsp - 1 - si
                            c0 = sp_ * 8
                            tsl = slice(c0 * 64, c0 * 64 + 512)
                            mI, mS = (maskU, maskUs) if d == 0 else (maskL, maskLs)
                            for h in range(H):
                                colb = lambda t_: t_[:, d, c0:c0 + 8, h:h + 1].to_broadcast([64, 8, 64])
                                tt([r_gt, r_const], [r_sp], Dg[:], bcI, colb(GCs), ALU.mult, eng="pool")
                                p1, rp1 = pR.next()
                                mm([r_sp, r_const], [rp1], fl(p1), ones_f[:], fl(Dg))
                                tt([rp1, r_gt], [r_sp], Zb[:], p1[:], colb(GCs), ALU.subtract)
                                k.I("dve", [r_sp], [r_sp], lambda e: e.tensor_scalar_min(out=Zb[:], in0=Zb[:], scalar1=0.0))
                                act([r_sp], [r_sp], DmT[:], Zb[:], AF.Exp)
                                tt([r_sp, r_const], [r_sp], DmTs[:], DmT[:], mS[:].unsqueeze(1).to_broadcast([64, 8, 64]), ALU.mult, eng="pool")
                                tt([r_sp, r_const], [r_sp], DmT[:], DmT[:], mI[:].unsqueeze(1).to_broadcast([64, 8, 64]), ALU.mult, eng="pool")
                                act([rp1], [r_sp], EgR[:], p1[:], AF.Exp)
                                tt([r_qg, r_sp], [r_sp], fl(qd), qg[:, h, tsl], fl(EgR), ALU.mult)
                                tt([r_gt, r_const], [r_sp], Dg[:], bcI, colb(BET), ALU.mult, eng="pool")
                                p2, rp2 = pR.next()
                                mm([r_sp, r_const], [rp2], fl(p2), ones_f[:], fl(Dg))
                                tt([r_kg, rp2], [r_sp], fl(kbT), kg[:, h, tsl], fl(p2), ALU.mult)
                                pa, rpa = pM.next()
                                pq_, rpq = pM.next()
                                for cc in range(8):
                                    csl = slice(c0 * 64 + cc * 64, c0 * 64 + (cc + 1) * 64)
                                    mm([r_kg, r_sp], [rpa], pa[:, cc, :], kg[:, h, csl], kbT[:, cc, :], inc=(cc == 7))
                                for cc in range(8):
                                    csl = slice(c0 * 64 + cc * 64, c0 * 64 + (cc + 1) * 64)
                                    mm([r_kg, r_qg], [rpq], pq_[:, cc, :], kg[:, h, csl], qg[:, h, csl], inc=(cc == 7))
                                stt([rpa, r_sp], [r_sp], NTb[:], pa[:], -1.0, DmTs[:], ALU.mult, ALU.mult)
                                tt([rpq, r_sp], [r_sp], attnT[:], pq_[:], DmT[:], ALU.mult)
                                pn, rpn = pM.next()
                                for cc in range(8):
                                    k.I("pe", [r_sp, r_const], [rpn], lambda e: e.transpose(pn[:, cc, :], NTb[:, cc, :], ident[0:64, 0:64]), inc=(cc == 7))
                                cp([rpn], [r_sp], Nb[:], pn[:], eng="act")
                                tt([r_sp, r_const], [r_sp], Xb[:], NTb[:], bcI, ALU.add, eng="pool")
                                M_, MT_ = Nb, NTb
                                bufs = [(Ma, MTa), (Mb, MTb)]
                                for rnd in range(5):
                                    Mn, MTn = bufs[rnd % 2]
                                    pm_, rpm_ = pM.next()
                                    for cc in range(8):
                                        mm([r_sp], [rpm_], pm_[:, cc, :], MT_[:, cc, :], M_[:, cc, :], inc=(cc == 7))
                                    cp([rpm_], [r_sp], Mn[:], pm_[:], eng="act")
                                    if rnd < 4:
                                        pmt, rpmt = pM.next()
                                        for cc in range(8):
                                            mm([r_sp], [rpmt], pmt[:, cc, :], M_[:, cc, :], MT_[:, cc, :], inc=(cc == 7))
                                        cp([rpmt], [r_sp], MTn[:], pmt[:], eng="act")
                                    px_, rpx = pM.next()
                                    for cc in range(8):
                                        mm([r_sp], [rpx], px_[:, cc, :], Mn[:, cc, :], Xb[:, cc, :], inc=(cc == 7))
                                    tt([rpx, r_sp], [r_sp], Xb[:], Xb[:], px_[:], ALU.add)
                                    M_, MT_ = Mn, MTn
                                pk_, rpk = pKt.next()
                                for cc in range(8):
                                    csl = slice(c0 * 64 + cc * 64, c0 * 64 + (cc + 1) * 64)
                                    k.I("pe", [r_kg, r_const], [rpk], lambda e: e.transpose(pk_[:, cc, :], kg[:, h, csl], ident_bf[:]), inc=(cc == 7))
                                tt([rpk, r_gt], [r_sp], kbg[:], pk_[:], colb(KBG), ALU.mult)
                                tt([rpk, r_gt], [r_sp], kd[:], pk_[:], colb(KDE), ALU.mult)
                                tt([r_Vtm, r_gt], [r_sp], vb[:], Vtm[:, c0:c0 + 8, h, :], colb(BET), ALU.mult, eng="pool")
                                pu_, rpu = pM.next()
                                for cc in range(8):
                                    mm([r_sp], [rpu], pu_[:, cc, :], Xb[:, cc, :], vb[:, cc, :], inc=(cc == 7))
                                cp([rpu], [r_sp], Ub[:], pu_[:], eng="act")
                                pw_, rpw = pM.next()
                                for cc in range(8):
                                    mm([r_sp], [rpw], pw_[:, cc, :], kbg[:, cc, :], Xb[:, cc, :], inc=(cc == 7))
                                cp([rpw], [r_sp], WTb[:], pw_[:], eng="act")
                                order = range(8) if d == 0 else range(7, -1, -1)
                                for cc in order:
                                    c = c0 + cc
                                    s_ = c // nchs
                                    csl = slice(c * 64, (c + 1) * 64)
                                    S_ = Sst[:, d, h, s_, :]
                                    ps3, rps3 = pSt.next()
                                    mm([r_sp, r_S], [rps3], ps3[:, 0, :], WTb[:, cc, :], S_)
                                    vn, rvn = vns.next()
                                    tt([r_sp, rps3], [rvn], vn[:], Ub[:, cc, :], ps3[:, 0, :], ALU.subtract)
                                    mm([r_S, r_sp], [rps3], ps3[:, 1, :], S_, qd[:, cc, :], start=True, stop=False, inc=False)
                                    mm([rvn, r_sp], [rps3], ps3[:, 1, :], vn[:], attnT[:, cc, :], start=False, stop=True)
                                    mm([r_sp, rvn], [rps3], ps3[:, 2, :], kd[:, cc, :], vn[:])
                                    tt([rps3, r_oT], [r_oT], oT[:, h, csl], oT[:, h, csl], ps3[:, 1, :], ALU.add)
                                    stt([r_S, r_gt, rps3], [r_S], S_, S_, GLb[:, d, c, h:h + 1], ps3[:, 2, :], ALU.mult, ALU.add)
                    if not samp:
                        k.dma("sp", gdn_out[j], Sst[:], [r_S], [r_fin])
                    k.barrier()
            with ExitStack() as st_f:
                get_h = make_get_h(st_f)
                out_sink = make_sink(st_f)
                ps1 = Rot(k, 2, [64, 512], F32, st=st_f, psum=True)
                ps2 = Rot(k, 2, [64, 512], F32, st=st_f, psum=True)
                ps3 = Rot(k, 2, [128, 512], F32, st=st_f, psum=True)
                sqs = Rot(k, 2, [64, 512], BF16, st=st_f)
                rss = Rot(k, 2, [64, 512], F32, st=st_f)
                zs = Rot(k, 2, [64, 512], F32, st=st_f)
                mixD = k.sb([64, H, 512], BF16, st=st_f)
                r_mixD = R()
                wz_res = wres(st_f, jb["w_in"][j, :, :, GZ:GZ + 64 * H], [128, 8, 64 * H]) if H == 2 else None
                for tt_ in range(ntile):
                    hts, rhts = get_h(tt_)
                    tsl = slice(tt_ * 512, (tt_ + 1) * 512)
                    for h in range(H):
                        if wz_res is not None:
                            (wz, rwz), c0, stw = wz_res, h * 64, None
                        else:
                            stw = ExitStack()
                            wz, rwz = load_w(stw, jb["w_in"][j, :, :, GZ + h * 64:GZ + (h + 1) * 64], [128, 8, 64])
                            c0 = 0
                        sq, rsq = sqs.next()
                        act([r_oT], [rsq], sq[:], oT[:, h, tsl], AF.Square)
                        p1, rp1 = ps1.next()
                        mm([rsq, r_const], [rp1], p1[:], ones_bf[0:64, 0:64], sq[:])
                        rs, rrs = rss.next()
                        rstd_of([rp1], [rrs], rs[:], p1[:], 64, rows=64)
                        p2, rp2 = ps2.next()
                        for kc in range(8):
                            mm([rwz, rhts], [rp2], p2[:], wz[:, kc, c0:c0 + 64], hts[:, kc, :], start=(kc == 0), stop=(kc == 7), inc=(kc == 7))
                        z, rz = zs.next()
                        act([rp2], [rz], z[:], p2[:], AF.Silu)
                        tt([r_oT, rrs], [rrs], rs[:], oT[:, h, tsl], rs[:], ALU.mult)
                        stt([rrs, r_par, rz], [r_mixD], mixD[:, h, :], rs[:], onorm[:, 0:1], z[:], ALU.mult, ALU.mult)
                        if stw is not None:
                            k.barrier()
                            stw.close()
                    for c in range(8):
                        with ExitStack() as stw:
                            wout, rwout = load_w(stw, jb["w_out"][j, :, :, :, c * 128:(c + 1) * 128], [64, 2, H, 128])
                            p3, rp3 = ps3.next()
                            for h in range(H):
                                mm([rwout, r_mixA], [rp3], p3[:], wout[:, 0, h, :], mixA[:, h, tsl], start=(h == 0), stop=False, inc=False)
                            for h in range(H):
                                mm([rwout, r_mixD], [rp3], p3[:], wout[:, 1, h, :], mixD[:, h, :], start=False, stop=(h == H - 1), inc=(h == H - 1))
                            out_sink(tt_, c, p3, rp3)
                            k.barrier()
                k.barrier()

        def mixer(l):
            j = l // 2
            job_fn = odd_job if l % 2 == 1 else even_job
            r_agin, r_agout, r_rsin, r_rsout = R(), R(), R(), R()
            with ExitStack() as st_h:
                hTp = k.sb([128, 8, 512], BF16, st=st_h)
                rhp = R()
                with ExitStack() as st_n:
                    ph = {"sq": Rot(k, 1, [128, 8, 512], BF16, st=st_n),
                          "rs": Rot(k, 2, [128, 512], F32, st=st_n),
                          "tmp": Rot(k, 2, [128, 512], F32, st=st_n)}
                    ps_ss = k.ps([128, 512], F32, st=st_n)
                    rps = R()
                    hTs = k.sb([128, 8, 1024], BF16, st=st_n)
                    rhs_ = R()
                    for t in (1, 2):
                        norm_mod(ph, l, 0, t, (lambda c, t=t: hTs[:, c, (t - 1) * 512:t * 512]), rhs_, ps_ss, rps)
                    norm_mod(ph, l, 0, 0, (lambda c: hTp[:, c, :]), rhp, ps_ss, rps)
                    for q_ in range(4):
                        k.dma("sp", ag_in.ap()[q_].bitcast(BF16).rearrange("(c p) t -> p c t", p=128),
                              hTs[:, 2 * q_:2 * q_ + 2, :], [rhs_], [r_agin])
                    for q_ in range(4):
                        k.cc("AllGather", ALU.bypass, GROUPS, ag_in.ap()[q_], ag_out.ap()[q_], [r_agin], [r_agout])
                    k.barrier()

                def sink_p(tt_, c, ps, rps_):
                    k.I("dve", [rps_, r_mod, rx[0]], [rx[0]], lambda e: e.scalar_tensor_tensor(
                        out=xT[:, c, 0:512], in0=ps[:], scalar=mvec(l, 2, c, 0), in1=xT[:, c, 0:512], op0=ALU.mult, op1=ALU.add))
                with ExitStack() as st_mem:
                    job_fn(l, "p", lambda st_: (lambda tt_: (hTp, rhp)), st_mem, lambda st_: sink_p)
                    k.barrier()
            with ExitStack() as st_s:
                agv = [ag_out.ap()[q_].bitcast(BF16).rearrange("(r c p) t -> r p c t", r=4, c=2) for q_ in range(4)]

                def make_get_hs(st_):
                    hbuf = Rot(k, 2, [128, 8, 512], BF16, st=st_)

                    def get_hs(tt_):
                        hb, rhb = hbuf.next()
                        for q_ in range(4):
                            k.dma("sp", hb[:, 2 * q_:2 * q_ + 2, :], agv[q_][tt_ // 2][:, :, (tt_ % 2) * 512:(tt_ % 2 + 1) * 512], [r_agout], [rhb])
                        return hb, rhb
                    return get_hs
                rsv = rs_in.ap().rearrange("(r c p) t -> r c p t", r=4, c=8)

                def make_sink_s(st_):
                    osb = Rot(k, 2, [128, 512], F32, st=st_)

                    def sink_s(tt_, c, ps, rps_):
                        ob, rob = osb.next()
                        cp([rps_], [rob], ob[:], ps[:], eng="act")
                        k.dma("sp", rsv[tt_ // 2, c][:, (tt_ % 2) * 512:(tt_ % 2 + 1) * 512], ob[:], [rob], [r_rsin])
                    return sink_s
                with ExitStack() as st_mem:
                    job_fn(l, "s", make_get_hs, st_mem, make_sink_s)
                    k.barrier()
                k.cc("ReduceScatter", ALU.add, GROUPS, rs_in.ap(), rs_out.ap(), [r_rsin], [r_rsout])
                rso = rs_out.ap().rearrange("(c p) t -> p c t", p=128)
                stg = Rot(k, 2, [128, 8, 512], F32, st=st_s)
                for t in (1, 2):
                    sg_, rsg_ = stg.next()
                    k.dma("sp", sg_[:], rso[:, :, (t - 1) * 512:t * 512], [r_rsout], [rsg_])
                    for c in range(8):
                        k.I("dve", [rsg_, r_mod, rx[t]], [rx[t]], lambda e: e.scalar_tensor_tensor(
                            out=xT[:, c, t * 512:(t + 1) * 512], in0=sg_[:, c, :], scalar=mvec(l, 2, c, 1),
                            in1=xT[:, c, t * 512:(t + 1) * 512], op0=ALU.mult, op1=ALU.add))
                k.barrier()

        r_fin = R()
        for l in ([] if ONLY_ADA else (LAYERS if LAYERS is not None else range(NLAYERS))):
            if MIXERS:
                mixer(l)
            if FFN:
                ffn(l)

        r_out = R()
        for t in range(NT):
            k.dma("sp", yT_d[:, :, t * 512:(t + 1) * 512], xT[:, :, t * 512:(t + 1) * 512], [rx[t]], [r_out])
            k.finish([r_out])
        k.finish(list(dbg.values()) + [r_fin])
        print("n_ins", k.n_ins)
    return nc


def to_fm(x):
    n = x.shape[0]
    return np.ascontiguousarray(x.reshape(n, 8, 128).transpose(2, 1, 0))


def from_fm(y):
    n = y.shape[2]
    return np.ascontiguousarray(y.transpose(2, 1, 0).reshape(n, 1024))


def prep_inputs(inp):
    f = lambda a: np.asarray(a, dtype=np.float32)
    xp, xs = f(inp["x_prompt"]), f(inp["x_sample"])
    shared = {}
    shared["w_ada"] = np.ascontiguousarray(f(inp["w_ada"]).reshape(DEPTH, 8, 128, 6 * D).transpose(0, 2, 1, 3))
    shared["b_adaT"] = np.ascontiguousarray(f(inp["b_ada"]).reshape(DEPTH, 48, 128).transpose(2, 0, 1))
    shared["nmT"] = np.ascontiguousarray(f(inp["norm_mix"]).reshape(DEPTH, 8, 128).transpose(2, 0, 1))
    shared["nfT"] = np.ascontiguousarray(f(inp["norm_ffn"]).reshape(DEPTH, 8, 128).transpose(2, 0, 1))
    wfi = f(inp["w_ffn_in"])
    g = wfi[:, :, :DFF].reshape(DEPTH, 8, 128, NJ, 128)
    u = wfi[:, :, DFF:].reshape(DEPTH, 8, 128, NJ, 128)
    gu = np.concatenate([g, u], axis=-1)
    shared["w_ffn_in"] = np.ascontiguousarray(gu.transpose(0, 2, 3, 1, 4))
    shared["w_ffn_out"] = np.ascontiguousarray(f(inp["w_ffn_out"]).reshape(DEPTH, NJ, 128, D).transpose(0, 2, 1, 3))
    maps = []
    for c in range(NCORES):
        g_, r_ = c // 4, c % 4
        toks = np.concatenate([xp[2 * c], xp[2 * c + 1], xs[g_, r_ * 1024:(r_ + 1) * 1024]], axis=0)
        m = dict(shared)
        m["xT"] = to_fm(toks)
        cnd = np.stack([f(inp["c_ctx"]), f(inp["c"])[g_]], axis=-1)
        m["condT"] = np.ascontiguousarray(cnd.reshape(8, 128, 2).transpose(1, 0, 2))
        maps.append(m)
    return maps


def _pm(w):
    return np.ascontiguousarray(w.reshape(8, 128, -1).transpose(1, 0, 2))


def prep_odd(inp, maps):
    f = lambda a: np.asarray(a, dtype=np.float32)
    w_in, w_out = f(inp["w_odd_in"]), f(inp["w_odd_out"])
    gbias, onorm = f(inp["mlstm_gate_bias"]), f(inp["mlstm_out_norm"])
    C0, n0, m0 = f(inp["state_mlstm_C"]), f(inp["state_mlstm_n"]), f(inp["state_mlstm_m"])

    def pack_in(j, heads):
        w = w_in[j]
        q = np.concatenate([w[:, h * 64:(h + 1) * 64] for h in heads], 1)
        kk = np.concatenate([w[:, 512 + h * 64:512 + (h + 1) * 64] for h in heads], 1)
        v = np.concatenate([w[:, 1024 + h * 128:1024 + (h + 1) * 128] for h in heads], 1)
        o = np.concatenate([w[:, 2048 + h * 128:2048 + (h + 1) * 128] for h in heads], 1)
        g = np.concatenate([w[:, 3072 + t * 8 + h:3072 + t * 8 + h + 1] for t in range(4) for h in heads], 1)
        return _pm(np.concatenate([q, kk, v, o, g], 1))

    def pack_out(j, heads):
        return np.ascontiguousarray(np.stack([w_out[j, h * 128:(h + 1) * 128, :] for h in heads], 1))

    def pack_gb(j, heads):
        row = np.concatenate([gbias[j, t, heads] for t in range(4)])
        return np.ascontiguousarray(np.tile(row[None, :], (64, 1)))

    def pack_on(j, heads):
        return np.ascontiguousarray(np.stack([onorm[j, h * 128:(h + 1) * 128] for h in heads], 1))

    allh = list(range(8))
    shared = {
        "wo_in_p": np.stack([pack_in(j, allh) for j in range(2)]),
        "wo_out_p": np.stack([pack_out(j, allh) for j in range(2)]),
        "ogb_p": np.stack([pack_gb(j, allh) for j in range(2)]),
        "onorm_p": np.stack([pack_on(j, allh) for j in range(2)]),
    }
    per_r = {}
    for r in range(4):
        hs = [2 * r, 2 * r + 1]
        per_r[r] = {
            "wo_in_s": np.stack([pack_in(j, hs) for j in range(2)]),
            "wo_out_s": np.stack([pack_out(j, hs) for j in range(2)]),
            "ogb_s": np.stack([pack_gb(j, hs) for j in range(2)]),
            "onorm_s": np.stack([pack_on(j, hs) for j in range(2)]),
        }
    for c in range(NCORES):
        g_, r_ = c // 4, c % 4
        hs = [2 * r_, 2 * r_ + 1]
        m = maps[c]
        m.update(shared)
        m.update(per_r[r_])
        Cs = C0[g_][:, :, hs]
        ns = n0[g_][:, :, hs]
        aug = np.concatenate([Cs, ns[..., None]], -1)
        m["oC0_s"] = np.ascontiguousarray(aug.transpose(0, 3, 1, 2, 4)[:, :, :, :, None, :])
        m["om0_s"] = np.ascontiguousarray(m0[g_][:, :, hs][:, None, :, :, None])

_ROPE_PERM = np.array([(r // 16) * 16 + ((r % 16) + 8) % 16 for r in range(32)])
_ROPE_SIGN = np.array([-1.0 if (r % 16) < 8 else 1.0 for r in range(32)], np.float32)


def _rope_tables():
    t = np.arange(4096)
    row, col = (t // 64).astype(np.float32), (t % 64).astype(np.float32)
    inv = (1.0 / (10000.0 ** (np.arange(0, 16, 2, dtype=np.float32) / 16))).astype(np.float32)
    ar, ac = row[:, None] * inv, col[:, None] * inv
    ang = np.concatenate([ar, ar, ac, ac], -1)
    cos = np.zeros((96, 4096), np.float32)
    sin = np.zeros((96, 4096), np.float32)
    cos[64:] = np.cos(ang).T
    sin[64:] = (np.sin(ang) * _ROPE_SIGN[None, :]).T
    return cos, sin


def prep_even(inp, maps):
    f = lambda a: np.asarray(a, dtype=np.float32)
    w_in, w_out = f(inp["w_even_in"]), f(inp["w_even_out"])
    w_uq, w_ukv = f(inp["w_mla_uq"]), f(inp["w_mla_ukv"])
    conv, alog, dtb = f(inp["gdn_conv"]), f(inp["gdn_a_log"]), f(inp["gdn_dt_bias"])
    qn, kn = f(inp["mla_q_norm"]), f(inp["mla_k_norm"])

    def pack_in(j, heads):
        w = w_in[j]
        z64 = np.zeros((1024, 64), np.float32)
        kr = w[:, 640:672]
        parts = [w[:, 0:384], w[:, 384:640], z64, kr, z64, kr[:, _ROPE_PERM]]
        for T_ in range(4):
            parts += [w[:, 672 + T_ * 512 + h * 64:672 + T_ * 512 + (h + 1) * 64] for h in heads]
        for gbase in (2720, 2736):
            parts += [w[:, gbase + d * 8 + h:gbase + d * 8 + h + 1] for d in range(2) for h in heads]
        return _pm(np.concatenate(parts, 1))

    def pack_uq(j, heads):
        out = np.zeros((384, len(heads), 192), np.float32)
        for i, h in enumerate(heads):
            out[:, i, 0:96] = w_uq[j][:, h * 96:(h + 1) * 96]
            out[:, i, 160:192] = w_uq[j][:, h * 96 + 64 + _ROPE_PERM]
        return np.ascontiguousarray(out.reshape(3, 128, len(heads), 192).transpose(1, 0, 2, 3))

    def pack_uk(j, heads):
        out = np.stack([w_ukv[j][:, h * 128:h * 128 + 64] for h in heads], 1)
        return np.ascontiguousarray(out.reshape(2, 128, len(heads), 64).transpose(1, 0, 2, 3))

    def pack_uv(j, heads):
        out = np.concatenate([w_ukv[j][:, h * 128 + 64:(h + 1) * 128] for h in heads], 1)
        return np.ascontiguousarray(out.reshape(2, 128, -1).transpose(1, 0, 2))

    def pack_out(j, heads):
        a = np.stack([w_out[j][h * 64:(h + 1) * 64] for h in heads], 1)
        dlt = np.stack([w_out[j][512 + h * 64:512 + (h + 1) * 64] for h in heads], 1)
        return np.ascontiguousarray(np.stack([a, dlt], 1))

    def pack_conv(j, heads):
        out = np.zeros((64, 3, len(heads), 3), np.float32)
        for T_ in range(3):
            for i, h in enumerate(heads):
                out[:, T_, i, :] = conv[j][:, T_ * 512 + h * 64:T_ * 512 + (h + 1) * 64].T
        return out

    def rep(v, heads):
        row = np.concatenate([v[d, heads] for d in range(2)])
        return np.ascontiguousarray(np.tile(row[None, :], (64, 1)))

    def packs(sfx, heads):
        return {
            "we_in_" + sfx: np.stack([pack_in(j, heads) for j in range(2)]),
            "we_uq_" + sfx: np.stack([pack_uq(j, heads) for j in range(2)]),
            "we_uk_" + sfx: np.stack([pack_uk(j, heads) for j in range(2)]),
            "we_uv_" + sfx: np.stack([pack_uv(j, heads) for j in range(2)]),
            "we_out_" + sfx: np.stack([pack_out(j, heads) for j in range(2)]),
            "we_conv_" + sfx: np.stack([pack_conv(j, heads) for j in range(2)]),
            "we_alog_" + sfx: np.stack([rep(alog[j], heads) for j in range(2)]),
            "we_dtb_" + sfx: np.stack([rep(dtb[j], heads) for j in range(2)]),
        }
    shared = packs("p", list(range(8)))
    shared["we_qan"] = np.ascontiguousarray(f(inp["mla_q_a_norm"]).reshape(2, 3, 128).transpose(0, 2, 1))
    shared["we_kvan"] = np.ascontiguousarray(f(inp["mla_kv_a_norm"]).reshape(2, 2, 128).transpose(0, 2, 1))

    def npk(g):
        out = np.zeros((2, 96, 2), np.float32)
        out[:, :, 0] = g
        out[:, 64:, 1] = g[:, 64 + _ROPE_PERM]
        return out
    shared["we_qn"], shared["we_kn"] = npk(qn), npk(kn)
    shared["we_onorm"] = np.ascontiguousarray(f(inp["gdn_out_norm"])[:, :, None])
    cos, sin = _rope_tables()
    shared["rope_cos"], shared["rope_sin"] = cos, sin
    per_r = {r: packs("s", [2 * r, 2 * r + 1]) for r in range(4)}
    cckv, ckr, sg = f(inp["cache_mla_ckv"]), f(inp["cache_mla_krope"]), f(inp["state_gdn"])
    for c in range(NCORES):
        g_, r_ = c // 4, c % 4
        hs = [2 * r_, 2 * r_ + 1]
        m = maps[c]
        m.update(shared)
        m.update(per_r[r_])
        m["we_ckvctx"] = np.ascontiguousarray(cckv[g_].transpose(0, 2, 1).reshape(2, 2, 128, 256).transpose(0, 2, 1, 3))
        kc_ = np.zeros((2, 96, 256), np.float32)
        kc_[:, 64:, :] = ckr[g_].transpose(0, 2, 1)
        m["we_krctx"] = kc_
        S = sg[g_][:, :, hs]
        m["we_S0"] = np.ascontiguousarray(S.transpose(0, 3, 1, 2, 4)[:, :, :, :, None, :])


_NC = None


def kernel(**inputs):
    global _NC
    maps = prep_inputs(inputs)
    prep_odd(inputs, maps)
    prep_even(inputs, maps)
    if _NC is None:
        _NC = build_program()
    res = run_bass_kernel_spmd(_NC, maps, core_ids=list(range(NCORES)))
    if DEBUG:
        kernel.res = res
    yp = np.zeros((16, 256, D), np.float32)
    ys = np.zeros((2, 4096, D), np.float32)
    for c in range(NCORES):
        y = from_fm(res.results[c]["yT"])
        yp[2 * c] = y[0:256]
        yp[2 * c + 1] = y[256:512]
        ys[c // 4, (c % 4) * 1024:(c % 4 + 1) * 1024] = y[512:1536]
    z = lambda *sh: np.zeros(sh, np.float32)
    mC, mn, mm_ = z(16, 2, 2, 8, 64, 128), z(16, 2, 2, 8, 64), z(16, 2, 2, 8)
    for c in range(NCORES):
        oc = res.results[c]["oC_out"]
        om = res.results[c]["om_out"]
        for s_ in range(2):
            a = oc[:, :, :, :, s_, :].transpose(0, 2, 3, 1, 4)
            mC[2 * c + s_] = a[..., :128]
            mn[2 * c + s_] = a[..., 128]
            mm_[2 * c + s_] = om[:, 0, :, :, s_]
    ockv, okr, ogdn = z(16, 2, 256, 256), z(16, 2, 256, 32), z(16, 2, 2, 8, 64, 64)
    for c in range(NCORES):
        ck = res.results[c]["ckv_out"]
        kr = res.results[c]["kr_out"]
        gd = res.results[c]["gdn_out"]
        for s_ in range(2):
            ockv[2 * c + s_] = ck[:, :, :, s_ * 256:(s_ + 1) * 256].transpose(0, 3, 2, 1).reshape(2, 256, 256)
            okr[2 * c + s_] = kr[:, :, s_ * 256:(s_ + 1) * 256].transpose(0, 2, 1)
            ogdn[2 * c + s_] = gd[:, :, :, :, s_, :].transpose(0, 2, 3, 1, 4)
    return (yp, ys, ockv, okr, ogdn, mC, mn, mm_)
```

```python
import numpy as np
from contextlib import ExitStack
import concourse.bass as bass
import concourse.mybir as mybir
from concourse.bass_utils import run_bass_kernel_spmd

F32 = mybir.dt.float32
BF16 = mybir.dt.bfloat16
AF = mybir.ActivationFunctionType
ALU = mybir.AluOpType
AX = mybir.AxisListType

NCORES = 8
D = 1024
DEPTH = 4
DFF = 2816
NJ = 22
NTOK = 1536
NT = 3
EPS = 1e-6
MIXERS = True
NLAYERS = DEPTH
DEBUG = False
ONLY_ADA = False
FFN = True
LAYERS = None


class R:
    __slots__ = ("w", "rd")

    def __init__(self):
        self.w = None
        self.rd = {}


class K:
    NDMA = 24

    def __init__(self, nc, stack):
        self.nc = nc
        self.b = {"pe": nc.tensor, "act": nc.scalar, "dve": nc.vector, "pool": nc.gpsimd, "sp": nc.sync}
        self.sem, self.cnt, self.seen = {}, {}, {}
        for e in self.b:
            self.sem[e] = stack.enter_context(nc.semaphore("s_" + e))
            self.cnt[e] = 0
            self.seen[e] = {}
        self.dsem = [stack.enter_context(nc.semaphore("d%d" % i)) for i in range(self.NDMA)]
        self.dval = [0] * self.NDMA
        self.dnext = 0
        self.ccsem = stack.enter_context(nc.semaphore("cc"))
        self.ccval = 0
        self.stack = stack
        self.n_ins = 0
        self.uid = 0

    def sb(self, shape, dt=F32, st=None, name=None):
        self.uid += 1
        return (st or self.stack).enter_context(self.nc.sbuf_tensor("%s%d" % (name or "t", self.uid), list(shape), dt))

    def ps(self, shape, dt=F32, st=None, name=None):
        self.uid += 1
        return (st or self.stack).enter_context(self.nc.psum_tensor("%s%d" % (name or "p", self.uid), list(shape), dt))

    def _semobj(self, key):
        if isinstance(key, str):
            return self.ccsem if key == "cc" else self.sem[key]
        return self.dsem[key]

    def _wait(self, eng, deps):
        seen = self.seen[eng]
        for key, val in deps.items():
            if eng == "pe" and key == "pe":
                continue
            if seen.get(key, 0) < val:
                self.b[eng].wait_ge(self._semobj(key), val)
                seen[key] = val

    @staticmethod
    def _deps(reads, writes):
        deps = {}

        def add(kv):
            if kv is not None and deps.get(kv[0], 0) < kv[1]:
                deps[kv[0]] = kv[1]
        for r in reads:
            add(r.w)
        for w in writes:
            add(w.w)
            for kv in w.rd.items():
                add(kv)
        return deps

    def I(self, eng, reads, writes, emit, inc=True):
        self._wait(eng, self._deps(reads, writes))
        ins = emit(self.b[eng])
        self.n_ins += 1
        val = self.cnt[eng] + 1
        if inc:
            ins.then_inc(self.sem[eng], 1)
            self.cnt[eng] = val
        for w in writes:
            w.w = (eng, val)
            w.rd = {}
        for r in reads:
            if r.rd.get(eng, 0) < val:
                r.rd[eng] = val
        return ins

    def dma(self, q, out, in_, reads, writes, **kw):
        deps = self._deps(reads, writes)
        s = self.dnext
        self.dnext = (s + 1) % self.NDMA
        if self.dval[s] > 0:
            deps[s] = max(deps.get(s, 0), self.dval[s])
        self._wait(q, deps)
        ins = self.b[q].dma_start(out=out, in_=in_, **kw)
        self.dval[s] += 16
        ins.then_inc(self.dsem[s], 16)
        self.n_ins += 1
        for w in writes:
            w.w = (s, self.dval[s])
            w.rd = {}
        for r in reads:
            r.rd[s] = self.dval[s]
        return ins

    def cc(self, kind, op, groups, in_ap, out_ap, reads, writes):
        deps = self._deps(reads, writes)
        if self.ccval > 0:
            deps["cc"] = self.ccval
        self._wait("pool", deps)
        ins = self.b["pool"].collective_compute(kind, op, replica_groups=groups, ins=[in_ap], outs=[out_ap])
        self.ccval += 1
        ins.then_inc(self.ccsem, 1)
        for w in writes:
            w.w = ("cc", self.ccval)
            w.rd = {}
        for r in reads:
            r.rd["cc"] = self.ccval
        return ins

    def barrier(self):
        deps = {e: c for e, c in self.cnt.items() if c > 0}
        for s, v in enumerate(self.dval):
            if v > 0:
                deps[s] = v
        if self.ccval > 0:
            deps["cc"] = self.ccval
        for e in self.b:
            self._wait(e, dict(deps))

    def finish(self, regions):
        deps = {}
        for r in regions:
            if r.w is not None:
                deps[r.w[0]] = max(deps.get(r.w[0], 0), r.w[1])
        self._wait("sp", deps)


class Rot:
    def __init__(self, k, n, shape, dt, st=None, psum=False):
        self.t = [(k.ps if psum else k.sb)(shape, dt, st=st) for _ in range(n)]
        self.r = [R() for _ in range(n)]
        self.i = 0

    def next(self):
        i = self.i
        self.i = (i + 1) % len(self.t)
        return self.t[i], self.r[i]


def build_program():
    nc = bass.Bass("TRN2", target_bir_lowering=False)

    def din(name, shape):
        return nc.dram_tensor(name, list(shape), F32, kind="ExternalInput").ap()

    def dout(name, shape):
        return nc.dram_tensor(name, list(shape), F32, kind="ExternalOutput").ap()

    xT_d = din("xT", [128, 8, NTOK])
    cond_d = din("condT", [128, 8, 2])
    wada_d = din("w_ada", [DEPTH, 128, 8, 6 * D])
    bada_d = din("b_adaT", [128, DEPTH, 48])
    nm_d = din("nmT", [128, DEPTH, 8])
    nf_d = din("nfT", [128, DEPTH, 8])
    wfi_d = din("w_ffn_in", [DEPTH, 128, NJ, 8, 256])
    wfo_d = din("w_ffn_out", [DEPTH, 128, NJ, D])
    yT_d = dout("yT", [128, 8, NTOK])
    OD = {
        "p": dict(H=8, NTk=512, nseq=2, nchs=4,
                  w_in=din("wo_in_p", [2, 128, 8, 388 * 8]), w_out=din("wo_out_p", [2, 128, 8, D]),
                  gb=din("ogb_p", [2, 64, 32]), onorm=din("onorm_p", [2, 128, 8])),
        "s": dict(H=2, NTk=4096, nseq=1, nchs=64,
                  w_in=din("wo_in_s", [2, 128, 8, 388 * 2]), w_out=din("wo_out_s", [2, 128, 2, D]),
                  gb=din("ogb_s", [2, 64, 8]), onorm=din("onorm_s", [2, 128, 2]),
                  C0=din("oC0_s", [2, 64, 2, 2, 1, 129]), m0=din("om0_s", [2, 1, 2, 2, 1])),
    }
    oC_out = dout("oC_out", [2, 64, 2, 8, 2, 129])
    om_out = dout("om_out", [2, 1, 2, 8, 2])

    def ev_decl(sfx, H, extra):
        WE = 832 + 260 * H
        d_ = dict(H=H, w_in=din("we_in_" + sfx, [2, 128, 8, WE]), w_uq=din("we_uq_" + sfx, [2, 128, 3, H, 192]),
                  w_uk=din("we_uk_" + sfx, [2, 128, 2, H, 64]), w_uv=din("we_uv_" + sfx, [2, 128, 2, H * 64]),
                  w_out=din("we_out_" + sfx, [2, 64, 2, H, D]), conv=din("we_conv_" + sfx, [2, 64, 3, H, 3]),
                  alog=din("we_alog_" + sfx, [2, 64, 2 * H]), dtb=din("we_dtb_" + sfx, [2, 64, 2 * H]))
        d_.update(extra)
        return d_
    EVN = dict(qan=din("we_qan", [2, 128, 3]), kvan=din("we_kvan", [2, 128, 2]), qn=din("we_qn", [2, 96, 2]),
               kn=din("we_kn", [2, 96, 2]), onorm=din("we_onorm", [2, 64, 1]))
    EV = {
        "p": ev_decl("p", 8, dict(NTk=512, nseq=2, nchs=4)),
        "s": ev_decl("s", 2, dict(NTk=4096, nseq=1, nchs=64, ckv_ctx=din("we_ckvctx", [2, 128, 2, 256]),
                                  kr_ctx=din("we_krctx", [2, 96, 256]), S0=din("we_S0", [2, 64, 2, 2, 1, 64]),
                                  cos=din("rope_cos", [96, 4096]), sin=din("rope_sin", [96, 4096]))),
    }
    ckv_out = dout("ckv_out", [2, 128, 2, 512])
    kr_out = dout("kr_out", [2, 32, 512])
    gdn_out = dout("gdn_out", [2, 64, 2, 8, 2, 64])
    xpre_p = nc.dram_tensor("xpre_p", [24, 64, 2, 258], F32)
    xpre_s = nc.dram_tensor("xpre_s", [6, 64, 1, 4098], F32)
    ag_in = nc.dram_tensor("ag_in", [4, 256, 512], F32)
    ag_out = nc.dram_tensor("ag_out", [4, 1024, 512], F32)
    rs_in = nc.dram_tensor("rs_in", [4096, 1024], F32)
    rs_out = nc.dram_tensor("rs_out", [1024, 1024], F32)
    GROUPS = [[0, 1, 2, 3], [4, 5, 6, 7]]
    dbg = {}

    def dump(k, name, ap, shape, reads):
        if not DEBUG:
            return
        d = dout("dbg_" + name, shape)
        r = R()
        k.dma("sp", d, ap, reads, [r])
        dbg[name] = r

    with ExitStack() as st:
        k = K(nc, st)
        xT = k.sb([128, 8, NTOK], F32, name="xT")
        rx = [R() for _ in range(NT)]
        for t in range(NT):
            k.dma("sp", xT[:, :, t * 512:(t + 1) * 512], xT_d[:, :, t * 512:(t + 1) * 512], [], [rx[t]])
        ones_bf = k.sb([128, 128], BF16, name="ones")
        r_const = R()
        k.I("pool", [], [r_const], lambda e: e.memset(ones_bf[:], 1.0))
        modA = k.sb([128, DEPTH, 2, 8, 2], F32, name="modA")
        mod = k.sb([128, DEPTH, 48, 2], F32, name="mod")
        r_mod = R()
        nm = k.sb([128, DEPTH, 8], F32, name="nm")
        nf = k.sb([128, DEPTH, 8], F32, name="nf")
        bada = k.sb([128, DEPTH, 48], F32, name="bada")
        r_small = R()
        k.dma("sp", nm[:], nm_d, [], [r_small])
        k.dma("sp", nf[:], nf_d, [], [r_small])
        k.dma("sp", bada[:], bada_d, [], [r_small])

        with ExitStack() as ph:
            cond = k.sb([128, 8, 2], F32, st=ph)
            r_c = R()
            k.dma("sp", cond[:], cond_d, [], [r_c])
            k.I("act", [r_c], [r_c], lambda e: e.activation(out=cond[:], in_=cond[:], func=AF.Silu))
            dump(k, "cond", cond[:], [128, 8, 2], [r_c])
            wst = Rot(k, 2, [128, 8, 512], F32, st=ph)
            pmod = Rot(k, 2, [128, 48, 2], F32, st=ph, psum=True)
            for l in range(DEPTH):
                pm, rpm = pmod.next()
                for blk in range(12):
                    wt, rw = wst.next()
                    k.dma("pool" if blk % 2 else "sp", wt[:], wada_d[l, :, :, blk * 512:(blk + 1) * 512], [], [rw])
                    for mm in range(4):
                        m = blk * 4 + mm
                        for kc in range(8):
                            k.I("pe", [rw, r_c], [rpm], lambda e: e.matmul(
                                pm[:, m, :], lhsT=wt[:, kc, mm * 128:(mm + 1) * 128], rhs=cond[:, kc, :],
                                start=(kc == 0), stop=(kc == 7)), inc=(kc == 7 and mm == 3))
                k.I("dve", [rpm, r_small], [r_mod], lambda e: e.tensor_tensor(
                    out=mod[:, l], in0=pm[:], in1=bada[:, l, :].unsqueeze(2).to_broadcast([128, 48, 2]), op=ALU.add))
                for which, (nw, part) in enumerate(((nm, 1), (nf, 4))):
                    k.I("dve", [r_mod, r_small], [r_mod], lambda e: e.scalar_tensor_tensor(
                        out=modA[:, l, which], in0=mod[:, l, part * 8:(part + 1) * 8, :], scalar=1.0,
                        in1=nw[:, l, :].unsqueeze(2).to_broadcast([128, 8, 2]), op0=ALU.add, op1=ALU.mult))
            dump(k, "mod", mod[:], [128, DEPTH, 48, 2], [r_mod])
            dump(k, "modA", modA[:], [128, DEPTH, 2, 8, 2], [r_mod])
            k.barrier()

        def mvec(l, part, c, cond_i):
            return mod[:, l, part * 8 + c, cond_i:cond_i + 1]

        def norm_mod(ph, l, which, t, out_fn, rh, ps_ss, rps):
            cond_i = 0 if t == 0 else 1
            part_b = 0 if which == 0 else 3
            sl = slice(t * 512, (t + 1) * 512)
            sq, rsq = ph["sq"].next()
            k.I("act", [rx[t]], [rsq], lambda e: e.activation(out=sq[:], in_=xT[:, :, sl], func=AF.Square))
            for c in range(8):
                k.I("pe", [rsq, r_const], [rps], lambda e: e.matmul(
                    ps_ss[:], lhsT=ones_bf[:], rhs=sq[:, c, :], start=(c == 0), stop=(c == 7)), inc=(c == 7))
            rs, rrs = ph["rs"].next()
            k.I("act", [rps], [rrs], lambda e: e.activation(out=rs[:], in_=ps_ss[:], func=AF.Sqrt, bias=eps_t[:], scale=1.0 / D))
            k.I("dve", [rrs], [rrs], lambda e: e.reciprocal(out=rs[:], in_=rs[:]))
            for c in range(8):
                tmp, rtmp = ph["tmp"].next()
                k.I("dve", [rx[t], rrs], [rtmp], lambda e: e.tensor_tensor(out=tmp[:], in0=xT[:, c, sl], in1=rs[:], op=ALU.mult))
                k.I("act", [rtmp, r_mod], [rh], lambda e: e.activation(
                    out=out_fn(c), in_=tmp[:], func=AF.Identity,
                    bias=mvec(l, part_b, c, cond_i), scale=modA[:, l, which, c, cond_i:cond_i + 1]))

        eps_t = k.sb([128, 1], F32, name="eps")
        k.I("pool", [], [r_const], lambda e: e.memset(eps_t[:], EPS))

        def ffn(l):
            with ExitStack() as ph_st:
                ph = {"sq": Rot(k, 1, [128, 8, 512], BF16, st=ph_st),
                      "rs": Rot(k, 2, [128, 512], F32, st=ph_st),
                      "tmp": Rot(k, 2, [128, 512], F32, st=ph_st)}
                hT = k.sb([128, 8, NTOK], BF16, st=ph_st)
                rh = [R() for _ in range(NT)]
                ps_ss = k.ps([128, 512], F32, st=ph_st)
                rps = R()
                for t in range(NT):
                    norm_mod(ph, l, 1, t, (lambda c, t=t: hT[:, c, t * 512:(t + 1) * 512]), rh[t], ps_ss, rps)
                if l == 0 and DEBUG:
                    hf = k.sb([128, 8, 512], F32, st=ph_st)
                    rhf = R()
                    k.I("dve", rh, [rhf], lambda e: e.tensor_copy(out=hf[:], in_=hT[:, :, 0:512]))
                    dump(k, "h", hf[:], [128, 8, 512], [rhf])
                aT = k.sb([128, 11, NTOK], BF16, st=ph_st)
                ra = [R() for _ in range(NT)]
                wst = Rot(k, 2, [128, 8, 256], F32, st=ph_st)
                wbf = Rot(k, 2, [128, 8, 256], BF16, st=ph_st)
                wost = Rot(k, 2, [128, 11, 128], F32, st=ph_st)
                wobf = Rot(k, 2, [128, 11, 128], BF16, st=ph_st)
                psg = Rot(k, 2, [128, 512], F32, st=ph_st, psum=True)
                psu = Rot(k, 2, [128, 512], F32, st=ph_st, psum=True)
                pso = Rot(k, 2, [128, 512], F32, st=ph_st, psum=True)
                sgs = Rot(k, 2, [128, 512], BF16, st=ph_st)
                for half in range(2):
                    for jj in range(11):
                        j = half * 11 + jj
                        ws, rws = wst.next()
                        k.dma("sp", ws[:], wfi_d[l, :, j], [], [rws])
                        wb, rwb = wbf.next()
                        k.I("pool", [rws], [rwb], lambda e: e.tensor_copy(out=wb[:], in_=ws[:]))
                        for t in range(NT):
                            sl = slice(t * 512, (t + 1) * 512)
                            pg, rpg = psg.next()
                            pu, rpu = psu.next()
                            for c in range(8):
                                k.I("pe", [rwb, rh[t]], [rpg], lambda e: e.matmul(
                                    pg[:], lhsT=wb[:, c, 0:128], rhs=hT[:, c, sl], start=(c == 0), stop=(c == 7)), inc=(c == 7))
                            for c in range(8):
                                k.I("pe", [rwb, rh[t]], [rpu], lambda e: e.matmul(
                                    pu[:], lhsT=wb[:, c, 128:256], rhs=hT[:, c, sl], start=(c == 0), stop=(c == 7)), inc=(c == 7))
                            sg, rsg = sgs.next()
                            k.I("act", [rpg], [rsg], lambda e: e.activation(out=sg[:], in_=pg[:], func=AF.Silu))
                            k.I("dve", [rsg, rpu], [ra[t]], lambda e: e.tensor_tensor(out=aT[:, jj, sl], in0=sg[:], in1=pu[:], op=ALU.mult))
                    for c in range(8):
                        ws, rws = wost.next()
                        k.dma("sp", ws[:], wfo_d[l, :, half * 11:(half + 1) * 11, c * 128:(c + 1) * 128], [], [rws])
                        wb, rwb = wobf.next()
                        k.I("pool", [rws], [rwb], lambda e: e.tensor_copy(out=wb[:], in_=ws[:]))
                        for t in range(NT):
                            sl = slice(t * 512, (t + 1) * 512)
                            cond_i = 0 if t == 0 else 1
                            po, rpo = pso.next()
                            for jj in range(11):
                                k.I("pe", [rwb, ra[t]], [rpo], lambda e: e.matmul(
                                    po[:], lhsT=wb[:, jj, :], rhs=aT[:, jj, sl], start=(jj == 0), stop=(jj == 10)), inc=(jj == 10))
                            k.I("dve", [rpo, r_mod, rx[t]], [rx[t]], lambda e: e.scalar_tensor_tensor(
                                out=xT[:, c, sl], in0=po[:], scalar=mvec(l, 5, c, cond_i), in1=xT[:, c, sl],
                                op0=ALU.mult, op1=ALU.add))
                k.barrier()


        ident = k.sb([128, 128], F32, name="ident")
        maskU = k.sb([64, 64], F32, name="maskU")
        maskL = k.sb([64, 64], F32, name="maskL")
        ones_f = k.sb([64, 64], F32, name="ones_f")
        k.I("pool", [], [r_const], lambda e: e.memset(ident[:], 0.0))
        k.I("pool", [r_const], [r_const], lambda e: e.affine_select(
            out=ident[:], in_=ident[:], pattern=[[-1, 128]], compare_op=ALU.not_equal, fill=1.0, base=0, channel_multiplier=1))
        k.I("pool", [], [r_const], lambda e: e.memset(ones_f[:], 1.0))
        k.I("pool", [], [r_const], lambda e: e.memset(maskU[:], 1.0))
        k.I("pool", [r_const], [r_const], lambda e: e.affine_select(
            out=maskU[:], in_=maskU[:], pattern=[[1, 64]], compare_op=ALU.is_ge, fill=0.0, base=0, channel_multiplier=-1))
        k.I("pool", [], [r_const], lambda e: e.memset(maskL[:], 1.0))
        k.I("pool", [r_const], [r_const], lambda e: e.affine_select(
            out=maskL[:], in_=maskL[:], pattern=[[-1, 64]], compare_op=ALU.is_ge, fill=0.0, base=0, channel_multiplier=1))
        dump(k, "maskU", maskU[:], [64, 64], [r_const])

        def mm(rd, wr, out, lhsT, rhs, start=True, stop=True, inc=True):
            return k.I("pe", rd, wr, lambda e: e.matmul(out, lhsT=lhsT, rhs=rhs, start=start, stop=stop), inc=inc)

        def act(rd, wr, out, in_, func, **kw):
            return k.I("act", rd, wr, lambda e: e.activation(out=out, in_=in_, func=func, **kw))

        def tt(rd, wr, out, in0, in1, op, eng="dve"):
            return k.I(eng, rd, wr, lambda e: e.tensor_tensor(out=out, in0=in0, in1=in1, op=op))

        def stt(rd, wr, out, in0, scalar, in1, op0, op1):
            return k.I("dve", rd, wr, lambda e: e.scalar_tensor_tensor(out=out, in0=in0, scalar=scalar, in1=in1, op0=op0, op1=op1))

        def tsm(rd, wr, out, in0, scalar, eng="dve"):
            return k.I(eng, rd, wr, lambda e: e.tensor_scalar_mul(out=out, in0=in0, scalar1=scalar))

        def cp(rd, wr, out, in_, eng="dve"):
            if eng == "act":
                return k.I("act", rd, wr, lambda e: e.copy(out=out, in_=in_))
            return k.I(eng, rd, wr, lambda e: e.tensor_copy(out=out, in_=in_))

        class WStream:
            def __init__(self, st_, shape, n=2, nstage=1):
                self.ws = Rot(k, nstage, shape, F32, st=st_)
                self.wb = Rot(k, n, shape, BF16, st=st_)

            def load(self, dram_ap, sub=None):
                ws, r1 = self.ws.next()
                wb, r2 = self.wb.next()
                wsv, wbv = (ws[:], wb[:]) if sub is None else (sub(ws), sub(wb))
                k.dma("sp", wsv, dram_ap, [], [r1])
                k.I("pool", [r1], [r2], lambda e: e.tensor_copy(out=wbv, in_=wsv))
                return wb, r2

        def load_w(st_, dram_ap, shape, q="sp"):
            ws = k.sb(shape, F32, st=st_)
            r1 = R()
            k.dma(q, ws[:], dram_ap, [], [r1])
            wb = k.sb(shape, BF16, st=st_)
            r2 = R()
            k.I("pool", [r1], [r2], lambda e: e.tensor_copy(out=wb[:], in_=ws[:]))
            return wb, r2

        def odd_job(l, key, make_get_h, st_mem, make_sink):
            jb = OD[key]
            j = l // 2
            H, NTk, nseq, nchs = jb["H"], jb["NTk"], jb["nseq"], jb["nchs"]
            nch = nseq * nchs
            U = 2 * H
            ntile = NTk // 512
            NC_ = 2 * nch * H
            Q0, K0, V0, O0, G0 = 0, 64 * H, 128 * H, 256 * H, 384 * H
            WTOT = 388 * H
            memT = k.sb([128, H, NTk], F32, st=st_mem)
            r_mem = R()
            k.I("pool", [], [r_mem], lambda e: e.memset(memT[:], 0.0))
            Cst = k.sb([64, 2, H, nseq, 129], F32, st=st_mem)
            r_C = R()
            m0row = k.sb([1, 2, H, nseq], F32, st=st_mem)
            mfin = k.sb([1, 2, H, nseq], F32, st=st_mem)
            r_m0 = R()
            if "C0" in jb:
                k.dma("sp", Cst[:], jb["C0"][j], [], [r_C])
                k.dma("sp", m0row[:], jb["m0"][j], [], [r_m0])
            else:
                k.I("pool", [], [r_C], lambda e: e.memset(Cst[:], 0.0))
                k.I("pool", [], [r_m0], lambda e: e.memset(m0row[:], 0.0))
            st_main = ExitStack()
            qT = k.sb([64, H, NTk], BF16, st=st_main)
            kT = k.sb([64, H, NTk], BF16, st=st_main)
            r_q, r_k = R(), R()
            Ktm = k.sb([64, nch, H * 64], BF16, st=st_main)
            Vtm = k.sb([64, nch, H, 129], BF16, st=st_main)
            Gtm = k.sb([64, nch, 4 * H], F32, st=st_main)
            r_Ktm, r_V, r_G = R(), R(), R()
            k.I("pool", [], [r_V], lambda e: e.memset(Vtm[:, :, :, 128:129], 1.0))
            with ExitStack() as st_proj:
                get_h = make_get_h(st_proj)
                pfm = Rot(k, 2, [64, 512], F32, st=st_proj, psum=True)
                ptm = Rot(k, 2, [64, 256], F32, st=st_proj, psum=True)
                wstr = WStream(st_proj, [128, 8, 256])
                for b0 in range(0, WTOT, 256):
                    b1 = min(b0 + 256, WTOT)
                    if b0 >= O0 and b1 <= G0:
                        continue
                    if True:
                        wb, rwb = wstr.load(jb["w_in"][j, :, :, b0:b1], sub=lambda t_: t_[:, :, 0:b1 - b0])
                        for tt_ in range(ntile):
                            hts, rhts = get_h(tt_)
                            tsl = slice(tt_ * 512, (tt_ + 1) * 512)
                            for h in range(H):
                                for (base, dst, rdst, isk) in ((Q0, qT, r_q, False), (K0, kT, r_k, True)):
                                    c0 = base + 64 * h
                                    if not (b0 <= c0 < b1):
                                        continue
                                    pp, rpp = pfm.next()
                                    for kc in range(8):
                                        mm([rwb, rhts], [rpp], pp[:], wb[:, kc, c0 - b0:c0 - b0 + 64], hts[:, kc, :],
                                           start=(kc == 0), stop=(kc == 7), inc=(kc == 7))
                                    if isk:
                                        k.I("act", [rpp], [rdst], lambda e: e.mul(out=dst[:, h, tsl], in_=pp[:], mul=0.125))
                                    else:
                                        cp([rpp], [rdst], dst[:, h, tsl], pp[:], eng="act")
                            for (base, width) in ((K0, 64 * H), (V0, 128 * H), (G0, 4 * H)):
                                lo, hi = max(base, b0), min(base + width, b1)
                                if lo >= hi:
                                    continue
                                for cc in range(8):
                                    c = tt_ * 8 + cc
                                    pp, rpp = ptm.next()
                                    for kc in range(8):
                                        mm([rwb, rhts], [rpp], pp[:, 0:hi - lo], hts[:, kc, cc * 64:(cc + 1) * 64], wb[:, kc, lo - b0:hi - b0],
                                           start=(kc == 0), stop=(kc == 7), inc=(kc == 7))
                                    if base == K0:
                                        tsm([rpp], [r_Ktm], Ktm[:, c, lo - K0:hi - K0], pp[:, 0:hi - lo], 0.125)
                                    elif base == V0:
                                        h0, h1 = (lo - V0) // 128, (hi - V0) // 128
                                        cp([rpp], [r_V], Vtm[:, c, h0:h1, 0:128],
                                           pp[:, 0:hi - lo].rearrange("p (h d) -> p h d", d=128), eng="act")
                                    else:
                                        cp([rpp], [r_G], Gtm[:, c, lo - G0:hi - G0], pp[:, 0:hi - lo])
                k.barrier()
            G = k.sb([64, 4, nch, H], F32, st=st_main)
            gb = k.sb([64, 4 * H], F32, st=st_main)
            r_g = R()
            k.dma("sp", gb[:], jb["gb"][j], [], [r_g])
            tt([r_G, r_g], [r_g], G[:], Gtm[:].rearrange("p c (t h) -> p t c h", t=4),
               gb[:].rearrange("p (t h) -> p t h", t=4).unsqueeze(2).to_broadcast([64, 4, nch, H]), ALU.add)
            Lg = k.sb([64, 2, nch, H], F32, st=st_main)
            act([r_g], [r_g], Lg[:], G[:, 2:4], AF.Exp, scale=-1.0)
            act([r_g], [r_g], Lg[:], Lg[:], AF.Ln, bias=1.0)
            Cc = k.sb([64, 2, nch, H], F32, st=st_main)
            CU = k.sb([64, 2, nch, H], F32, st=st_main)
            rows = k.sb([1, 7, 2, nch, H], F32, st=st_main)
            rowp = k.sb([1, 8, 2, H, nseq, nchs], F32, st=st_main)
            BC = k.sb([64, 5, 2, nch, H], F32, st=st_main)
            wsr = k.sb([64, 2, nch, H], F32, st=st_main)
            ws0 = k.sb([64, 2, nch, H], F32, st=st_main)
            flo = k.sb([64, 2, nch, H], F32, st=st_main)
            Mcol = k.sb([128, 1], F32, st=st_main)
            r_row, r_bc, r_tok = R(), R(), R()
            with ExitStack() as st_g:
                pg = k.ps([64, 2, nch, H], F32, st=st_g)
                r_pg = R()
                nfl = nch * H
                mm([r_g, r_const], [r_pg], pg[:, 0].rearrange("p c h -> p (c h)"), maskU[:], Lg[:, 0].rearrange("p c h -> p (c h)"))
                mm([r_g, r_const], [r_pg], pg[:, 1].rearrange("p c h -> p (c h)"), maskL[:], Lg[:, 1].rearrange("p c h -> p (c h)"))
                tt([r_pg, r_g], [r_tok], Cc[:], G[:, 0:2], pg[:], ALU.add)
                cp([r_pg], [r_tok], CU[:], pg[:])
                ptr = k.ps([128, 64], F32, st=st_g)
                r_ptr = R()
                prow = k.ps([1, 512], F32, st=st_g)
                r_prow = R()
                Cflat = Cc[:].rearrange("p d c h -> p (d c h)")
                for g0 in range(0, NC_, 128):
                    k.I("pe", [r_tok, r_const], [r_ptr], lambda e: e.transpose(ptr[:], Cflat[:, g0:g0 + 128], ident[0:64, 0:64]))
                    k.I("dve", [r_ptr], [r_tok], lambda e: e.reduce_max(out=Mcol[:], in_=ptr[:], axis=AX.X))
                    mm([r_tok, r_const], [r_prow], prow[:, g0:g0 + 128], Mcol[:], ident[:])
                cp([r_prow], [r_row], rows[:, 0].rearrange("o d c h -> o (d c h)"), prow[:, 0:NC_])
                mm([r_g, r_const], [r_prow], prow[:, 0:NC_], ones_f[:, 0:1], Lg[:].rearrange("p d c h -> p (d c h)"))
                k.I("act", [r_prow], [r_row], lambda e: e.mul(out=rows[:, 1].rearrange("o d c h -> o (d c h)"), in_=prow[:, 0:NC_], mul=-1.0))

                def to_proc(dst_q, src_q):
                    for d in range(2):
                        src = rows[:, src_q, d].rearrange("o (s c) h -> o h s c", s=nseq)
                        if d == 1:
                            src = src[:, :, :, ::-1]
                        cp([r_row], [r_row], rowp[:, dst_q, d], src)

                def to_nat(dst_q, src_q):
                    for d in range(2):
                        dst = rows[:, dst_q, d].rearrange("o (s c) h -> o h s c", s=nseq)
                        src = rowp[:, src_q, d]
                        if d == 1:
                            src = src[:, :, :, ::-1]
                        cp([r_row], [r_row], dst, src)
                Mp, Bp, Gp, D0, D1, MS, MB, AA = range(8)
                MX = D1
                to_proc(Mp, 0)
                to_proc(Bp, 1)
                rp = lambda q: rowp[:, q]
                tt([r_row], [r_row], rp(Gp), rp(Bp), rp(Mp), ALU.add)
                cp([r_row], [r_row], rp(D0), rp(Bp))
                k.I("pool", [r_row], [r_row], lambda e: e.memset(rowp[:, D0, :, :, :, 0:1], -1e30))
                cp([r_row], [r_row], rp(D1), rp(Gp))
                tt([r_row, r_m0], [r_row], rowp[:, MS, :, :, :, 0], rowp[:, Bp, :, :, :, 0], m0row[:], ALU.add)
                tt([r_row], [r_row], rowp[:, D1, :, :, :, 0], rowp[:, MS, :, :, :, 0], rowp[:, Gp, :, :, :, 0], ALU.max)
                fl = lambda q: rowp[:, q].rearrange("o d h s c -> o (d h s c)")
                k.I("dve", [r_row], [r_row], lambda e: e.tensor_tensor_scan(
                    out=fl(MS), data0=fl(D0), data1=fl(D1), initial=0.0, op0=ALU.add, op1=ALU.max))
                cp([r_row, r_m0], [r_row], rowp[:, MB, :, :, :, 0], m0row[:])
                if nchs > 1:
                    cp([r_row], [r_row], rowp[:, MB, :, :, :, 1:nchs], rowp[:, MS, :, :, :, 0:nchs - 1])
                tt([r_row], [r_row], rp(MX), rp(MB), rp(Mp), ALU.max)
                tt([r_row], [r_row], rp(AA), rp(MB), rp(MX), ALU.subtract)
                act([r_row], [r_row], rp(AA), rp(AA), AF.Exp)
                to_nat(2, MX)
                to_nat(3, AA)
                DEC = AA
                tt([r_row], [r_row], rp(DEC), rp(Bp), rp(MB), ALU.add)
                tt([r_row], [r_row], rp(DEC), rp(DEC), rp(MS), ALU.subtract)
                act([r_row], [r_row], rp(DEC), rp(DEC), AF.Exp)
                tt([r_row], [r_row], rp(D0), rp(Gp), rp(MS), ALU.subtract)
                act([r_row], [r_row], rp(D0), rp(D0), AF.Exp)
                to_nat(4, DEC)
                to_nat(5, D0)
                cp([r_row], [r_row], rows[:, 6], rows[:, 0])
                src = rows[:, 2:7].rearrange("o q d c h -> o (q d c h)")
                dstf = BC[:].rearrange("p q d c h -> p (q d c h)")
                pb = Rot(k, 2, [64, 512], F32, st=st_g, psum=True)
                for n0 in range(0, 5 * NC_, 512):
                    n1 = min(n0 + 512, 5 * NC_)
                    pp, rpp = pb.next()
                    mm([r_row, r_const], [rpp], pp[:, 0:n1 - n0], ones_f[0:1, :], src[:, n0:n1])
                    cp([rpp], [r_bc], dstf[:, n0:n1], pp[:, 0:n1 - n0])
                tt([r_tok, r_bc], [r_tok], wsr[:], Cc[:], BC[:, 0], ALU.subtract)
                act([r_tok], [r_tok], wsr[:], wsr[:], AF.Exp)
                tt([r_tok, r_bc], [r_tok], ws0[:], Cc[:], BC[:, 4], ALU.subtract)
                act([r_tok], [r_tok], ws0[:], ws0[:], AF.Exp)
                tt([r_tok, r_bc], [r_tok], flo[:], CU[:], BC[:, 0], ALU.subtract)
                act([r_tok], [r_tok], flo[:], flo[:], AF.Exp)
                if key == "p":
                    cp([r_row], [r_row], mfin[:], rowp[:, MS, :, :, :, nchs - 1])
                    k.dma("sp", om_out[j], mfin[:], [r_row], [r_fin])
                k.barrier()
            with ExitStack() as st_l:
                pq = Rot(k, 2, [64, 64], F32, st=st_l, psum=True)
                px = Rot(k, 2, [64, 129], F32, st=st_l, psum=True)
                pu = Rot(k, 2, [64, 129], F32, st=st_l, psum=True)
                pt = Rot(k, 2, [128, 64], F32, st=st_l, psum=True)
                PTs = Rot(k, 2, [64, 64], BF16, st=st_l)
                qss = Rot(k, 2, [64, 64], F32, st=st_l)
                kwss = Rot(k, 2, [64, 64], BF16, st=st_l)
                dns = Rot(k, 2, [64, 1], F32, st=st_l)
                hts_ = Rot(k, 2, [64, 128], F32, st=st_l)
                tmps = Rot(k, 2, [64, 129], F32, st=st_l)
                for cc in range(nchs):
                    for s_ in range(nseq):
                        for d in range(2):
                            c = s_ * nchs + (cc if d == 0 else nchs - 1 - cc)
                            tok = slice(c * 64, (c + 1) * 64)
                            msk = maskU if d == 0 else maskL
                            for h in range(H):
                                col = lambda t_: t_[:, d, c, h:h + 1]
                                p1, rp1 = pq.next()
                                mm([r_k, r_q], [rp1], p1[:], kT[:, h, tok], qT[:, h, tok])
                                PT, rPT = PTs.next()
                                stt([rp1, r_tok, r_const], [rPT], PT[:], p1[:], col(wsr), msk[:], ALU.mult, ALU.mult)
                                qs, rqs = qss.next()
                                k.I("act", [r_q, r_bc], [rqs], lambda e: e.mul(out=qs[:], in_=qT[:, h, tok], mul=BC[:, 1, d, c, h:h + 1]))
                                p2, rp2 = px.next()
                                mm([rPT, r_V], [rp2], p2[:], PT[:], Vtm[:, c, h, :], start=True, stop=False, inc=False)
                                mm([rqs, r_C], [rp2], p2[:], qs[:], Cst[:, d, h, s_, :], start=False, stop=True)
                                kws, rkws = kwss.next()
                                tsm([r_Ktm, r_tok], [rkws], kws[:], Ktm[:, c, h * 64:(h + 1) * 64], col(ws0), eng="pool")
                                p3, rp3 = pu.next()
                                mm([rkws, r_V], [rp3], p3[:], kws[:], Vtm[:, c, h, :])
                                dn, rdn = dns.next()
                                act([rp2], [rdn], dn[:], p2[:, 128:129], AF.Abs)
                                tt([rdn, r_tok], [rdn], dn[:], dn[:], col(flo), ALU.max)
                                k.I("dve", [rdn], [rdn], lambda e: e.reciprocal(out=dn[:], in_=dn[:]))
                                hh, rhh = hts_.next()
                                tsm([rp2, rdn], [rhh], hh[:], p2[:, 0:128], dn[:, 0:1])
                                p4, rp4 = pt.next()
                                k.I("pe", [rhh, r_const], [rp4], lambda e: e.transpose(p4[:], hh[:], ident[0:64, 0:64]))
                                tt([rp4, r_mem], [r_mem], memT[:, h, tok], memT[:, h, tok], p4[:], ALU.add)
                                tmp, rtmp = tmps.next()
                                k.I("act", [rp3, r_bc], [rtmp], lambda e: e.mul(out=tmp[:], in_=p3[:], mul=BC[:, 3, d, c, h:h + 1]))
                                stt([r_C, r_bc, rtmp], [r_C], Cst[:, d, h, s_, :], Cst[:, d, h, s_, :], BC[:, 2, d, c, h:h + 1], tmp[:], ALU.mult, ALU.add)
                if key == "p":
                    k.dma("sp", oC_out[j], Cst[:], [r_C], [r_fin])
                k.barrier()
            st_main.close()
            with ExitStack() as st_f:
                get_h = make_get_h(st_f)
                out_sink = make_sink(st_f)
                onw = k.sb([128, H], F32, st=st_f)
                r_on = R()
                k.dma("sp", onw[:], jb["onorm"][j], [], [r_on])
                wo_in, r_woin = wres(st_f, jb["w_in"][j, :, :, O0:O0 + 128 * H], [128, 8, 256]) if H == 2 else (None, None)
                wo_str = WStream(st_f, [128, 8, 128]) if H != 2 else None
                wout_str = WStream(st_f, [128, H, 128])
                ps1 = Rot(k, 2, [128, 512], F32, st=st_f, psum=True)
                ps2 = Rot(k, 2, [128, 512], F32, st=st_f, psum=True)
                ps3 = Rot(k, 2, [128, 512], F32, st=st_f, psum=True)
                sqs = Rot(k, 2, [128, 512], BF16, st=st_f)
                rss = Rot(k, 2, [128, 512], F32, st=st_f)
                sgs = Rot(k, 2, [128, 512], F32, st=st_f)
                mixT = k.sb([128, H, 512], BF16, st=st_f)
                r_mix = R()
                for tt_ in range(ntile):
                    hts, rhts = get_h(tt_)
                    tsl = slice(tt_ * 512, (tt_ + 1) * 512)
                    for h in range(H):
                        if H == 2:
                            wo, rwo, c0 = wo_in, r_woin, h * 128
                            stw = None
                        else:
                            stw = None
                            wo, rwo = wo_str.load(jb["w_in"][j, :, :, O0 + h * 128:O0 + (h + 1) * 128])
                            c0 = 0
                        sq, rsq = sqs.next()
                        act([r_mem], [rsq], sq[:], memT[:, h, tsl], AF.Square)
                        p1, rp1 = ps1.next()
                        mm([rsq, r_const], [rp1], p1[:], ones_bf[:], sq[:])
                        rs, rrs = rss.next()
                        act([rp1], [rrs], rs[:], p1[:], AF.Sqrt, bias=eps_t[:], scale=1.0 / 128)
                        k.I("dve", [rrs], [rrs], lambda e: e.reciprocal(out=rs[:], in_=rs[:]))
                        p2, rp2 = ps2.next()
                        for kc in range(8):
                            mm([rwo, rhts], [rp2], p2[:], wo[:, kc, c0:c0 + 128], hts[:, kc, :], start=(kc == 0), stop=(kc == 7), inc=(kc == 7))
                        sg, rsg = sgs.next()
                        act([rp2], [rsg], sg[:], p2[:], AF.Sigmoid)
                        tt([r_mem, rrs], [rrs], rs[:], memT[:, h, tsl], rs[:], ALU.mult)
                        stt([rrs, r_on, rsg], [r_mix], mixT[:, h, :], rs[:], onw[:, h:h + 1], sg[:], ALU.mult, ALU.mult)
                        if stw is not None:
                            k.barrier()
                            stw.close()
                    for c in range(8):
                        if True:
                            wout, rwout = wout_str.load(jb["w_out"][j, :, :, c * 128:(c + 1) * 128])
                            p3, rp3 = ps3.next()
                            for h in range(H):
                                mm([rwout, r_mix], [rp3], p3[:], wout[:, h, :], mixT[:, h, :], start=(h == 0), stop=(h == H - 1), inc=(h == H - 1))
                            out_sink(tt_, c, p3, rp3)
                k.barrier()


        maskUs = k.sb([64, 64], F32, name="maskUs")
        maskLs = k.sb([64, 64], F32, name="maskLs")
        ident_bf = k.sb([64, 64], BF16, name="ident_bf")
        ones128 = k.sb([128, 64], F32, name="ones128")
        k.I("pool", [r_const], [r_const], lambda e: e.tensor_tensor(out=maskUs[:], in0=maskU[:], in1=ident[0:64, 0:64], op=ALU.subtract))
        k.I("pool", [r_const], [r_const], lambda e: e.tensor_tensor(out=maskLs[:], in0=maskL[:], in1=ident[0:64, 0:64], op=ALU.subtract))
        k.I("pool", [r_const], [r_const], lambda e: e.tensor_copy(out=ident_bf[:], in_=ident[0:64, 0:64]))
        k.I("pool", [], [r_const], lambda e: e.memset(ones128[:], 1.0))
        zpad = k.sb([64, 24 * 2], F32, name="zpad")
        k.I("pool", [], [r_const], lambda e: e.memset(zpad[:], 0.0))
        r_xpre = R()
        for (xp_, n_, ns_, L_) in ((xpre_p, 24, 2, 256), (xpre_s, 6, 1, 4096)):
            for col in (0, L_ + 1):
                for i_ in range(n_):
                    k.dma("sp", xp_.ap()[i_, :, :, col:col + 1], zpad[:, 0:ns_].unsqueeze(2), [r_const], [r_xpre],
                          allow_slow_non_contiguous=True)

        def wres(st_, dram_ap, shape):
            wb = k.sb(shape, BF16, st=st_)
            r2 = R()
            with ExitStack() as tmp:
                ws = k.sb(shape, F32, st=tmp)
                r1 = R()
                k.dma("sp", ws[:], dram_ap, [], [r1])
                k.I("pool", [r1], [r2], lambda e: e.tensor_copy(out=wb[:], in_=ws[:]))
                k.barrier()
            return wb, r2

        def rstd_of(rd, wr, out, ps, n, rows=128):
            act(rd, wr, out, ps, AF.Sqrt, bias=eps_t[0:rows, :], scale=1.0 / n)
            k.I("dve", wr, wr, lambda e: e.reciprocal(out=out, in_=out))

        def even_job(l, key, make_get_h, st_mem, make_sink):
            jb = EV[key]
            j = l // 2
            H, NTk, nseq, nchs = jb["H"], jb["NTk"], jb["nseq"], jb["nchs"]
            nch = nseq * nchs
            ntile = NTk // 512
            samp = key == "s"
            NCTX = 256 if samp else 0
            nkt = (NTk + NCTX) // 128
            Ls = nchs * 64
            CQ, CKV, KR, KRR, GQ = 0, 384, 640, 736, 832
            GK, GV, GZ, GA = GQ + 64 * H, GQ + 128 * H, GQ + 192 * H, GQ + 256 * H
            WE = GA + 4 * H
            xpre = (xpre_s if samp else xpre_p).ap()
            SC = 96 ** -0.5
            mixA = k.sb([64, H, NTk], BF16, st=st_mem)
            Sst = k.sb([64, 2, H, nseq, 64], F32, st=st_mem)
            r_mixA, r_oT, r_S, r_par = R(), R(), R(), R()
            if samp:
                k.dma("sp", Sst[:], jb["S0"][j], [], [r_S])
            else:
                k.I("pool", [], [r_S], lambda e: e.memset(Sst[:], 0.0))
            qan = k.sb([128, 3], F32, st=st_mem)
            kvan = k.sb([128, 2], F32, st=st_mem)
            qn = k.sb([96, 2], F32, st=st_mem)
            kn = k.sb([96, 2], F32, st=st_mem)
            onorm = k.sb([64, 1], F32, st=st_mem)
            convw = k.sb([64, 3, H, 3], F32, st=st_mem)
            alog = k.sb([64, 2 * H], F32, st=st_mem)
            dtb = k.sb([64, 2 * H], F32, st=st_mem)
            for t_, d_ in ((qan, EVN["qan"]), (kvan, EVN["kvan"]), (qn, EVN["qn"]), (kn, EVN["kn"]), (onorm, EVN["onorm"]),
                           (convw, jb["conv"]), (alog, jb["alog"]), (dtb, jb["dtb"])):
                k.dma("sp", t_[:], d_[j], [], [r_par])
            act([r_par], [r_par], alog[:], alog[:], AF.Exp)
            k.I("act", [r_par], [r_par], lambda e: e.mul(out=alog[:], in_=alog[:], mul=-1.0))

            with ExitStack() as st_a:
                get_h = make_get_h(st_a)
                qT = k.sb([96, H, NTk], BF16, st=st_a)
                kT = k.sb([96, H, NTk + NCTX], BF16, st=st_a)
                Vt = k.sb([128, nkt, H, 65], BF16, st=st_a)
                r_qT, r_kT, r_Vt = R(), R(), R()
                k.I("pool", [], [r_Vt], lambda e: e.memset(Vt[:, :, :, 64:65], 1.0))
                wsh, r_wsh = wres(st_a, jb["w_in"][j, :, :, 0:832], [128, 8, 832])
                uq, r_uq = wres(st_a, jb["w_uq"][j], [128, 3, H, 192])
                uk, r_uk = wres(st_a, jb["w_uk"][j], [128, 2, H, 64])
                uv, r_uv = wres(st_a, jb["w_uv"][j], [128, 2, H * 64])
                with ExitStack() as st_t:
                    pA = Rot(k, 2, [128, 512], F32, st=st_t, psum=True)
                    pSm = Rot(k, 2, [128, 512], F32, st=st_t, psum=True)
                    pVv = Rot(k, 2, [128, 512], F32, st=st_t, psum=True)
                    cqf = k.sb([128, 3, 512], F32, st=st_t)
                    cqn = k.sb([128, 3, 512], BF16, st=st_t)
                    ckf = k.sb([128, 2, 512], F32, st=st_t)
                    ckb = k.sb([128, 2, 512], BF16, st=st_t)
                    krf = k.sb([96, 512], F32, st=st_t)
                    krrf = k.sb([96, 512], F32, st=st_t)
                    kpre = k.sb([96, 512], F32, st=st_t)
                    sqb = k.sb([128, 3, 512], BF16, st=st_t)
                    rsb = k.sb([128, 512], F32, st=st_t)
                    tmpA = Rot(k, 2, [128, 512], F32, st=st_t)
                    tmpB = Rot(k, 2, [128, 512], F32, st=st_t)
                    cosb = k.sb([96, 512], F32, st=st_t)
                    sinb = k.sb([96, 512], F32, st=st_t)
                    r_cq, r_ck, r_kr, r_kp, r_sq, r_rs, r_cs = R(), R(), R(), R(), R(), R(), R()

                    def kv_side(n, ksl, kt0, rope):
                        cp([r_kr], [r_kp], kpre[64:96, 0:n], krf[64:96, 0:n], eng="act")
                        for h in range(H):
                            pp, rpp = pA.next()
                            for kc in range(2):
                                mm([r_uk, r_ck], [rpp], pp[0:64, 0:n], uk[:, kc, h, :], ckb[:, kc, 0:n], start=(kc == 0), stop=(kc == 1), inc=(kc == 1))
                            cp([rpp], [r_kp], kpre[0:64, 0:n], pp[0:64, 0:n], eng="act")
                            act([r_kp], [r_sq], sqb[0:96, 0, 0:n], kpre[:, 0:n], AF.Square)
                            ps_, rps_ = pSm.next()
                            mm([r_sq, r_const], [rps_], ps_[0:96, 0:n], ones_bf[0:96, 0:96], sqb[0:96, 0, 0:n])
                            rstd_of([rps_], [r_rs], rsb[0:96, 0:n], ps_[0:96, 0:n], 96, rows=96)
                            t1, rt1 = tmpA.next()
                            tt([r_kp, r_rs], [rt1], t1[0:96, 0:n], kpre[:, 0:n], rsb[0:96, 0:n], ALU.mult)
                            if not rope:
                                k.I("act", [rt1, r_par], [r_kT], lambda e: e.mul(out=kT[:, h, ksl], in_=t1[0:96, 0:n], mul=kn[:, 0:1]))
                            else:
                                k.I("act", [rt1, r_par], [r_kT], lambda e: e.mul(out=kT[0:64, h, ksl], in_=t1[0:64, 0:n], mul=kn[0:64, 0:1]))
                                k.I("act", [rt1, r_par], [rt1], lambda e: e.mul(out=t1[64:96, 0:n], in_=t1[64:96, 0:n], mul=kn[64:96, 0:1]))
                                tt([rt1, r_cs], [rt1], t1[64:96, 0:n], t1[64:96, 0:n], cosb[64:96, 0:n], ALU.mult)
                                t2, rt2 = tmpB.next()
                                tt([r_kr, r_rs], [rt2], t2[64:96, 0:n], krrf[64:96, 0:n], rsb[64:96, 0:n], ALU.mult)
                                k.I("act", [rt2, r_par], [rt2], lambda e: e.mul(out=t2[64:96, 0:n], in_=t2[64:96, 0:n], mul=kn[64:96, 1:2]))
                                tt([rt2, r_cs], [rt2], t2[64:96, 0:n], t2[64:96, 0:n], sinb[64:96, 0:n], ALU.mult)
                                tt([rt1, rt2], [r_kT], kT[64:96, h, ksl], t1[64:96, 0:n], t2[64:96, 0:n], ALU.add)
                        for q_ in range(n // 128):
                            pp, rpp = pVv.next()
                            for kc in range(2):
                                mm([r_uv, r_ck], [rpp], pp[:, 0:H * 64], ckb[:, kc, q_ * 128:(q_ + 1) * 128], uv[:, kc, :], start=(kc == 0), stop=(kc == 1), inc=(kc == 1))
                            cp([rpp], [r_Vt], Vt[:, kt0 + q_, :, 0:64], pp[:, 0:H * 64].rearrange("p (h d) -> p h d", d=64), eng="act")

                    if samp:
                        k.dma("sp", ckf[:, :, 0:256], jb["ckv_ctx"][j], [], [r_ck])
                        cp([r_ck], [r_ck], ckb[:, :, 0:256], ckf[:, :, 0:256])
                        k.dma("sp", krf[:, 0:256], jb["kr_ctx"][j], [], [r_kr])
                        kv_side(256, slice(0, 256), 0, False)
                    for tt_ in range(ntile):
                        hts, rhts = get_h(tt_)
                        tsl = slice(tt_ * 512, (tt_ + 1) * 512)
                        ksl = slice(NCTX + tt_ * 512, NCTX + (tt_ + 1) * 512)
                        if samp:
                            k.dma("sp", cosb[64:96, :], jb["cos"][64:96, tsl], [], [r_cs])
                            k.dma("sp", sinb[64:96, :], jb["sin"][64:96, tsl], [], [r_cs])
                        for c in range(3):
                            pp, rpp = pA.next()
                            for kc in range(8):
                                mm([r_wsh, rhts], [rpp], pp[:], wsh[:, kc, CQ + c * 128:CQ + (c + 1) * 128], hts[:, kc, :], start=(kc == 0), stop=(kc == 7), inc=(kc == 7))
                            cp([rpp], [r_cq], cqf[:, c, :], pp[:], eng="act")
                        act([r_cq], [r_sq], sqb[:], cqf[:], AF.Square)
                        ps_, rps_ = pSm.next()
                        for c in range(3):
                            mm([r_sq, r_const], [rps_], ps_[:], ones_bf[:], sqb[:, c, :], start=(c == 0), stop=(c == 2), inc=(c == 2))
                        rstd_of([rps_], [r_rs], rsb[:], ps_[:], 384)
                        for c in range(3):
                            t1, rt1 = tmpA.next()
                            tt([r_cq, r_rs], [rt1], t1[:], cqf[:, c, :], rsb[:], ALU.mult)
                            k.I("act", [rt1, r_par], [r_cq], lambda e: e.mul(out=cqn[:, c, :], in_=t1[:], mul=qan[:, c:c + 1]))
                        for c in range(2):
                            pp, rpp = pA.next()
                            for kc in range(8):
                                mm([r_wsh, rhts], [rpp], pp[:], wsh[:, kc, CKV + c * 128:CKV + (c + 1) * 128], hts[:, kc, :], start=(kc == 0), stop=(kc == 7), inc=(kc == 7))
                            cp([rpp], [r_ck], ckf[:, c, :], pp[:], eng="act")
                        act([r_ck], [r_sq], sqb[:, 0:2, :], ckf[:], AF.Square)
                        ps_, rps_ = pSm.next()
                        for c in range(2):
                            mm([r_sq, r_const], [rps_], ps_[:], ones_bf[:], sqb[:, c, :], start=(c == 0), stop=(c == 1), inc=(c == 1))
                        rstd_of([rps_], [r_rs], rsb[:], ps_[:], 256)
                        for c in range(2):
                            tt([r_ck, r_rs], [r_ck], ckf[:, c, :], ckf[:, c, :], rsb[:], ALU.mult)
                            k.I("act", [r_ck, r_par], [r_ck], lambda e: e.mul(out=ckf[:, c, :], in_=ckf[:, c, :], mul=kvan[:, c:c + 1]))
                        cp([r_ck], [r_ck], ckb[:], ckf[:])
                        if not samp:
                            k.dma("sp", ckv_out[j], ckf[:], [r_ck], [r_fin])
                        for (c0, dst) in ((KR, krf), (KRR, krrf)):
                            if dst is krrf and not samp:
                                continue
                            pp, rpp = pA.next()
                            for kc in range(8):
                                mm([r_wsh, rhts], [rpp], pp[0:96, :], wsh[:, kc, c0:c0 + 96], hts[:, kc, :], start=(kc == 0), stop=(kc == 7), inc=(kc == 7))
                            cp([rpp], [r_kr], dst[64:96, :], pp[64:96, :], eng="act")
                        if not samp:
                            k.dma("sp", kr_out[j], krf[64:96, :], [r_kr], [r_fin])
                        for h in range(H):
                            pp, rpp = pA.next()
                            for kc in range(3):
                                mm([r_uq, r_cq], [rpp], pp[0:96, :], uq[:, kc, h, 0:96], cqn[:, kc, :], start=(kc == 0), stop=(kc == 2), inc=(kc == 2))
                            t0, rt0 = tmpB.next()
                            cp([rpp], [rt0], t0[0:96, :], pp[0:96, :], eng="act")
                            act([rt0], [r_sq], sqb[0:96, 0, :], t0[0:96, :], AF.Square)
                            ps_, rps_ = pSm.next()
                            mm([r_sq, r_const], [rps_], ps_[0:96, :], ones_bf[0:96, 0:96], sqb[0:96, 0, :])
                            rstd_of([rps_], [r_rs], rsb[0:96, :], ps_[0:96, :], 96, rows=96)
                            t1, rt1 = tmpA.next()
                            tt([rt0, r_rs], [rt1], t1[0:96, :], t0[0:96, :], rsb[0:96, :], ALU.mult)
                            if not samp:
                                k.I("act", [rt1, r_par], [r_qT], lambda e: e.mul(out=qT[:, h, tsl], in_=t1[0:96, :], mul=qn[:, 0:1]))
                            else:
                                k.I("act", [rt1, r_par], [r_qT], lambda e: e.mul(out=qT[0:64, h, tsl], in_=t1[0:64, :], mul=qn[0:64, 0:1]))
                                k.I("act", [rt1, r_par], [rt1], lambda e: e.mul(out=t1[64:96, :], in_=t1[64:96, :], mul=qn[64:96, 0:1]))
                                tt([rt1, r_cs], [rt1], t1[64:96, :], t1[64:96, :], cosb[64:96, :], ALU.mult)
                                pp2, rpp2 = pA.next()
                                for kc in range(3):
                                    mm([r_uq, r_cq], [rpp2], pp2[0:96, :], uq[:, kc, h, 96:192], cqn[:, kc, :], start=(kc == 0), stop=(kc == 2), inc=(kc == 2))
                                t2, rt2 = tmpB.next()
                                tt([rpp2, r_rs], [rt2], t2[64:96, :], pp2[64:96, :], rsb[64:96, :], ALU.mult)
                                k.I("act", [rt2, r_par], [rt2], lambda e: e.mul(out=t2[64:96, :], in_=t2[64:96, :], mul=qn[64:96, 1:2]))
                                tt([rt2, r_cs], [rt2], t2[64:96, :], t2[64:96, :], sinb[64:96, :], ALU.mult)
                                tt([rt1, rt2], [r_qT], qT[64:96, h, tsl], t1[64:96, :], t2[64:96, :], ALU.add)
                        kv_side(512, ksl, NCTX // 128 + tt_ * 4, samp)
                    k.barrier()
                with ExitStack() as st_t:
                    pSc = Rot(k, 3, [128, 512], F32, st=st_t, psum=True)
                    pO = Rot(k, 2, [128, 512], F32, st=st_t, psum=True)
                    pB = Rot(k, 1, [64, 512], F32, st=st_t, psum=True)
                    Pb = Rot(k, 3, [128, 512], BF16, st=st_t)
                    rdb = Rot(k, 2, [128, 512], F32, st=st_t)
                    bcb = Rot(k, 2, [64, 512], F32, st=st_t)
                    if samp:
                        blocks = [(slice(b * 512, (b + 1) * 512), 512, list(range(nkt))) for b in range(ntile)]
                    else:
                        blocks = [(slice(s_ * 256, (s_ + 1) * 256), 256, [2 * s_, 2 * s_ + 1]) for s_ in range(nseq)]
                    for h in range(H):
                        for (qsl, n, kts) in blocks:
                            po, rpo = pO.next()
                            AH = 2
                            sq_ = []

                            def issue_s(kt_):
                                ps__, rps__ = pSc.next()
                                mm([r_kT, r_qT], [rps__], ps__[:, 0:n], kT[:, h, kt_ * 128:(kt_ + 1) * 128], qT[:, h, qsl])
                                sq_.append((ps__, rps__))
                            for kt_ in kts[:AH]:
                                issue_s(kt_)
                            for i_, kt in enumerate(kts):
                                ps_, rps_ = sq_.pop(0)
                                if i_ + AH < len(kts):
                                    issue_s(kts[i_ + AH])
                                P, rP = Pb.next()
                                act([rps_], [rP], P[:, 0:n], ps_[:, 0:n], AF.Exp, scale=SC)
                                mm([r_Vt, rP], [rpo], po[0:65, 0:n], Vt[:, kt, h, :], P[:, 0:n], start=(i_ == 0), stop=(i_ == len(kts) - 1), inc=(i_ == len(kts) - 1))
                            rd, rrd = rdb.next()
                            k.I("dve", [rpo], [rrd], lambda e: e.reciprocal(out=rd[64:65, 0:n], in_=po[64:65, 0:n]))
                            pb_, rpb = pB.next()
                            mm([rrd, r_const], [rpb], pb_[:, 0:n], ones128[64:65, :], rd[64:65, 0:n])
                            bc, rbc = bcb.next()
                            cp([rpb], [rbc], bc[:, 0:n], pb_[:, 0:n], eng="act")
                            tt([rpo, rbc], [r_mixA], mixA[:, h, qsl], po[0:64, 0:n], bc[:, 0:n], ALU.mult)
                    k.barrier()
            oT = k.sb([64, H, NTk], F32, st=st_mem)
            k.I("pool", [], [r_oT], lambda e: e.memset(oT[:], 0.0))
            with ExitStack() as st_b:
                qg = k.sb([64, H, NTk], BF16, st=st_b)
                kg = k.sb([64, H, NTk], BF16, st=st_b)
                Vtm = k.sb([64, nch, H, 64], BF16, st=st_b)
                Gtm = k.sb([64, nch, 4 * H], F32, st=st_b)
                r_qg, r_kg, r_Ktm, r_Vtm, r_G = R(), R(), R(), R(), R()
                with ExitStack() as st_p:
                    get_h = make_get_h(st_p)
                    pfm = Rot(k, 2, [64, 512], F32, st=st_p, psum=True)
                    ptm = Rot(k, 2, [64, 64], F32, st=st_p, psum=True)
                    xsb = Rot(k, 2, [64, 512], F32, st=st_p)
                    wstr = WStream(st_p, [128, 8, 256])
                    for b0 in range(GQ, WE, 256):
                        b1 = min(b0 + 256, WE)
                        if b0 >= GZ and b1 <= GA:
                            continue
                        if True:
                            wb, rwb = wstr.load(jb["w_in"][j, :, :, b0:b1], sub=lambda t_: t_[:, :, 0:b1 - b0])
                            for tt_ in range(ntile):
                                hts, rhts = get_h(tt_)
                                for T_ in range(3):
                                    for h in range(H):
                                        c0 = GQ + (T_ * H + h) * 64
                                        if not (b0 <= c0 < b1):
                                            continue
                                        pp, rpp = pfm.next()
                                        for kc in range(8):
                                            mm([rwb, rhts], [rpp], pp[:], wb[:, kc, c0 - b0:c0 - b0 + 64], hts[:, kc, :], start=(kc == 0), stop=(kc == 7), inc=(kc == 7))
                                        xs, rxs = xsb.next()
                                        cp([rpp], [rxs], xs[:], pp[:], eng="act")
                                        if samp:
                                            k.dma("sp", xpre[T_ * H + h][:, 0, 1 + tt_ * 512:1 + (tt_ + 1) * 512], xs[:], [rxs], [r_xpre])
                                        else:
                                            k.dma("sp", xpre[T_ * H + h][:, :, 1:257], xs[:].rearrange("p (s t) -> p s t", s=2), [rxs], [r_xpre])
                                lo, hi = max(GA, b0), min(GA + 4 * H, b1)
                                if lo < hi:
                                    for cc in range(8):
                                        c = tt_ * 8 + cc
                                        pp, rpp = ptm.next()
                                        for kc in range(8):
                                            mm([rwb, rhts], [rpp], pp[:, 0:hi - lo], hts[:, kc, cc * 64:(cc + 1) * 64], wb[:, kc, lo - b0:hi - b0], start=(kc == 0), stop=(kc == 7), inc=(kc == 7))
                                        cp([rpp], [r_G], Gtm[:, c, lo - GA:hi - GA], pp[:, 0:hi - lo])
                    k.barrier()
                with ExitStack() as st_c:
                    xin = k.sb([64, nseq, Ls + 2], F32, st=st_c)
                    yb = k.sb([64, nseq, Ls], F32, st=st_c)
                    r_xin, r_y = R(), R()
                    pss = Rot(k, 2, [64, 512], F32, st=st_c, psum=True)
                    ptr = Rot(k, 2, [64, 8, 64], F32, st=st_c, psum=True)
                    sqc = Rot(k, 2, [64, 512], BF16, st=st_c)
                    rsc = Rot(k, 2, [64, 512], F32, st=st_c)
                    ynf = Rot(k, 2, [64, 512], F32, st=st_c)
                    yfl = yb[:].rearrange("p s t -> p (s t)")
                    for T_ in range(3):
                        for h in range(H):
                            k.dma("sp", xin[:], xpre[T_ * H + h], [r_xpre], [r_xin])
                            k.I("dve", [r_xin, r_par], [r_y], lambda e: e.tensor_scalar_mul(out=yb[:], in0=xin[:, :, 1:Ls + 1], scalar1=convw[:, T_, h, 1:2]))
                            stt([r_xin, r_par, r_y], [r_y], yb[:], xin[:, :, 0:Ls], convw[:, T_, h, 0:1], yb[:], ALU.mult, ALU.add)
                            stt([r_xin, r_par, r_y], [r_y], yb[:], xin[:, :, 2:Ls + 2], convw[:, T_, h, 2:3], yb[:], ALU.mult, ALU.add)
                            act([r_y], [r_y], yb[:], yb[:], AF.Silu)
                            for tt_ in range(ntile):
                                tsl = slice(tt_ * 512, (tt_ + 1) * 512)
                                if T_ < 2:
                                    sq, rsq = sqc.next()
                                    act([r_y], [rsq], sq[:], yfl[:, tsl], AF.Square)
                                    ps_, rps_ = pss.next()
                                    mm([rsq, r_const], [rps_], ps_[:], ones_bf[0:64, 0:64], sq[:])
                                    rs, rrs = rsc.next()
                                    rstd_of([rps_], [rrs], rs[:], ps_[:], 1.0, rows=64)
                                    yn, ryn = ynf.next()
                                    tt([r_y, rrs], [ryn], yn[:], yfl[:, tsl], rs[:], ALU.mult)
                                    if T_ == 0:
                                        k.I("act", [ryn], [r_qg], lambda e: e.mul(out=qg[:, h, tsl], in_=yn[:], mul=0.125))
                                        continue
                                    cp([ryn], [r_kg], kg[:, h, tsl], yn[:], eng="act")
                                    continue
                                else:
                                    src_t, rsrc = None, r_y
                                    dst_t, rdst = Vtm, r_Vtm
                                pt_, rpt = ptr.next()
                                for cc in range(8):
                                    src = (src_t[:, cc * 64:(cc + 1) * 64] if src_t is not None else yfl[:, tt_ * 512 + cc * 64:tt_ * 512 + (cc + 1) * 64])
                                    k.I("pe", [rsrc, r_const], [rpt], lambda e: e.transpose(pt_[:, cc, :], src, ident[0:64, 0:64]), inc=(cc == 7))
                                cp([rpt], [rdst], dst_t[:, tt_ * 8:(tt_ + 1) * 8, h, :], pt_[:])
                    k.barrier()
                Gg = k.sb([64, 4, nch, H], F32, st=st_b)
                GCs = k.sb([64, 2, nch, H], F32, st=st_b)
                BET = k.sb([64, 2, nch, H], F32, st=st_b)
                KBG = k.sb([64, 2, nch, H], F32, st=st_b)
                KDE = k.sb([64, 2, nch, H], F32, st=st_b)
                GLb = k.sb([64, 2, nch, H], F32, st=st_b)
                r_gt = R()
                with ExitStack() as st_g:
                    pg = k.ps([64, 2, nch, H], F32, st=st_g)
                    pt2 = k.ps([64, 2, nch, H], F32, st=st_g)
                    r_pg, r_pt2 = R(), R()
                    cp([r_G], [r_gt], Gg[:], Gtm[:].rearrange("p c (t h) -> p t c h", t=4))
                    tt([r_gt, r_par], [r_gt], Gg[:, 0:2], Gg[:, 0:2], dtb[:].rearrange("p (d h) -> p d h", d=2).unsqueeze(2).to_broadcast([64, 2, nch, H]), ALU.add)
                    act([r_gt], [r_gt], Gg[:, 0:2], Gg[:, 0:2], AF.Exp)
                    act([r_gt], [r_gt], Gg[:, 0:2], Gg[:, 0:2], AF.Ln, bias=1.0)
                    tt([r_gt, r_par], [r_gt], Gg[:, 0:2], Gg[:, 0:2], alog[:].rearrange("p (d h) -> p d h", d=2).unsqueeze(2).to_broadcast([64, 2, nch, H]), ALU.mult)
                    act([r_gt], [r_gt], BET[:], Gg[:, 2:4], AF.Sigmoid)
                    fl3 = lambda t_: t_.rearrange("p c h -> p (c h)")
                    mm([r_gt, r_const], [r_pg], fl3(pg[:, 0]), maskU[:], fl3(Gg[:, 0]))
                    mm([r_gt, r_const], [r_pg], fl3(pg[:, 1]), maskL[:], fl3(Gg[:, 1]))
                    cp([r_pg], [r_gt], GCs[:], pg[:])
                    mm([r_gt, r_const], [r_pt2], pt2[:].rearrange("p d c h -> p (d c h)"), ones_f[:], Gg[:, 0:2].rearrange("p d c h -> p (d c h)"))
                    act([r_pt2], [r_gt], GLb[:], pt2[:], AF.Exp)
                    tt([r_pt2, r_gt], [r_gt], KDE[:], pt2[:], GCs[:], ALU.subtract)
                    act([r_gt], [r_gt], KDE[:], KDE[:], AF.Exp)
                    act([r_gt], [r_gt], KBG[:], GCs[:], AF.Exp)
                    tt([r_gt], [r_gt], KBG[:], KBG[:], BET[:], ALU.mult)
                    k.barrier()
                with ExitStack() as st_l:
                    pR = Rot(k, 2, [64, 8, 64], F32, st=st_l, psum=True)
                    pM = Rot(k, 4, [64, 8, 64], F32, st=st_l, psum=True)
                    pKt = Rot(k, 1, [64, 8, 64], BF16, st=st_l, psum=True)
                    pSt = Rot(k, 1, [64, 3, 64], F32, st=st_l, psum=True)
                    B3 = lambda: k.sb([64, 8, 64], F32, st=st_l)
                    Dg, Zb, DmT, DmTs, EgR, qd, NTb, Nb, Xb, attnT, Ma, Mb, MTa, MTb, vb, kbg, kd, Ub, WTb = [B3() for _ in range(19)]
                    kbT = k.sb([64, 8, 64], BF16, st=st_l)
                    vns = Rot(k, 2, [64, 64], F32, st=st_l)
                    r_sp = R()
                    r_vn = R()
                    nsp = nch // 8
                    fl = lambda t_: t_[:].rearrange("p c i -> p (c i)")
                    bcI = ident[0:64, 0:64].unsqueeze(1).to_broadcast([64, 8, 64])
                    for si in range(nsp):
                        for d in range(2):
                            sp_ = si if d == 0 else nsp - 1 - si
                            c0 = sp_ * 8
                            tsl = slice(c0 * 64, c0 * 64 + 512)
                            mI, mS = (maskU, maskUs) if d == 0 else (maskL, maskLs)
                            for h in range(H):
                                colb = lambda t_: t_[:, d, c0:c0 + 8, h:h + 1].to_broadcast([64, 8, 64])
                                tt([r_gt, r_const], [r_sp], Dg[:], bcI, colb(GCs), ALU.mult, eng="pool")
                                p1, rp1 = pR.next()
                                mm([r_sp, r_const], [rp1], fl(p1), ones_f[:], fl(Dg))
                                tt([rp1, r_gt], [r_sp], Zb[:], p1[:], colb(GCs), ALU.subtract)
                                k.I("dve", [r_sp], [r_sp], lambda e: e.tensor_scalar_min(out=Zb[:], in0=Zb[:], scalar1=0.0))
                                act([r_sp], [r_sp], DmT[:], Zb[:], AF.Exp)
                                tt([r_sp, r_const], [r_sp], DmTs[:], DmT[:], mS[:].unsqueeze(1).to_broadcast([64, 8, 64]), ALU.mult, eng="pool")
                                tt([r_sp, r_const], [r_sp], DmT[:], DmT[:], mI[:].unsqueeze(1).to_broadcast([64, 8, 64]), ALU.mult, eng="pool")
                                act([rp1], [r_sp], EgR[:], p1[:], AF.Exp)
                                tt([r_qg, r_sp], [r_sp], fl(qd), qg[:, h, tsl], fl(EgR), ALU.mult)
                                tt([r_gt, r_const], [r_sp], Dg[:], bcI, colb(BET), ALU.mult, eng="pool")
                                p2, rp2 = pR.next()
                                mm([r_sp, r_const], [rp2], fl(p2), ones_f[:], fl(Dg))
                                tt([r_kg, rp2], [r_sp], fl(kbT), kg[:, h, tsl], fl(p2), ALU.mult)
                                pa, rpa = pM.next()
                                pq_, rpq = pM.next()
                                for cc in range(8):
                                    csl = slice(c0 * 64 + cc * 64, c0 * 64 + (cc + 1) * 64)
                                    mm([r_kg, r_sp], [rpa], pa[:, cc, :], kg[:, h, csl], kbT[:, cc, :], inc=(cc == 7))
                                for cc in range(8):
                                    csl = slice(c0 * 64 + cc * 64, c0 * 64 + (cc + 1) * 64)
                                    mm([r_kg, r_qg], [rpq], pq_[:, cc, :], kg[:, h, csl], qg[:, h, csl], inc=(cc == 7))
                                stt([rpa, r_sp], [r_sp], NTb[:], pa[:], -1.0, DmTs[:], ALU.mult, ALU.mult)
                                tt([rpq, r_sp], [r_sp], attnT[:], pq_[:], DmT[:], ALU.mult)
                                pn, rpn = pM.next()
                                for cc in range(8):
                                    k.I("pe", [r_sp, r_const], [rpn], lambda e: e.transpose(pn[:, cc, :], NTb[:, cc, :], ident[0:64, 0:64]), inc=(cc == 7))
                                cp([rpn], [r_sp], Nb[:], pn[:], eng="act")
                                tt([r_sp, r_const], [r_sp], Xb[:], NTb[:], bcI, ALU.add, eng="pool")
                                M_, MT_ = Nb, NTb
                                bufs = [(Ma, MTa), (Mb, MTb)]
                                for rnd in range(5):
                                    Mn, MTn = bufs[rnd % 2]
                                    pm_, rpm_ = pM.next()
                                    for cc in range(8):
                                        mm([r_sp], [rpm_], pm_[:, cc, :], MT_[:, cc, :], M_[:, cc, :], inc=(cc == 7))
                                    cp([rpm_], [r_sp], Mn[:], pm_[:], eng="act")
                                    if rnd < 4:
                                        pmt, rpmt = pM.next()
                                        for cc in range(8):
                                            mm([r_sp], [rpmt], pmt[:, cc, :], M_[:, cc, :], MT_[:, cc, :], inc=(cc == 7))
                                        cp([rpmt], [r_sp], MTn[:], pmt[:], eng="act")
                                    px_, rpx = pM.next()
                                    for cc in range(8):
                                        mm([r_sp], [rpx], px_[:, cc, :], Mn[:, cc, :], Xb[:, cc, :], inc=(cc == 7))
                                    tt([rpx, r_sp], [r_sp], Xb[:], Xb[:], px_[:], ALU.add)
                                    M_, MT_ = Mn, MTn
                                pk_, rpk = pKt.next()
                                for cc in range(8):
                                    csl = slice(c0 * 64 + cc * 64, c0 * 64 + (cc + 1) * 64)
                                    k.I("pe", [r_kg, r_const], [rpk], lambda e: e.transpose(pk_[:, cc, :], kg[:, h, csl], ident_bf[:]), inc=(cc == 7))
                                tt([rpk, r_gt], [r_sp], kbg[:], pk_[:], colb(KBG), ALU.mult)
                                tt([rpk, r_gt], [r_sp], kd[:], pk_[:], colb(KDE), ALU.mult)
                                tt([r_Vtm, r_gt], [r_sp], vb[:], Vtm[:, c0:c0 + 8, h, :], colb(BET), ALU.mult, eng="pool")
                                pu_, rpu = pM.next()
                                for cc in range(8):
                                    mm([r_sp], [rpu], pu_[:, cc, :], Xb[:, cc, :], vb[:, cc, :], inc=(cc == 7))
                                cp([rpu], [r_sp], Ub[:], pu_[:], eng="act")
                                pw_, rpw = pM.next()
                                for cc in range(8):
                                    mm([r_sp], [rpw], pw_[:, cc, :], kbg[:, cc, :], Xb[:, cc, :], inc=(cc == 7))
                                cp([rpw], [r_sp], WTb[:], pw_[:], eng="act")
                                order = range(8) if d == 0 else range(7, -1, -1)
                                for cc in order:
                                    c = c0 + cc
                                    s_ = c // nchs
                                    csl = slice(c * 64, (c + 1) * 64)
                                    S_ = Sst[:, d, h, s_, :]
                                    ps3, rps3 = pSt.next()
                                    mm([r_sp, r_S], [rps3], ps3[:, 0, :], WTb[:, cc, :], S_)
                                    vn, rvn = vns.next()
                                    tt([r_sp, rps3], [rvn], vn[:], Ub[:, cc, :], ps3[:, 0, :], ALU.subtract)
                                    mm([r_S, r_sp], [rps3], ps3[:, 1, :], S_, qd[:, cc, :], start=True, stop=False, inc=False)
                                    mm([rvn, r_sp], [rps3], ps3[:, 1, :], vn[:], attnT[:, cc, :], start=False, stop=True)
                                    mm([r_sp, rvn], [rps3], ps3[:, 2, :], kd[:, cc, :], vn[:])
                                    tt([rps3, r_oT], [r_oT], oT[:, h, csl], oT[:, h, csl], ps3[:, 1, :], ALU.add)
                                    stt([r_S, r_gt, rps3], [r_S], S_, S_, GLb[:, d, c, h:h + 1], ps3[:, 2, :], ALU.mult, ALU.add)
                    if not samp:
                        k.dma("sp", gdn_out[j], Sst[:], [r_S], [r_fin])
                    k.barrier()
            with ExitStack() as st_f:
                get_h = make_get_h(st_f)
                out_sink = make_sink(st_f)
                ps1 = Rot(k, 2, [64, 512], F32, st=st_f, psum=True)
                ps2 = Rot(k, 2, [64, 512], F32, st=st_f, psum=True)
                ps3 = Rot(k, 2, [128, 512], F32, st=st_f, psum=True)
                sqs = Rot(k, 2, [64, 512], BF16, st=st_f)
                rss = Rot(k, 2, [64, 512], F32, st=st_f)
                zs = Rot(k, 2, [64, 512], F32, st=st_f)
                mixD = k.sb([64, H, 512], BF16, st=st_f)
                r_mixD = R()
                wz_res = wres(st_f, jb["w_in"][j, :, :, GZ:GZ + 64 * H], [128, 8, 64 * H]) if H == 2 else None
                wz_str = WStream(st_f, [128, 8, 64]) if H != 2 else None
                wout_str = WStream(st_f, [64, 2, H, 128])
                for tt_ in range(ntile):
                    hts, rhts = get_h(tt_)
                    tsl = slice(tt_ * 512, (tt_ + 1) * 512)
                    for h in range(H):
                        if wz_res is not None:
                            (wz, rwz), c0, stw = wz_res, h * 64, None
                        else:
                            stw = None
                            wz, rwz = wz_str.load(jb["w_in"][j, :, :, GZ + h * 64:GZ + (h + 1) * 64])
                            c0 = 0
                        sq, rsq = sqs.next()
                        act([r_oT], [rsq], sq[:], oT[:, h, tsl], AF.Square)
                        p1, rp1 = ps1.next()
                        mm([rsq, r_const], [rp1], p1[:], ones_bf[0:64, 0:64], sq[:])
                        rs, rrs = rss.next()
                        rstd_of([rp1], [rrs], rs[:], p1[:], 64, rows=64)
                        p2, rp2 = ps2.next()
                        for kc in range(8):
                            mm([rwz, rhts], [rp2], p2[:], wz[:, kc, c0:c0 + 64], hts[:, kc, :], start=(kc == 0), stop=(kc == 7), inc=(kc == 7))
                        z, rz = zs.next()
                        act([rp2], [rz], z[:], p2[:], AF.Silu)
                        tt([r_oT, rrs], [rrs], rs[:], oT[:, h, tsl], rs[:], ALU.mult)
                        stt([rrs, r_par, rz], [r_mixD], mixD[:, h, :], rs[:], onorm[:, 0:1], z[:], ALU.mult, ALU.mult)
                        if stw is not None:
                            k.barrier()
                            stw.close()
                    for c in range(8):
                        if True:
                            wout, rwout = wout_str.load(jb["w_out"][j, :, :, :, c * 128:(c + 1) * 128])
                            p3, rp3 = ps3.next()
                            for h in range(H):
                                mm([rwout, r_mixA], [rp3], p3[:], wout[:, 0, h, :], mixA[:, h, tsl], start=(h == 0), stop=False, inc=False)
                            for h in range(H):
                                mm([rwout, r_mixD], [rp3], p3[:], wout[:, 1, h, :], mixD[:, h, :], start=False, stop=(h == H - 1), inc=(h == H - 1))
                            out_sink(tt_, c, p3, rp3)
                k.barrier()

        def mixer(l):
            j = l // 2
            job_fn = odd_job if l % 2 == 1 else even_job
            r_agin, r_agout, r_rsin, r_rsout = R(), R(), R(), R()
            with ExitStack() as st_h:
                hTp = k.sb([128, 8, 512], BF16, st=st_h)
                rhp = R()
                with ExitStack() as st_n:
                    ph = {"sq": Rot(k, 1, [128, 8, 512], BF16, st=st_n),
                          "rs": Rot(k, 2, [128, 512], F32, st=st_n),
                          "tmp": Rot(k, 2, [128, 512], F32, st=st_n)}
                    ps_ss = k.ps([128, 512], F32, st=st_n)
                    rps = R()
                    hTs = k.sb([128, 8, 1024], BF16, st=st_n)
                    rhs_ = R()
                    for t in (1, 2):
                        norm_mod(ph, l, 0, t, (lambda c, t=t: hTs[:, c, (t - 1) * 512:t * 512]), rhs_, ps_ss, rps)
                    norm_mod(ph, l, 0, 0, (lambda c: hTp[:, c, :]), rhp, ps_ss, rps)
                    for q_ in range(4):
                        k.dma("sp", ag_in.ap()[q_].bitcast(BF16).rearrange("(c p) t -> p c t", p=128),
                              hTs[:, 2 * q_:2 * q_ + 2, :], [rhs_], [r_agin])
                    for q_ in range(4):
                        k.cc("AllGather", ALU.bypass, GROUPS, ag_in.ap()[q_], ag_out.ap()[q_], [r_agin], [r_agout])
                    k.barrier()

                def sink_p(tt_, c, ps, rps_):
                    k.I("dve", [rps_, r_mod, rx[0]], [rx[0]], lambda e: e.scalar_tensor_tensor(
                        out=xT[:, c, 0:512], in0=ps[:], scalar=mvec(l, 2, c, 0), in1=xT[:, c, 0:512], op0=ALU.mult, op1=ALU.add))
                with ExitStack() as st_mem:
                    job_fn(l, "p", lambda st_: (lambda tt_: (hTp, rhp)), st_mem, lambda st_: sink_p)
                    k.barrier()
            with ExitStack() as st_s:
                agv = [ag_out.ap()[q_].bitcast(BF16).rearrange("(r c p) t -> r p c t", r=4, c=2) for q_ in range(4)]

                def make_get_hs(st_):
                    hbuf = Rot(k, 2, [128, 8, 512], BF16, st=st_)

                    def get_hs(tt_):
                        hb, rhb = hbuf.next()
                        for q_ in range(4):
                            k.dma("sp", hb[:, 2 * q_:2 * q_ + 2, :], agv[q_][tt_ // 2][:, :, (tt_ % 2) * 512:(tt_ % 2 + 1) * 512], [r_agout], [rhb])
                        return hb, rhb
                    return get_hs
                rsv = rs_in.ap().rearrange("(r c p) t -> r c p t", r=4, c=8)

                def make_sink_s(st_):
                    osb = Rot(k, 2, [128, 512], F32, st=st_)

                    def sink_s(tt_, c, ps, rps_):
                        ob, rob = osb.next()
                        cp([rps_], [rob], ob[:], ps[:], eng="act")
                        k.dma("sp", rsv[tt_ // 2, c][:, (tt_ % 2) * 512:(tt_ % 2 + 1) * 512], ob[:], [rob], [r_rsin])
                    return sink_s
                with ExitStack() as st_mem:
                    job_fn(l, "s", make_get_hs, st_mem, make_sink_s)
                    k.barrier()
                k.cc("ReduceScatter", ALU.add, GROUPS, rs_in.ap(), rs_out.ap(), [r_rsin], [r_rsout])
                rso = rs_out.ap().rearrange("(c p) t -> p c t", p=128)
                stg = Rot(k, 2, [128, 8, 512], F32, st=st_s)
                for t in (1, 2):
                    sg_, rsg_ = stg.next()
                    k.dma("sp", sg_[:], rso[:, :, (t - 1) * 512:t * 512], [r_rsout], [rsg_])
                    for c in range(8):
                        k.I("dve", [rsg_, r_mod, rx[t]], [rx[t]], lambda e: e.scalar_tensor_tensor(
                            out=xT[:, c, t * 512:(t + 1) * 512], in0=sg_[:, c, :], scalar=mvec(l, 2, c, 1),
                            in1=xT[:, c, t * 512:(t + 1) * 512], op0=ALU.mult, op1=ALU.add))
                k.barrier()

        r_fin = R()
        for l in ([] if ONLY_ADA else (LAYERS if LAYERS is not None else range(NLAYERS))):
            if MIXERS:
                mixer(l)
            if FFN:
                ffn(l)

        r_out = R()
        for t in range(NT):
            k.dma("sp", yT_d[:, :, t * 512:(t + 1) * 512], xT[:, :, t * 512:(t + 1) * 512], [rx[t]], [r_out])
            k.finish([r_out])
        k.finish(list(dbg.values()) + [r_fin])
        print("n_ins", k.n_ins)
    return nc


def to_fm(x):
    n = x.shape[0]
    return np.ascontiguousarray(x.reshape(n, 8, 128).transpose(2, 1, 0))


def from_fm(y):
    n = y.shape[2]
    return np.ascontiguousarray(y.transpose(2, 1, 0).reshape(n, 1024))


def prep_inputs(inp):
    f = lambda a: np.asarray(a, dtype=np.float32)
    xp, xs = f(inp["x_prompt"]), f(inp["x_sample"])
    shared = {}
    shared["w_ada"] = np.ascontiguousarray(f(inp["w_ada"]).reshape(DEPTH, 8, 128, 6 * D).transpose(0, 2, 1, 3))
    shared["b_adaT"] = np.ascontiguousarray(f(inp["b_ada"]).reshape(DEPTH, 48, 128).transpose(2, 0, 1))
    shared["nmT"] = np.ascontiguousarray(f(inp["norm_mix"]).reshape(DEPTH, 8, 128).transpose(2, 0, 1))
    shared["nfT"] = np.ascontiguousarray(f(inp["norm_ffn"]).reshape(DEPTH, 8, 128).transpose(2, 0, 1))
    wfi = f(inp["w_ffn_in"])
    g = wfi[:, :, :DFF].reshape(DEPTH, 8, 128, NJ, 128)
    u = wfi[:, :, DFF:].reshape(DEPTH, 8, 128, NJ, 128)
    gu = np.concatenate([g, u], axis=-1)
    shared["w_ffn_in"] = np.ascontiguousarray(gu.transpose(0, 2, 3, 1, 4))
    shared["w_ffn_out"] = np.ascontiguousarray(f(inp["w_ffn_out"]).reshape(DEPTH, NJ, 128, D).transpose(0, 2, 1, 3))
    maps = []
    for c in range(NCORES):
        g_, r_ = c // 4, c % 4
        toks = np.concatenate([xp[2 * c], xp[2 * c + 1], xs[g_, r_ * 1024:(r_ + 1) * 1024]], axis=0)
        m = dict(shared)
        m["xT"] = to_fm(toks)
        cnd = np.stack([f(inp["c_ctx"]), f(inp["c"])[g_]], axis=-1)
        m["condT"] = np.ascontiguousarray(cnd.reshape(8, 128, 2).transpose(1, 0, 2))
        maps.append(m)
    return maps


def _pm(w):
    return np.ascontiguousarray(w.reshape(8, 128, -1).transpose(1, 0, 2))


def prep_odd(inp, maps):
    f = lambda a: np.asarray(a, dtype=np.float32)
    w_in, w_out = f(inp["w_odd_in"]), f(inp["w_odd_out"])
    gbias, onorm = f(inp["mlstm_gate_bias"]), f(inp["mlstm_out_norm"])
    C0, n0, m0 = f(inp["state_mlstm_C"]), f(inp["state_mlstm_n"]), f(inp["state_mlstm_m"])

    def pack_in(j, heads):
        w = w_in[j]
        q = np.concatenate([w[:, h * 64:(h + 1) * 64] for h in heads], 1)
        kk = np.concatenate([w[:, 512 + h * 64:512 + (h + 1) * 64] for h in heads], 1)
        v = np.concatenate([w[:, 1024 + h * 128:1024 + (h + 1) * 128] for h in heads], 1)
        o = np.concatenate([w[:, 2048 + h * 128:2048 + (h + 1) * 128] for h in heads], 1)
        g = np.concatenate([w[:, 3072 + t * 8 + h:3072 + t * 8 + h + 1] for t in range(4) for h in heads], 1)
        return _pm(np.concatenate([q, kk, v, o, g], 1))

    def pack_out(j, heads):
        return np.ascontiguousarray(np.stack([w_out[j, h * 128:(h + 1) * 128, :] for h in heads], 1))

    def pack_gb(j, heads):
        row = np.concatenate([gbias[j, t, heads] for t in range(4)])
        return np.ascontiguousarray(np.tile(row[None, :], (64, 1)))

    def pack_on(j, heads):
        return np.ascontiguousarray(np.stack([onorm[j, h * 128:(h + 1) * 128] for h in heads], 1))

    allh = list(range(8))
    shared = {
        "wo_in_p": np.stack([pack_in(j, allh) for j in range(2)]),
        "wo_out_p": np.stack([pack_out(j, allh) for j in range(2)]),
        "ogb_p": np.stack([pack_gb(j, allh) for j in range(2)]),
        "onorm_p": np.stack([pack_on(j, allh) for j in range(2)]),
    }
    per_r = {}
    for r in range(4):
        hs = [2 * r, 2 * r + 1]
        per_r[r] = {
            "wo_in_s": np.stack([pack_in(j, hs) for j in range(2)]),
            "wo_out_s": np.stack([pack_out(j, hs) for j in range(2)]),
            "ogb_s": np.stack([pack_gb(j, hs) for j in range(2)]),
            "onorm_s": np.stack([pack_on(j, hs) for j in range(2)]),
        }
    for c in range(NCORES):
        g_, r_ = c // 4, c % 4
        hs = [2 * r_, 2 * r_ + 1]
        m = maps[c]
        m.update(shared)
        m.update(per_r[r_])
        Cs = C0[g_][:, :, hs]
        ns = n0[g_][:, :, hs]
        aug = np.concatenate([Cs, ns[..., None]], -1)
        m["oC0_s"] = np.ascontiguousarray(aug.transpose(0, 3, 1, 2, 4)[:, :, :, :, None, :])
        m["om0_s"] = np.ascontiguousarray(m0[g_][:, :, hs][:, None, :, :, None])

_ROPE_PERM = np.array([(r // 16) * 16 + ((r % 16) + 8) % 16 for r in range(32)])
_ROPE_SIGN = np.array([-1.0 if (r % 16) < 8 else 1.0 for r in range(32)], np.float32)


def _rope_tables():
    t = np.arange(4096)
    row, col = (t // 64).astype(np.float32), (t % 64).astype(np.float32)
    inv = (1.0 / (10000.0 ** (np.arange(0, 16, 2, dtype=np.float32) / 16))).astype(np.float32)
    ar, ac = row[:, None] * inv, col[:, None] * inv
    ang = np.concatenate([ar, ar, ac, ac], -1)
    cos = np.zeros((96, 4096), np.float32)
    sin = np.zeros((96, 4096), np.float32)
    cos[64:] = np.cos(ang).T
    sin[64:] = (np.sin(ang) * _ROPE_SIGN[None, :]).T
    return cos, sin


def prep_even(inp, maps):
    f = lambda a: np.asarray(a, dtype=np.float32)
    w_in, w_out = f(inp["w_even_in"]), f(inp["w_even_out"])
    w_uq, w_ukv = f(inp["w_mla_uq"]), f(inp["w_mla_ukv"])
    conv, alog, dtb = f(inp["gdn_conv"]), f(inp["gdn_a_log"]), f(inp["gdn_dt_bias"])
    qn, kn = f(inp["mla_q_norm"]), f(inp["mla_k_norm"])

    def pack_in(j, heads):
        w = w_in[j]
        z64 = np.zeros((1024, 64), np.float32)
        kr = w[:, 640:672]
        parts = [w[:, 0:384], w[:, 384:640], z64, kr, z64, kr[:, _ROPE_PERM]]
        for T_ in range(4):
            parts += [w[:, 672 + T_ * 512 + h * 64:672 + T_ * 512 + (h + 1) * 64] for h in heads]
        for gbase in (2720, 2736):
            parts += [w[:, gbase + d * 8 + h:gbase + d * 8 + h + 1] for d in range(2) for h in heads]
        return _pm(np.concatenate(parts, 1))

    def pack_uq(j, heads):
        out = np.zeros((384, len(heads), 192), np.float32)
        for i, h in enumerate(heads):
            out[:, i, 0:96] = w_uq[j][:, h * 96:(h + 1) * 96]
            out[:, i, 160:192] = w_uq[j][:, h * 96 + 64 + _ROPE_PERM]
        return np.ascontiguousarray(out.reshape(3, 128, len(heads), 192).transpose(1, 0, 2, 3))

    def pack_uk(j, heads):
        out = np.stack([w_ukv[j][:, h * 128:h * 128 + 64] for h in heads], 1)
        return np.ascontiguousarray(out.reshape(2, 128, len(heads), 64).transpose(1, 0, 2, 3))

    def pack_uv(j, heads):
        out = np.concatenate([w_ukv[j][:, h * 128 + 64:(h + 1) * 128] for h in heads], 1)
        return np.ascontiguousarray(out.reshape(2, 128, -1).transpose(1, 0, 2))

    def pack_out(j, heads):
        a = np.stack([w_out[j][h * 64:(h + 1) * 64] for h in heads], 1)
        dlt = np.stack([w_out[j][512 + h * 64:512 + (h + 1) * 64] for h in heads], 1)
        return np.ascontiguousarray(np.stack([a, dlt], 1))

    def pack_conv(j, heads):
        out = np.zeros((64, 3, len(heads), 3), np.float32)
        for T_ in range(3):
            for i, h in enumerate(heads):
                out[:, T_, i, :] = conv[j][:, T_ * 512 + h * 64:T_ * 512 + (h + 1) * 64].T
        return out

    def rep(v, heads):
        row = np.concatenate([v[d, heads] for d in range(2)])
        return np.ascontiguousarray(np.tile(row[None, :], (64, 1)))

    def packs(sfx, heads):
        return {
            "we_in_" + sfx: np.stack([pack_in(j, heads) for j in range(2)]),
            "we_uq_" + sfx: np.stack([pack_uq(j, heads) for j in range(2)]),
            "we_uk_" + sfx: np.stack([pack_uk(j, heads) for j in range(2)]),
            "we_uv_" + sfx: np.stack([pack_uv(j, heads) for j in range(2)]),
            "we_out_" + sfx: np.stack([pack_out(j, heads) for j in range(2)]),
            "we_conv_" + sfx: np.stack([pack_conv(j, heads) for j in range(2)]),
            "we_alog_" + sfx: np.stack([rep(alog[j], heads) for j in range(2)]),
            "we_dtb_" + sfx: np.stack([rep(dtb[j], heads) for j in range(2)]),
        }
    shared = packs("p", list(range(8)))
    shared["we_qan"] = np.ascontiguousarray(f(inp["mla_q_a_norm"]).reshape(2, 3, 128).transpose(0, 2, 1))
    shared["we_kvan"] = np.ascontiguousarray(f(inp["mla_kv_a_norm"]).reshape(2, 2, 128).transpose(0, 2, 1))

    def npk(g):
        out = np.zeros((2, 96, 2), np.float32)
        out[:, :, 0] = g
        out[:, 64:, 1] = g[:, 64 + _ROPE_PERM]
        return out
    shared["we_qn"], shared["we_kn"] = npk(qn), npk(kn)
    shared["we_onorm"] = np.ascontiguousarray(f(inp["gdn_out_norm"])[:, :, None])
    cos, sin = _rope_tables()
    shared["rope_cos"], shared["rope_sin"] = cos, sin
    per_r = {r: packs("s", [2 * r, 2 * r + 1]) for r in range(4)}
    cckv, ckr, sg = f(inp["cache_mla_ckv"]), f(inp["cache_mla_krope"]), f(inp["state_gdn"])
    for c in range(NCORES):
        g_, r_ = c // 4, c % 4
        hs = [2 * r_, 2 * r_ + 1]
        m = maps[c]
        m.update(shared)
        m.update(per_r[r_])
        m["we_ckvctx"] = np.ascontiguousarray(cckv[g_].transpose(0, 2, 1).reshape(2, 2, 128, 256).transpose(0, 2, 1, 3))
        kc_ = np.zeros((2, 96, 256), np.float32)
        kc_[:, 64:, :] = ckr[g_].transpose(0, 2, 1)
        m["we_krctx"] = kc_
        S = sg[g_][:, :, hs]
        m["we_S0"] = np.ascontiguousarray(S.transpose(0, 3, 1, 2, 4)[:, :, :, :, None, :])


_NC = None


def kernel(**inputs):
    global _NC
    maps = prep_inputs(inputs)
    prep_odd(inputs, maps)
    prep_even(inputs, maps)
    if _NC is None:
        _NC = build_program()
    res = run_bass_kernel_spmd(_NC, maps, core_ids=list(range(NCORES)))
    if DEBUG:
        kernel.res = res
    yp = np.zeros((16, 256, D), np.float32)
    ys = np.zeros((2, 4096, D), np.float32)
    for c in range(NCORES):
        y = from_fm(res.results[c]["yT"])
        yp[2 * c] = y[0:256]
        yp[2 * c + 1] = y[256:512]
        ys[c // 4, (c % 4) * 1024:(c % 4 + 1) * 1024] = y[512:1536]
    z = lambda *sh: np.zeros(sh, np.float32)
    mC, mn, mm_ = z(16, 2, 2, 8, 64, 128), z(16, 2, 2, 8, 64), z(16, 2, 2, 8)
    for c in range(NCORES):
        oc = res.results[c]["oC_out"]
        om = res.results[c]["om_out"]
        for s_ in range(2):
            a = oc[:, :, :, :, s_, :].transpose(0, 2, 3, 1, 4)
            mC[2 * c + s_] = a[..., :128]
            mn[2 * c + s_] = a[..., 128]
            mm_[2 * c + s_] = om[:, 0, :, :, s_]
    ockv, okr, ogdn = z(16, 2, 256, 256), z(16, 2, 256, 32), z(16, 2, 2, 8, 64, 64)
    for c in range(NCORES):
        ck = res.results[c]["ckv_out"]
        kr = res.results[c]["kr_out"]
        gd = res.results[c]["gdn_out"]
        for s_ in range(2):
            ockv[2 * c + s_] = ck[:, :, :, s_ * 256:(s_ + 1) * 256].transpose(0, 3, 2, 1).reshape(2, 256, 256)
            okr[2 * c + s_] = kr[:, :, s_ * 256:(s_ + 1) * 256].transpose(0, 2, 1)
            ogdn[2 * c + s_] = gd[:, :, :, :, s_, :].transpose(0, 2, 3, 1, 4)
    return (yp, ys, ockv, okr, ogdn, mC, mn, mm_)
```

```python
import numpy as np
from contextlib import ExitStack
import concourse.bass as bass
import concourse.mybir as mybir
from concourse.bass_utils import run_bass_kernel_spmd

F32 = mybir.dt.float32
BF16 = mybir.dt.bfloat16
AF = mybir.ActivationFunctionType
ALU = mybir.AluOpType
AX = mybir.AxisListType

NCORES = 8
D = 1024
DEPTH = 4
DFF = 2816
NJ = 22
NTOK = 1536
NT = 3
EPS = 1e-6
MIXERS = True
NLAYERS = DEPTH
DEBUG = False
ONLY_ADA = False
FFN = True
LAYERS = None
NU_GDN = 2


class R:
    __slots__ = ("w", "rd")

    def __init__(self):
        self.w = None
        self.rd = {}


class K:
    NDMA = 24

    def __init__(self, nc, stack):
        self.nc = nc
        self.b = {"pe": nc.tensor, "act": nc.scalar, "dve": nc.vector, "pool": nc.gpsimd, "sp": nc.sync}
        self.sem, self.cnt, self.seen = {}, {}, {}
        for e in self.b:
            self.sem[e] = stack.enter_context(nc.semaphore("s_" + e))
            self.cnt[e] = 0
            self.seen[e] = {}
        self.dsem = [stack.enter_context(nc.semaphore("d%d" % i)) for i in range(self.NDMA)]
        self.dval = [0] * self.NDMA
        self.dnext = 0
        self.ccsem = stack.enter_context(nc.semaphore("cc"))
        self.ccval = 0
        self.stack = stack
        self.n_ins = 0
        self.uid = 0

    def sb(self, shape, dt=F32, st=None, name=None):
        self.uid += 1
        return (st or self.stack).enter_context(self.nc.sbuf_tensor("%s%d" % (name or "t", self.uid), list(shape), dt))

    def ps(self, shape, dt=F32, st=None, name=None):
        self.uid += 1
        return (st or self.stack).enter_context(self.nc.psum_tensor("%s%d" % (name or "p", self.uid), list(shape), dt))

    def _semobj(self, key):
        if isinstance(key, str):
            return self.ccsem if key == "cc" else self.sem[key]
        return self.dsem[key]

    def _wait(self, eng, deps):
        seen = self.seen[eng]
        for key, val in deps.items():
            if eng == "pe" and key == "pe":
                continue
            if seen.get(key, 0) < val:
                self.b[eng].wait_ge(self._semobj(key), val)
                seen[key] = val

    @staticmethod
    def _deps(reads, writes):
        deps = {}

        def add(kv):
            if kv is not None and deps.get(kv[0], 0) < kv[1]:
                deps[kv[0]] = kv[1]
        for r in reads:
            add(r.w)
        for w in writes:
            add(w.w)
            for kv in w.rd.items():
                add(kv)
        return deps

    def I(self, eng, reads, writes, emit, inc=True):
        self._wait(eng, self._deps(reads, writes))
        ins = emit(self.b[eng])
        self.n_ins += 1
        val = self.cnt[eng] + 1
        if inc:
            ins.then_inc(self.sem[eng], 1)
            self.cnt[eng] = val
        for w in writes:
            w.w = (eng, val)
            w.rd = {}
        for r in reads:
            if r.rd.get(eng, 0) < val:
                r.rd[eng] = val
        return ins

    def dma(self, q, out, in_, reads, writes, **kw):
        deps = self._deps(reads, writes)
        s = self.dnext
        self.dnext = (s + 1) % self.NDMA
        if self.dval[s] > 0:
            deps[s] = max(deps.get(s, 0), self.dval[s])
        self._wait(q, deps)
        ins = self.b[q].dma_start(out=out, in_=in_, **kw)
        self.dval[s] += 16
        ins.then_inc(self.dsem[s], 16)
        self.n_ins += 1
        for w in writes:
            w.w = (s, self.dval[s])
            w.rd = {}
        for r in reads:
            r.rd[s] = self.dval[s]
        return ins

    def cc(self, kind, op, groups, in_ap, out_ap, reads, writes):
        deps = self._deps(reads, writes)
        if self.ccval > 0:
            deps["cc"] = self.ccval
        self._wait("pool", deps)
        ins = self.b["pool"].collective_compute(kind, op, replica_groups=groups, ins=[in_ap], outs=[out_ap])
        self.ccval += 1
        ins.then_inc(self.ccsem, 1)
        for w in writes:
            w.w = ("cc", self.ccval)
            w.rd = {}
        for r in reads:
            r.rd["cc"] = self.ccval
        return ins

    def barrier(self):
        deps = {e: c for e, c in self.cnt.items() if c > 0}
        for s, v in enumerate(self.dval):
            if v > 0:
                deps[s] = v
        if self.ccval > 0:
            deps["cc"] = self.ccval
        for e in self.b:
            self._wait(e, dict(deps))

    def finish(self, regions):
        deps = {}
        for r in regions:
            if r.w is not None:
                deps[r.w[0]] = max(deps.get(r.w[0], 0), r.w[1])
        self._wait("sp", deps)


class Rot:
    def __init__(self, k, n, shape, dt, st=None, psum=False):
        self.t = [(k.ps if psum else k.sb)(shape, dt, st=st) for _ in range(n)]
        self.r = [R() for _ in range(n)]
        self.i = 0

    def next(self):
        i = self.i
        self.i = (i + 1) % len(self.t)
        return self.t[i], self.r[i]


def build_program():
    nc = bass.Bass("TRN2", target_bir_lowering=False)

    def din(name, shape):
        return nc.dram_tensor(name, list(shape), F32, kind="ExternalInput").ap()

    def dout(name, shape):
        return nc.dram_tensor(name, list(shape), F32, kind="ExternalOutput").ap()

    xT_d = din("xT", [128, 8, NTOK])
    cond_d = din("condT", [128, 8, 2])
    wada_d = din("w_ada", [DEPTH, 128, 8, 6 * D])
    bada_d = din("b_adaT", [128, DEPTH, 48])
    nm_d = din("nmT", [128, DEPTH, 8])
    nf_d = din("nfT", [128, DEPTH, 8])
    wfi_d = din("w_ffn_in", [DEPTH, 128, NJ, 8, 256])
    wfo_d = din("w_ffn_out", [DEPTH, 128, NJ, D])
    yT_d = dout("yT", [128, 8, NTOK])
    OD = {
        "p": dict(H=8, NTk=512, nseq=2, nchs=4,
                  w_in=din("wo_in_p", [2, 128, 8, 388 * 8]), w_out=din("wo_out_p", [2, 128, 8, D]),
                  gb=din("ogb_p", [2, 64, 32]), onorm=din("onorm_p", [2, 128, 8])),
        "s": dict(H=2, NTk=4096, nseq=1, nchs=64,
                  w_in=din("wo_in_s", [2, 128, 8, 388 * 2]), w_out=din("wo_out_s", [2, 128, 2, D]),
                  gb=din("ogb_s", [2, 64, 8]), onorm=din("onorm_s", [2, 128, 2]),
                  C0=din("oC0_s", [2, 64, 2, 2, 1, 129]), m0=din("om0_s", [2, 1, 2, 2, 1])),
    }
    oC_out = dout("oC_out", [2, 64, 2, 8, 2, 129])
    om_out = dout("om_out", [2, 1, 2, 8, 2])

    def ev_decl(sfx, H, extra):
        WE = 832 + 260 * H
        d_ = dict(H=H, w_in=din("we_in_" + sfx, [2, 128, 8, WE]), w_uq=din("we_uq_" + sfx, [2, 128, 3, H, 192]),
                  w_uk=din("we_uk_" + sfx, [2, 128, 2, H, 64]), w_uv=din("we_uv_" + sfx, [2, 128, 2, H * 64]),
                  w_out=din("we_out_" + sfx, [2, 64, 2, H, D]), conv=din("we_conv_" + sfx, [2, 64, 3, H, 3]),
                  alog=din("we_alog_" + sfx, [2, 64, 2 * H]), dtb=din("we_dtb_" + sfx, [2, 64, 2 * H]))
        d_.update(extra)
        return d_
    EVN = dict(qan=din("we_qan", [2, 128, 3]), kvan=din("we_kvan", [2, 128, 2]), qn=din("we_qn", [2, 96, 2]),
               kn=din("we_kn", [2, 96, 2]), onorm=din("we_onorm", [2, 64, 1]))
    EV = {
        "p": ev_decl("p", 8, dict(NTk=512, nseq=2, nchs=4)),
        "s": ev_decl("s", 2, dict(NTk=4096, nseq=1, nchs=64, ckv_ctx=din("we_ckvctx", [2, 128, 2, 256]),
                                  kr_ctx=din("we_krctx", [2, 96, 256]), S0=din("we_S0", [2, 64, 2, 2, 1, 64]),
                                  cos=din("rope_cos", [96, 4096]), sin=din("rope_sin", [96, 4096]))),
    }
    ckv_out = dout("ckv_out", [2, 128, 2, 512])
    kr_out = dout("kr_out", [2, 32, 512])
    gdn_out = dout("gdn_out", [2, 64, 2, 8, 2, 64])
    xpre_p = nc.dram_tensor("xpre_p", [24, 64, 2, 258], F32)
    xpre_s = nc.dram_tensor("xpre_s", [6, 64, 1, 4098], F32)
    ag_in = nc.dram_tensor("ag_in", [4, 256, 512], F32)
    ag_out = nc.dram_tensor("ag_out", [4, 1024, 512], F32)
    rs_in = nc.dram_tensor("rs_in", [4096, 1024], F32)
    rs_out = nc.dram_tensor("rs_out", [1024, 1024], F32)
    GROUPS = [[0, 1, 2, 3], [4, 5, 6, 7]]
    dbg = {}

    def dump(k, name, ap, shape, reads):
        if not DEBUG:
            return
        d = dout("dbg_" + name, shape)
        r = R()
        k.dma("sp", d, ap, reads, [r])
        dbg[name] = r

    with ExitStack() as st:
        k = K(nc, st)
        xT = k.sb([128, 8, NTOK], F32, name="xT")
        rx = [R() for _ in range(NT)]
        for t in range(NT):
            k.dma("sp", xT[:, :, t * 512:(t + 1) * 512], xT_d[:, :, t * 512:(t + 1) * 512], [], [rx[t]])
        ones_bf = k.sb([128, 128], BF16, name="ones")
        r_const = R()
        k.I("pool", [], [r_const], lambda e: e.memset(ones_bf[:], 1.0))
        modA = k.sb([128, DEPTH, 2, 8, 2], F32, name="modA")
        mod = k.sb([128, DEPTH, 48, 2], F32, name="mod")
        r_mod = R()
        nm = k.sb([128, DEPTH, 8], F32, name="nm")
        nf = k.sb([128, DEPTH, 8], F32, name="nf")
        bada = k.sb([128, DEPTH, 48], F32, name="bada")
        r_small = R()
        k.dma("sp", nm[:], nm_d, [], [r_small])
        k.dma("sp", nf[:], nf_d, [], [r_small])
        k.dma("sp", bada[:], bada_d, [], [r_small])

        with ExitStack() as ph:
            cond = k.sb([128, 8, 2], F32, st=ph)
            r_c = R()
            k.dma("sp", cond[:], cond_d, [], [r_c])
            k.I("act", [r_c], [r_c], lambda e: e.activation(out=cond[:], in_=cond[:], func=AF.Silu))
            dump(k, "cond", cond[:], [128, 8, 2], [r_c])
            wst = Rot(k, 2, [128, 8, 512], F32, st=ph)
            pmod = Rot(k, 2, [128, 48, 2], F32, st=ph, psum=True)
            for l in range(DEPTH):
                pm, rpm = pmod.next()
                for blk in range(12):
                    wt, rw = wst.next()
                    k.dma("pool" if blk % 2 else "sp", wt[:], wada_d[l, :, :, blk * 512:(blk + 1) * 512], [], [rw])
                    for mm in range(4):
                        m = blk * 4 + mm
                        for kc in range(8):
                            k.I("pe", [rw, r_c], [rpm], lambda e: e.matmul(
                                pm[:, m, :], lhsT=wt[:, kc, mm * 128:(mm + 1) * 128], rhs=cond[:, kc, :],
                                start=(kc == 0), stop=(kc == 7)), inc=(kc == 7 and mm == 3))
                k.I("dve", [rpm, r_small], [r_mod], lambda e: e.tensor_tensor(
                    out=mod[:, l], in0=pm[:], in1=bada[:, l, :].unsqueeze(2).to_broadcast([128, 48, 2]), op=ALU.add))
                for which, (nw, part) in enumerate(((nm, 1), (nf, 4))):
                    k.I("dve", [r_mod, r_small], [r_mod], lambda e: e.scalar_tensor_tensor(
                        out=modA[:, l, which], in0=mod[:, l, part * 8:(part + 1) * 8, :], scalar=1.0,
                        in1=nw[:, l, :].unsqueeze(2).to_broadcast([128, 8, 2]), op0=ALU.add, op1=ALU.mult))
            dump(k, "mod", mod[:], [128, DEPTH, 48, 2], [r_mod])
            dump(k, "modA", modA[:], [128, DEPTH, 2, 8, 2], [r_mod])
            k.barrier()

        def mvec(l, part, c, cond_i):
            return mod[:, l, part * 8 + c, cond_i:cond_i + 1]

        def norm_mod(ph, l, which, t, out_fn, rh, ps_ss, rps):
            cond_i = 0 if t == 0 else 1
            part_b = 0 if which == 0 else 3
            sl = slice(t * 512, (t + 1) * 512)
            sq, rsq = ph["sq"].next()
            k.I("act", [rx[t]], [rsq], lambda e: e.activation(out=sq[:], in_=xT[:, :, sl], func=AF.Square))
            for c in range(8):
                k.I("pe", [rsq, r_const], [rps], lambda e: e.matmul(
                    ps_ss[:], lhsT=ones_bf[:], rhs=sq[:, c, :], start=(c == 0), stop=(c == 7)), inc=(c == 7))
            rs, rrs = ph["rs"].next()
            k.I("act", [rps], [rrs], lambda e: e.activation(out=rs[:], in_=ps_ss[:], func=AF.Sqrt, bias=eps_t[:], scale=1.0 / D))
            k.I("dve", [rrs], [rrs], lambda e: e.reciprocal(out=rs[:], in_=rs[:]))
            for c in range(8):
                tmp, rtmp = ph["tmp"].next()
                k.I("dve", [rx[t], rrs], [rtmp], lambda e: e.tensor_tensor(out=tmp[:], in0=xT[:, c, sl], in1=rs[:], op=ALU.mult))
                k.I("act", [rtmp, r_mod], [rh], lambda e: e.activation(
                    out=out_fn(c), in_=tmp[:], func=AF.Identity,
                    bias=mvec(l, part_b, c, cond_i), scale=modA[:, l, which, c, cond_i:cond_i + 1]))

        eps_t = k.sb([128, 1], F32, name="eps")
        k.I("pool", [], [r_const], lambda e: e.memset(eps_t[:], EPS))

        def ffn(l):
            with ExitStack() as ph_st:
                ph = {"sq": Rot(k, 1, [128, 8, 512], BF16, st=ph_st),
                      "rs": Rot(k, 2, [128, 512], F32, st=ph_st),
                      "tmp": Rot(k, 2, [128, 512], F32, st=ph_st)}
                hT = k.sb([128, 8, NTOK], BF16, st=ph_st)
                rh = [R() for _ in range(NT)]
                ps_ss = k.ps([128, 512], F32, st=ph_st)
                rps = R()
                for t in range(NT):
                    norm_mod(ph, l, 1, t, (lambda c, t=t: hT[:, c, t * 512:(t + 1) * 512]), rh[t], ps_ss, rps)
                if l == 0 and DEBUG:
                    hf = k.sb([128, 8, 512], F32, st=ph_st)
                    rhf = R()
                    k.I("dve", rh, [rhf], lambda e: e.tensor_copy(out=hf[:], in_=hT[:, :, 0:512]))
                    dump(k, "h", hf[:], [128, 8, 512], [rhf])
                aT = k.sb([128, 11, NTOK], BF16, st=ph_st)
                ra = [R() for _ in range(NT)]
                wst = Rot(k, 2, [128, 8, 256], F32, st=ph_st)
                wbf = Rot(k, 2, [128, 8, 256], BF16, st=ph_st)
                wost = Rot(k, 2, [128, 11, 128], F32, st=ph_st)
                wobf = Rot(k, 2, [128, 11, 128], BF16, st=ph_st)
                psg = Rot(k, 2, [128, 512], F32, st=ph_st, psum=True)
                psu = Rot(k, 2, [128, 512], F32, st=ph_st, psum=True)
                pso = Rot(k, 2, [128, 512], F32, st=ph_st, psum=True)
                sgs = Rot(k, 2, [128, 512], BF16, st=ph_st)
                for half in range(2):
                    for jj in range(11):
                        j = half * 11 + jj
                        ws, rws = wst.next()
                        k.dma("sp", ws[:], wfi_d[l, :, j], [], [rws])
                        wb, rwb = wbf.next()
                        k.I("pool", [rws], [rwb], lambda e: e.tensor_copy(out=wb[:], in_=ws[:]))
                        for t in range(NT):
                            sl = slice(t * 512, (t + 1) * 512)
                            pg, rpg = psg.next()
                            pu, rpu = psu.next()
                            for c in range(8):
                                k.I("pe", [rwb, rh[t]], [rpg], lambda e: e.matmul(
                                    pg[:], lhsT=wb[:, c, 0:128], rhs=hT[:, c, sl], start=(c == 0), stop=(c == 7)), inc=(c == 7))
                            for c in range(8):
                                k.I("pe", [rwb, rh[t]], [rpu], lambda e: e.matmul(
                                    pu[:], lhsT=wb[:, c, 128:256], rhs=hT[:, c, sl], start=(c == 0), stop=(c == 7)), inc=(c == 7))
                            sg, rsg = sgs.next()
                            k.I("act", [rpg], [rsg], lambda e: e.activation(out=sg[:], in_=pg[:], func=AF.Silu))
                            k.I("dve", [rsg, rpu], [ra[t]], lambda e: e.tensor_tensor(out=aT[:, jj, sl], in0=sg[:], in1=pu[:], op=ALU.mult))
                    for c in range(8):
                        ws, rws = wost.next()
                        k.dma("sp", ws[:], wfo_d[l, :, half * 11:(half + 1) * 11, c * 128:(c + 1) * 128], [], [rws])
                        wb, rwb = wobf.next()
                        k.I("pool", [rws], [rwb], lambda e: e.tensor_copy(out=wb[:], in_=ws[:]))
                        for t in range(NT):
                            sl = slice(t * 512, (t + 1) * 512)
                            cond_i = 0 if t == 0 else 1
                            po, rpo = pso.next()
                            for jj in range(11):
                                k.I("pe", [rwb, ra[t]], [rpo], lambda e: e.matmul(
                                    po[:], lhsT=wb[:, jj, :], rhs=aT[:, jj, sl], start=(jj == 0), stop=(jj == 10)), inc=(jj == 10))
                            k.I("dve", [rpo, r_mod, rx[t]], [rx[t]], lambda e: e.scalar_tensor_tensor(
                                out=xT[:, c, sl], in0=po[:], scalar=mvec(l, 5, c, cond_i), in1=xT[:, c, sl],
                                op0=ALU.mult, op1=ALU.add))
                k.barrier()


        ident = k.sb([128, 128], F32, name="ident")
        maskU = k.sb([64, 64], F32, name="maskU")
        maskL = k.sb([64, 64], F32, name="maskL")
        ones_f = k.sb([64, 64], F32, name="ones_f")
        k.I("pool", [], [r_const], lambda e: e.memset(ident[:], 0.0))
        k.I("pool", [r_const], [r_const], lambda e: e.affine_select(
            out=ident[:], in_=ident[:], pattern=[[-1, 128]], compare_op=ALU.not_equal, fill=1.0, base=0, channel_multiplier=1))
        k.I("pool", [], [r_const], lambda e: e.memset(ones_f[:], 1.0))
        k.I("pool", [], [r_const], lambda e: e.memset(maskU[:], 1.0))
        k.I("pool", [r_const], [r_const], lambda e: e.affine_select(
            out=maskU[:], in_=maskU[:], pattern=[[1, 64]], compare_op=ALU.is_ge, fill=0.0, base=0, channel_multiplier=-1))
        k.I("pool", [], [r_const], lambda e: e.memset(maskL[:], 1.0))
        k.I("pool", [r_const], [r_const], lambda e: e.affine_select(
            out=maskL[:], in_=maskL[:], pattern=[[-1, 64]], compare_op=ALU.is_ge, fill=0.0, base=0, channel_multiplier=1))
        dump(k, "maskU", maskU[:], [64, 64], [r_const])

        def mm(rd, wr, out, lhsT, rhs, start=True, stop=True, inc=True):
            return k.I("pe", rd, wr, lambda e: e.matmul(out, lhsT=lhsT, rhs=rhs, start=start, stop=stop), inc=inc)

        def act(rd, wr, out, in_, func, **kw):
            return k.I("act", rd, wr, lambda e: e.activation(out=out, in_=in_, func=func, **kw))

        def tt(rd, wr, out, in0, in1, op, eng="dve"):
            return k.I(eng, rd, wr, lambda e: e.tensor_tensor(out=out, in0=in0, in1=in1, op=op))

        def stt(rd, wr, out, in0, scalar, in1, op0, op1):
            return k.I("dve", rd, wr, lambda e: e.scalar_tensor_tensor(out=out, in0=in0, scalar=scalar, in1=in1, op0=op0, op1=op1))

        def tsm(rd, wr, out, in0, scalar, eng="dve"):
            return k.I(eng, rd, wr, lambda e: e.tensor_scalar_mul(out=out, in0=in0, scalar1=scalar))

        def cp(rd, wr, out, in_, eng="dve"):
            if eng == "act":
                return k.I("act", rd, wr, lambda e: e.copy(out=out, in_=in_))
            return k.I(eng, rd, wr, lambda e: e.tensor_copy(out=out, in_=in_))

        class WStream:
            def __init__(self, st_, shape, n=2, nstage=1):
                self.ws = Rot(k, nstage, shape, F32, st=st_)
                self.wb = Rot(k, n, shape, BF16, st=st_)

            def load(self, dram_ap, sub=None):
                ws, r1 = self.ws.next()
                wb, r2 = self.wb.next()
                wsv, wbv = (ws[:], wb[:]) if sub is None else (sub(ws), sub(wb))
                k.dma("sp", wsv, dram_ap, [], [r1])
                k.I("pool", [r1], [r2], lambda e: e.tensor_copy(out=wbv, in_=wsv))
                return wb, r2

        def load_w(st_, dram_ap, shape, q="sp"):
            ws = k.sb(shape, F32, st=st_)
            r1 = R()
            k.dma(q, ws[:], dram_ap, [], [r1])
            wb = k.sb(shape, BF16, st=st_)
            r2 = R()
            k.I("pool", [r1], [r2], lambda e: e.tensor_copy(out=wb[:], in_=ws[:]))
            return wb, r2

        def odd_job(l, key, make_get_h, st_mem, make_sink):
            jb = OD[key]
            j = l // 2
            H, NTk, nseq, nchs = jb["H"], jb["NTk"], jb["nseq"], jb["nchs"]
            nch = nseq * nchs
            U = 2 * H
            ntile = NTk // 512
            NC_ = 2 * nch * H
            Q0, K0, V0, O0, G0 = 0, 64 * H, 128 * H, 256 * H, 384 * H
            WTOT = 388 * H
            memT = k.sb([128, H, NTk], F32, st=st_mem)
            r_mem = R()
            k.I("pool", [], [r_mem], lambda e: e.memset(memT[:], 0.0))
            Cst = k.sb([64, 2, H, nseq, 129], F32, st=st_mem)
            r_C = R()
            m0row = k.sb([1, 2, H, nseq], F32, st=st_mem)
            mfin = k.sb([1, 2, H, nseq], F32, st=st_mem)
            r_m0 = R()
            if "C0" in jb:
                k.dma("sp", Cst[:], jb["C0"][j], [], [r_C])
                k.dma("sp", m0row[:], jb["m0"][j], [], [r_m0])
            else:
                k.I("pool", [], [r_C], lambda e: e.memset(Cst[:], 0.0))
                k.I("pool", [], [r_m0], lambda e: e.memset(m0row[:], 0.0))
            st_main = ExitStack()
            qT = k.sb([64, H, NTk], BF16, st=st_main)
            kT = k.sb([64, H, NTk], BF16, st=st_main)
            r_q, r_k = R(), R()
            Ktm = k.sb([64, nch, H * 64], BF16, st=st_main)
            Vtm = k.sb([64, nch, H, 129], BF16, st=st_main)
            Gtm = k.sb([64, nch, 4 * H], F32, st=st_main)
            r_Ktm, r_V, r_G = R(), R(), R()
            k.I("pool", [], [r_V], lambda e: e.memset(Vtm[:, :, :, 128:129], 1.0))
            with ExitStack() as st_proj:
                get_h = make_get_h(st_proj)
                pfm = Rot(k, 2, [64, 512], F32, st=st_proj, psum=True)
                ptm = Rot(k, 2, [64, 256], F32, st=st_proj, psum=True)
                wstr = WStream(st_proj, [128, 8, 256])
                for b0 in range(0, WTOT, 256):
                    b1 = min(b0 + 256, WTOT)
                    if b0 >= O0 and b1 <= G0:
                        continue
                    if True:
                        wb, rwb = wstr.load(jb["w_in"][j, :, :, b0:b1], sub=lambda t_: t_[:, :, 0:b1 - b0])
                        for tt_ in range(ntile):
                            hts, rhts = get_h(tt_)
                            tsl = slice(tt_ * 512, (tt_ + 1) * 512)
                            for h in range(H):
                                for (base, dst, rdst, isk) in ((Q0, qT, r_q, False), (K0, kT, r_k, True)):
                                    c0 = base + 64 * h
                                    if not (b0 <= c0 < b1):
                                        continue
                                    pp, rpp = pfm.next()
                                    for kc in range(8):
                                        mm([rwb, rhts], [rpp], pp[:], wb[:, kc, c0 - b0:c0 - b0 + 64], hts[:, kc, :],
                                           start=(kc == 0), stop=(kc == 7), inc=(kc == 7))
                                    if isk:
                                        k.I("act", [rpp], [rdst], lambda e: e.mul(out=dst[:, h, tsl], in_=pp[:], mul=0.125))
                                    else:
                                        cp([rpp], [rdst], dst[:, h, tsl], pp[:], eng="act")
                            for (base, width) in ((K0, 64 * H), (V0, 128 * H), (G0, 4 * H)):
                                lo, hi = max(base, b0), min(base + width, b1)
                                if lo >= hi:
                                    continue
                                for cc in range(8):
                                    c = tt_ * 8 + cc
                                    pp, rpp = ptm.next()
                                    for kc in range(8):
                                        mm([rwb, rhts], [rpp], pp[:, 0:hi - lo], hts[:, kc, cc * 64:(cc + 1) * 64], wb[:, kc, lo - b0:hi - b0],
                                           start=(kc == 0), stop=(kc == 7), inc=(kc == 7))
                                    if base == K0:
                                        tsm([rpp], [r_Ktm], Ktm[:, c, lo - K0:hi - K0], pp[:, 0:hi - lo], 0.125)
                                    elif base == V0:
                                        h0, h1 = (lo - V0) // 128, (hi - V0) // 128
                                        cp([rpp], [r_V], Vtm[:, c, h0:h1, 0:128],
                                           pp[:, 0:hi - lo].rearrange("p (h d) -> p h d", d=128), eng="act")
                                    else:
                                        cp([rpp], [r_G], Gtm[:, c, lo - G0:hi - G0], pp[:, 0:hi - lo])
                k.barrier()
            G = k.sb([64, 4, nch, H], F32, st=st_main)
            gb = k.sb([64, 4 * H], F32, st=st_main)
            r_g = R()
            k.dma("sp", gb[:], jb["gb"][j], [], [r_g])
            tt([r_G, r_g], [r_g], G[:], Gtm[:].rearrange("p c (t h) -> p t c h", t=4),
               gb[:].rearrange("p (t h) -> p t h", t=4).unsqueeze(2).to_broadcast([64, 4, nch, H]), ALU.add)
            Lg = k.sb([64, 2, nch, H], F32, st=st_main)
            act([r_g], [r_g], Lg[:], G[:, 2:4], AF.Exp, scale=-1.0)
            act([r_g], [r_g], Lg[:], Lg[:], AF.Ln, bias=1.0)
            Cc = k.sb([64, 2, nch, H], F32, st=st_main)
            CU = k.sb([64, 2, nch, H], F32, st=st_main)
            rows = k.sb([1, 7, 2, nch, H], F32, st=st_main)
            rowp = k.sb([1, 8, 2, H, nseq, nchs], F32, st=st_main)
            BC = k.sb([64, 5, 2, nch, H], F32, st=st_main)
            wsr = k.sb([64, 2, nch, H], F32, st=st_main)
            ws0 = k.sb([64, 2, nch, H], F32, st=st_main)
            flo = k.sb([64, 2, nch, H], F32, st=st_main)
            Mcol = k.sb([128, 1], F32, st=st_main)
            r_row, r_bc, r_tok = R(), R(), R()
            with ExitStack() as st_g:
                pg = k.ps([64, 2, nch, H], F32, st=st_g)
                r_pg = R()
                nfl = nch * H
                mm([r_g, r_const], [r_pg], pg[:, 0].rearrange("p c h -> p (c h)"), maskU[:], Lg[:, 0].rearrange("p c h -> p (c h)"))
                mm([r_g, r_const], [r_pg], pg[:, 1].rearrange("p c h -> p (c h)"), maskL[:], Lg[:, 1].rearrange("p c h -> p (c h)"))
                tt([r_pg, r_g], [r_tok], Cc[:], G[:, 0:2], pg[:], ALU.add)
                cp([r_pg], [r_tok], CU[:], pg[:])
                ptr = k.ps([128, 64], F32, st=st_g)
                r_ptr = R()
                prow = k.ps([1, 512], F32, st=st_g)
                r_prow = R()
                Cflat = Cc[:].rearrange("p d c h -> p (d c h)")
                for g0 in range(0, NC_, 128):
                    k.I("pe", [r_tok, r_const], [r_ptr], lambda e: e.transpose(ptr[:], Cflat[:, g0:g0 + 128], ident[0:64, 0:64]))
                    k.I("dve", [r_ptr], [r_tok], lambda e: e.reduce_max(out=Mcol[:], in_=ptr[:], axis=AX.X))
                    mm([r_tok, r_const], [r_prow], prow[:, g0:g0 + 128], Mcol[:], ident[:])
                cp([r_prow], [r_row], rows[:, 0].rearrange("o d c h -> o (d c h)"), prow[:, 0:NC_])
                mm([r_g, r_const], [r_prow], prow[:, 0:NC_], ones_f[:, 0:1], Lg[:].rearrange("p d c h -> p (d c h)"))
                k.I("act", [r_prow], [r_row], lambda e: e.mul(out=rows[:, 1].rearrange("o d c h -> o (d c h)"), in_=prow[:, 0:NC_], mul=-1.0))

                def to_proc(dst_q, src_q):
                    for d in range(2):
                        src = rows[:, src_q, d].rearrange("o (s c) h -> o h s c", s=nseq)
                        if d == 1:
                            src = src[:, :, :, ::-1]
                        cp([r_row], [r_row], rowp[:, dst_q, d], src)

                def to_nat(dst_q, src_q):
                    for d in range(2):
                        dst = rows[:, dst_q, d].rearrange("o (s c) h -> o h s c", s=nseq)
                        src = rowp[:, src_q, d]
                        if d == 1:
                            src = src[:, :, :, ::-1]
                        cp([r_row], [r_row], dst, src)
                Mp, Bp, Gp, D0, D1, MS, MB, AA = range(8)
                MX = D1
                to_proc(Mp, 0)
                to_proc(Bp, 1)
                rp = lambda q: rowp[:, q]
                tt([r_row], [r_row], rp(Gp), rp(Bp), rp(Mp), ALU.add)
                cp([r_row], [r_row], rp(D0), rp(Bp))
                k.I("pool", [r_row], [r_row], lambda e: e.memset(rowp[:, D0, :, :, :, 0:1], -1e30))
                cp([r_row], [r_row], rp(D1), rp(Gp))
                tt([r_row, r_m0], [r_row], rowp[:, MS, :, :, :, 0], rowp[:, Bp, :, :, :, 0], m0row[:], ALU.add)
                tt([r_row], [r_row], rowp[:, D1, :, :, :, 0], rowp[:, MS, :, :, :, 0], rowp[:, Gp, :, :, :, 0], ALU.max)
                fl = lambda q: rowp[:, q].rearrange("o d h s c -> o (d h s c)")
                k.I("dve", [r_row], [r_row], lambda e: e.tensor_tensor_scan(
                    out=fl(MS), data0=fl(D0), data1=fl(D1), initial=0.0, op0=ALU.add, op1=ALU.max))
                cp([r_row, r_m0], [r_row], rowp[:, MB, :, :, :, 0], m0row[:])
                if nchs > 1:
                    cp([r_row], [r_row], rowp[:, MB, :, :, :, 1:nchs], rowp[:, MS, :, :, :, 0:nchs - 1])
                tt([r_row], [r_row], rp(MX), rp(MB), rp(Mp), ALU.max)
                tt([r_row], [r_row], rp(AA), rp(MB), rp(MX), ALU.subtract)
                act([r_row], [r_row], rp(AA), rp(AA), AF.Exp)
                to_nat(2, MX)
                to_nat(3, AA)
                DEC = AA
                tt([r_row], [r_row], rp(DEC), rp(Bp), rp(MB), ALU.add)
                tt([r_row], [r_row], rp(DEC), rp(DEC), rp(MS), ALU.subtract)
                act([r_row], [r_row], rp(DEC), rp(DEC), AF.Exp)
                tt([r_row], [r_row], rp(D0), rp(Gp), rp(MS), ALU.subtract)
                act([r_row], [r_row], rp(D0), rp(D0), AF.Exp)
                to_nat(4, DEC)
                to_nat(5, D0)
                cp([r_row], [r_row], rows[:, 6], rows[:, 0])
                src = rows[:, 2:7].rearrange("o q d c h -> o (q d c h)")
                dstf = BC[:].rearrange("p q d c h -> p (q d c h)")
                pb = Rot(k, 2, [64, 512], F32, st=st_g, psum=True)
                for n0 in range(0, 5 * NC_, 512):
                    n1 = min(n0 + 512, 5 * NC_)
                    pp, rpp = pb.next()
                    mm([r_row, r_const], [rpp], pp[:, 0:n1 - n0], ones_f[0:1, :], src[:, n0:n1])
                    cp([rpp], [r_bc], dstf[:, n0:n1], pp[:, 0:n1 - n0])
                tt([r_tok, r_bc], [r_tok], wsr[:], Cc[:], BC[:, 0], ALU.subtract)
                act([r_tok], [r_tok], wsr[:], wsr[:], AF.Exp)
                tt([r_tok, r_bc], [r_tok], ws0[:], Cc[:], BC[:, 4], ALU.subtract)
                act([r_tok], [r_tok], ws0[:], ws0[:], AF.Exp)
                tt([r_tok, r_bc], [r_tok], flo[:], CU[:], BC[:, 0], ALU.subtract)
                act([r_tok], [r_tok], flo[:], flo[:], AF.Exp)
                if key == "p":
                    cp([r_row], [r_row], mfin[:], rowp[:, MS, :, :, :, nchs - 1])
                    k.dma("sp", om_out[j], mfin[:], [r_row], [r_fin])
                k.barrier()
            with ExitStack() as st_l:
                pq = Rot(k, 2, [64, 64], F32, st=st_l, psum=True)
                px = Rot(k, 2, [64, 129], F32, st=st_l, psum=True)
                pu = Rot(k, 2, [64, 129], F32, st=st_l, psum=True)
                pt = Rot(k, 2, [128, 64], F32, st=st_l, psum=True)
                PTs = Rot(k, 2, [64, 64], BF16, st=st_l)
                qss = Rot(k, 2, [64, 64], F32, st=st_l)
                kwss = Rot(k, 2, [64, 64], BF16, st=st_l)
                dns = Rot(k, 2, [64, 1], F32, st=st_l)
                hts_ = Rot(k, 2, [64, 128], F32, st=st_l)
                tmps = Rot(k, 2, [64, 129], F32, st=st_l)
                for cc in range(nchs):
                    for s_ in range(nseq):
                        for d in range(2):
                            c = s_ * nchs + (cc if d == 0 else nchs - 1 - cc)
                            tok = slice(c * 64, (c + 1) * 64)
                            msk = maskU if d == 0 else maskL
                            for h in range(H):
                                col = lambda t_: t_[:, d, c, h:h + 1]
                                p1, rp1 = pq.next()
                                mm([r_k, r_q], [rp1], p1[:], kT[:, h, tok], qT[:, h, tok])
                                PT, rPT = PTs.next()
                                stt([rp1, r_tok, r_const], [rPT], PT[:], p1[:], col(wsr), msk[:], ALU.mult, ALU.mult)
                                qs, rqs = qss.next()
                                k.I("act", [r_q, r_bc], [rqs], lambda e: e.mul(out=qs[:], in_=qT[:, h, tok], mul=BC[:, 1, d, c, h:h + 1]))
                                p2, rp2 = px.next()
                                mm([rPT, r_V], [rp2], p2[:], PT[:], Vtm[:, c, h, :], start=True, stop=False, inc=False)
                                mm([rqs, r_C], [rp2], p2[:], qs[:], Cst[:, d, h, s_, :], start=False, stop=True)
                                kws, rkws = kwss.next()
                                tsm([r_Ktm, r_tok], [rkws], kws[:], Ktm[:, c, h * 64:(h + 1) * 64], col(ws0), eng="pool")
                                p3, rp3 = pu.next()
                                mm([rkws, r_V], [rp3], p3[:], kws[:], Vtm[:, c, h, :])
                                dn, rdn = dns.next()
                                act([rp2], [rdn], dn[:], p2[:, 128:129], AF.Abs)
                                tt([rdn, r_tok], [rdn], dn[:], dn[:], col(flo), ALU.max)
                                k.I("dve", [rdn], [rdn], lambda e: e.reciprocal(out=dn[:], in_=dn[:]))
                                hh, rhh = hts_.next()
                                tsm([rp2, rdn], [rhh], hh[:], p2[:, 0:128], dn[:, 0:1])
                                p4, rp4 = pt.next()
                                k.I("pe", [rhh, r_const], [rp4], lambda e: e.transpose(p4[:], hh[:], ident[0:64, 0:64]))
                                tt([rp4, r_mem], [r_mem], memT[:, h, tok], memT[:, h, tok], p4[:], ALU.add)
                                tmp, rtmp = tmps.next()
                                k.I("act", [rp3, r_bc], [rtmp], lambda e: e.mul(out=tmp[:], in_=p3[:], mul=BC[:, 3, d, c, h:h + 1]))
                                stt([r_C, r_bc, rtmp], [r_C], Cst[:, d, h, s_, :], Cst[:, d, h, s_, :], BC[:, 2, d, c, h:h + 1], tmp[:], ALU.mult, ALU.add)
                if key == "p":
                    k.dma("sp", oC_out[j], Cst[:], [r_C], [r_fin])
                k.barrier()
            st_main.close()
            with ExitStack() as st_f:
                get_h = make_get_h(st_f)
                out_sink = make_sink(st_f)
                onw = k.sb([128, H], F32, st=st_f)
                r_on = R()
                k.dma("sp", onw[:], jb["onorm"][j], [], [r_on])
                wo_in, r_woin = wres(st_f, jb["w_in"][j, :, :, O0:O0 + 128 * H], [128, 8, 256]) if H == 2 else (None, None)
                wo_str = WStream(st_f, [128, 8, 128]) if H != 2 else None
                wout_str = WStream(st_f, [128, H, 128])
                ps1 = Rot(k, 2, [128, 512], F32, st=st_f, psum=True)
                ps2 = Rot(k, 2, [128, 512], F32, st=st_f, psum=True)
                ps3 = Rot(k, 2, [128, 512], F32, st=st_f, psum=True)
                sqs = Rot(k, 2, [128, 512], BF16, st=st_f)
                rss = Rot(k, 2, [128, 512], F32, st=st_f)
                sgs = Rot(k, 2, [128, 512], F32, st=st_f)
                mixT = k.sb([128, H, 512], BF16, st=st_f)
                r_mix = R()
                for tt_ in range(ntile):
                    hts, rhts = get_h(tt_)
                    tsl = slice(tt_ * 512, (tt_ + 1) * 512)
                    for h in range(H):
                        if H == 2:
                            wo, rwo, c0 = wo_in, r_woin, h * 128
                            stw = None
                        else:
                            stw = None
                            wo, rwo = wo_str.load(jb["w_in"][j, :, :, O0 + h * 128:O0 + (h + 1) * 128])
                            c0 = 0
                        sq, rsq = sqs.next()
                        act([r_mem], [rsq], sq[:], memT[:, h, tsl], AF.Square)
                        p1, rp1 = ps1.next()
                        mm([rsq, r_const], [rp1], p1[:], ones_bf[:], sq[:])
                        rs, rrs = rss.next()
                        act([rp1], [rrs], rs[:], p1[:], AF.Sqrt, bias=eps_t[:], scale=1.0 / 128)
                        k.I("dve", [rrs], [rrs], lambda e: e.reciprocal(out=rs[:], in_=rs[:]))
                        p2, rp2 = ps2.next()
                        for kc in range(8):
                            mm([rwo, rhts], [rp2], p2[:], wo[:, kc, c0:c0 + 128], hts[:, kc, :], start=(kc == 0), stop=(kc == 7), inc=(kc == 7))
                        sg, rsg = sgs.next()
                        act([rp2], [rsg], sg[:], p2[:], AF.Sigmoid)
                        tt([r_mem, rrs], [rrs], rs[:], memT[:, h, tsl], rs[:], ALU.mult)
                        stt([rrs, r_on, rsg], [r_mix], mixT[:, h, :], rs[:], onw[:, h:h + 1], sg[:], ALU.mult, ALU.mult)
                        if stw is not None:
                            k.barrier()
                            stw.close()
                    for c in range(8):
                        if True:
                            wout, rwout = wout_str.load(jb["w_out"][j, :, :, c * 128:(c + 1) * 128])
                            p3, rp3 = ps3.next()
                            for h in range(H):
                                mm([rwout, r_mix], [rp3], p3[:], wout[:, h, :], mixT[:, h, :], start=(h == 0), stop=(h == H - 1), inc=(h == H - 1))
                            out_sink(tt_, c, p3, rp3)
                k.barrier()


        maskUs = k.sb([64, 64], F32, name="maskUs")
        maskLs = k.sb([64, 64], F32, name="maskLs")
        ident_bf = k.sb([64, 64], BF16, name="ident_bf")
        ones128 = k.sb([128, 64], F32, name="ones128")
        k.I("pool", [r_const], [r_const], lambda e: e.tensor_tensor(out=maskUs[:], in0=maskU[:], in1=ident[0:64, 0:64], op=ALU.subtract))
        k.I("pool", [r_const], [r_const], lambda e: e.tensor_tensor(out=maskLs[:], in0=maskL[:], in1=ident[0:64, 0:64], op=ALU.subtract))
        k.I("pool", [r_const], [r_const], lambda e: e.tensor_copy(out=ident_bf[:], in_=ident[0:64, 0:64]))
        k.I("pool", [], [r_const], lambda e: e.memset(ones128[:], 1.0))
        zpad = k.sb([64, 24 * 2], F32, name="zpad")
        k.I("pool", [], [r_const], lambda e: e.memset(zpad[:], 0.0))
        r_xpre = R()
        for (xp_, n_, ns_, L_) in ((xpre_p, 24, 2, 256), (xpre_s, 6, 1, 4096)):
            for col in (0, L_ + 1):
                for i_ in range(n_):
                    k.dma("sp", xp_.ap()[i_, :, :, col:col + 1], zpad[:, 0:ns_].unsqueeze(2), [r_const], [r_xpre],
                          allow_slow_non_contiguous=True)

        def wres(st_, dram_ap, shape):
            wb = k.sb(shape, BF16, st=st_)
            r2 = R()
            with ExitStack() as tmp:
                ws = k.sb(shape, F32, st=tmp)
                r1 = R()
                k.dma("sp", ws[:], dram_ap, [], [r1])
                k.I("pool", [r1], [r2], lambda e: e.tensor_copy(out=wb[:], in_=ws[:]))
                k.barrier()
            return wb, r2

        def rstd_of(rd, wr, out, ps, n, rows=128):
            act(rd, wr, out, ps, AF.Sqrt, bias=eps_t[0:rows, :], scale=1.0 / n)
            k.I("dve", wr, wr, lambda e: e.reciprocal(out=out, in_=out))

        def even_job(l, key, make_get_h, st_mem, make_sink):
            jb = EV[key]
            j = l // 2
            H, NTk, nseq, nchs = jb["H"], jb["NTk"], jb["nseq"], jb["nchs"]
            nch = nseq * nchs
            ntile = NTk // 512
            samp = key == "s"
            NCTX = 256 if samp else 0
            nkt = (NTk + NCTX) // 128
            Ls = nchs * 64
            CQ, CKV, KR, KRR, GQ = 0, 384, 640, 736, 832
            GK, GV, GZ, GA = GQ + 64 * H, GQ + 128 * H, GQ + 192 * H, GQ + 256 * H
            WE = GA + 4 * H
            xpre = (xpre_s if samp else xpre_p).ap()
            SC = 96 ** -0.5
            mixA = k.sb([64, H, NTk], BF16, st=st_mem)
            Sst = k.sb([64, 2, H, nseq, 64], F32, st=st_mem)
            r_mixA, r_oT, r_S, r_par = R(), R(), R(), R()
            if samp:
                k.dma("sp", Sst[:], jb["S0"][j], [], [r_S])
            else:
                k.I("pool", [], [r_S], lambda e: e.memset(Sst[:], 0.0))
            qan = k.sb([128, 3], F32, st=st_mem)
            kvan = k.sb([128, 2], F32, st=st_mem)
            qn = k.sb([96, 2], F32, st=st_mem)
            kn = k.sb([96, 2], F32, st=st_mem)
            onorm = k.sb([64, 1], F32, st=st_mem)
            convw = k.sb([64, 3, H, 3], F32, st=st_mem)
            alog = k.sb([64, 2 * H], F32, st=st_mem)
            dtb = k.sb([64, 2 * H], F32, st=st_mem)
            for t_, d_ in ((qan, EVN["qan"]), (kvan, EVN["kvan"]), (qn, EVN["qn"]), (kn, EVN["kn"]), (onorm, EVN["onorm"]),
                           (convw, jb["conv"]), (alog, jb["alog"]), (dtb, jb["dtb"])):
                k.dma("sp", t_[:], d_[j], [], [r_par])
            act([r_par], [r_par], alog[:], alog[:], AF.Exp)
            k.I("act", [r_par], [r_par], lambda e: e.mul(out=alog[:], in_=alog[:], mul=-1.0))

            with ExitStack() as st_a:
                get_h = make_get_h(st_a)
                qT = k.sb([96, H, NTk], BF16, st=st_a)
                kT = k.sb([96, H, NTk + NCTX], BF16, st=st_a)
                Vt = k.sb([128, nkt, H, 65], BF16, st=st_a)
                r_qT, r_kT, r_Vt = R(), R(), R()
                k.I("pool", [], [r_Vt], lambda e: e.memset(Vt[:, :, :, 64:65], 1.0))
                wsh, r_wsh = wres(st_a, jb["w_in"][j, :, :, 0:832], [128, 8, 832])
                uq, r_uq = wres(st_a, jb["w_uq"][j], [128, 3, H, 192])
                uk, r_uk = wres(st_a, jb["w_uk"][j], [128, 2, H, 64])
                uv, r_uv = wres(st_a, jb["w_uv"][j], [128, 2, H * 64])
                with ExitStack() as st_t:
                    pA = Rot(k, 2, [128, 512], F32, st=st_t, psum=True)
                    pSm = Rot(k, 2, [128, 512], F32, st=st_t, psum=True)
                    pVv = Rot(k, 2, [128, 512], F32, st=st_t, psum=True)
                    cqf = k.sb([128, 3, 512], F32, st=st_t)
                    cqn = k.sb([128, 3, 512], BF16, st=st_t)
                    ckf = k.sb([128, 2, 512], F32, st=st_t)
                    ckb = k.sb([128, 2, 512], BF16, st=st_t)
                    krf = k.sb([96, 512], F32, st=st_t)
                    krrf = k.sb([96, 512], F32, st=st_t)
                    kpre = k.sb([96, 512], F32, st=st_t)
                    sqb = k.sb([128, 3, 512], BF16, st=st_t)
                    rsb = k.sb([128, 512], F32, st=st_t)
                    tmpA = Rot(k, 2, [128, 512], F32, st=st_t)
                    tmpB = Rot(k, 2, [128, 512], F32, st=st_t)
                    cosb = k.sb([96, 512], F32, st=st_t)
                    sinb = k.sb([96, 512], F32, st=st_t)
                    r_cq, r_ck, r_kr, r_kp, r_sq, r_rs, r_cs = R(), R(), R(), R(), R(), R(), R()

                    def kv_side(n, ksl, kt0, rope):
                        cp([r_kr], [r_kp], kpre[64:96, 0:n], krf[64:96, 0:n], eng="act")
                        for h in range(H):
                            pp, rpp = pA.next()
                            for kc in range(2):
                                mm([r_uk, r_ck], [rpp], pp[0:64, 0:n], uk[:, kc, h, :], ckb[:, kc, 0:n], start=(kc == 0), stop=(kc == 1), inc=(kc == 1))
                            cp([rpp], [r_kp], kpre[0:64, 0:n], pp[0:64, 0:n], eng="act")
                            act([r_kp], [r_sq], sqb[0:96, 0, 0:n], kpre[:, 0:n], AF.Square)
                            ps_, rps_ = pSm.next()
                            mm([r_sq, r_const], [rps_], ps_[0:96, 0:n], ones_bf[0:96, 0:96], sqb[0:96, 0, 0:n])
                            rstd_of([rps_], [r_rs], rsb[0:96, 0:n], ps_[0:96, 0:n], 96, rows=96)
                            t1, rt1 = tmpA.next()
                            tt([r_kp, r_rs], [rt1], t1[0:96, 0:n], kpre[:, 0:n], rsb[0:96, 0:n], ALU.mult)
                            if not rope:
                                k.I("act", [rt1, r_par], [r_kT], lambda e: e.mul(out=kT[:, h, ksl], in_=t1[0:96, 0:n], mul=kn[:, 0:1]))
                            else:
                                k.I("act", [rt1, r_par], [r_kT], lambda e: e.mul(out=kT[0:64, h, ksl], in_=t1[0:64, 0:n], mul=kn[0:64, 0:1]))
                                k.I("act", [rt1, r_par], [rt1], lambda e: e.mul(out=t1[64:96, 0:n], in_=t1[64:96, 0:n], mul=kn[64:96, 0:1]))
                                tt([rt1, r_cs], [rt1], t1[64:96, 0:n], t1[64:96, 0:n], cosb[64:96, 0:n], ALU.mult)
                                t2, rt2 = tmpB.next()
                                tt([r_kr, r_rs], [rt2], t2[64:96, 0:n], krrf[64:96, 0:n], rsb[64:96, 0:n], ALU.mult)
                                k.I("act", [rt2, r_par], [rt2], lambda e: e.mul(out=t2[64:96, 0:n], in_=t2[64:96, 0:n], mul=kn[64:96, 1:2]))
                                tt([rt2, r_cs], [rt2], t2[64:96, 0:n], t2[64:96, 0:n], sinb[64:96, 0:n], ALU.mult)
                                tt([rt1, rt2], [r_kT], kT[64:96, h, ksl], t1[64:96, 0:n], t2[64:96, 0:n], ALU.add)
                        for q_ in range(n // 128):
                            pp, rpp = pVv.next()
                            for kc in range(2):
                                mm([r_uv, r_ck], [rpp], pp[:, 0:H * 64], ckb[:, kc, q_ * 128:(q_ + 1) * 128], uv[:, kc, :], start=(kc == 0), stop=(kc == 1), inc=(kc == 1))
                            cp([rpp], [r_Vt], Vt[:, kt0 + q_, :, 0:64], pp[:, 0:H * 64].rearrange("p (h d) -> p h d", d=64), eng="act")

                    if samp:
                        k.dma("sp", ckf[:, :, 0:256], jb["ckv_ctx"][j], [], [r_ck])
                        cp([r_ck], [r_ck], ckb[:, :, 0:256], ckf[:, :, 0:256])
                        k.dma("sp", krf[:, 0:256], jb["kr_ctx"][j], [], [r_kr])
                        kv_side(256, slice(0, 256), 0, False)
                    for tt_ in range(ntile):
                        hts, rhts = get_h(tt_)
                        tsl = slice(tt_ * 512, (tt_ + 1) * 512)
                        ksl = slice(NCTX + tt_ * 512, NCTX + (tt_ + 1) * 512)
                        if samp:
                            k.dma("sp", cosb[64:96, :], jb["cos"][64:96, tsl], [], [r_cs])
                            k.dma("sp", sinb[64:96, :], jb["sin"][64:96, tsl], [], [r_cs])
                        for c in range(3):
                            pp, rpp = pA.next()
                            for kc in range(8):
                                mm([r_wsh, rhts], [rpp], pp[:], wsh[:, kc, CQ + c * 128:CQ + (c + 1) * 128], hts[:, kc, :], start=(kc == 0), stop=(kc == 7), inc=(kc == 7))
                            cp([rpp], [r_cq], cqf[:, c, :], pp[:], eng="act")
                        act([r_cq], [r_sq], sqb[:], cqf[:], AF.Square)
                        ps_, rps_ = pSm.next()
                        for c in range(3):
                            mm([r_sq, r_const], [rps_], ps_[:], ones_bf[:], sqb[:, c, :], start=(c == 0), stop=(c == 2), inc=(c == 2))
                        rstd_of([rps_], [r_rs], rsb[:], ps_[:], 384)
                        for c in range(3):
                            t1, rt1 = tmpA.next()
                            tt([r_cq, r_rs], [rt1], t1[:], cqf[:, c, :], rsb[:], ALU.mult)
                            k.I("act", [rt1, r_par], [r_cq], lambda e: e.mul(out=cqn[:, c, :], in_=t1[:], mul=qan[:, c:c + 1]))
                        for c in range(2):
                            pp, rpp = pA.next()
                            for kc in range(8):
                                mm([r_wsh, rhts], [rpp], pp[:], wsh[:, kc, CKV + c * 128:CKV + (c + 1) * 128], hts[:, kc, :], start=(kc == 0), stop=(kc == 7), inc=(kc == 7))
                            cp([rpp], [r_ck], ckf[:, c, :], pp[:], eng="act")
                        act([r_ck], [r_sq], sqb[:, 0:2, :], ckf[:], AF.Square)
                        ps_, rps_ = pSm.next()
                        for c in range(2):
                            mm([r_sq, r_const], [rps_], ps_[:], ones_bf[:], sqb[:, c, :], start=(c == 0), stop=(c == 1), inc=(c == 1))
                        rstd_of([rps_], [r_rs], rsb[:], ps_[:], 256)
                        for c in range(2):
                            tt([r_ck, r_rs], [r_ck], ckf[:, c, :], ckf[:, c, :], rsb[:], ALU.mult)
                            k.I("act", [r_ck, r_par], [r_ck], lambda e: e.mul(out=ckf[:, c, :], in_=ckf[:, c, :], mul=kvan[:, c:c + 1]))
                        cp([r_ck], [r_ck], ckb[:], ckf[:])
                        if not samp:
                            k.dma("sp", ckv_out[j], ckf[:], [r_ck], [r_fin])
                        for (c0, dst) in ((KR, krf), (KRR, krrf)):
                            if dst is krrf and not samp:
                                continue
                            pp, rpp = pA.next()
                            for kc in range(8):
                                mm([r_wsh, rhts], [rpp], pp[0:96, :], wsh[:, kc, c0:c0 + 96], hts[:, kc, :], start=(kc == 0), stop=(kc == 7), inc=(kc == 7))
                            cp([rpp], [r_kr], dst[64:96, :], pp[64:96, :], eng="act")
                        if not samp:
                            k.dma("sp", kr_out[j], krf[64:96, :], [r_kr], [r_fin])
                        for h in range(H):
                            pp, rpp = pA.next()
                            for kc in range(3):
                                mm([r_uq, r_cq], [rpp], pp[0:96, :], uq[:, kc, h, 0:96], cqn[:, kc, :], start=(kc == 0), stop=(kc == 2), inc=(kc == 2))
                            t0, rt0 = tmpB.next()
                            cp([rpp], [rt0], t0[0:96, :], pp[0:96, :], eng="act")
                            act([rt0], [r_sq], sqb[0:96, 0, :], t0[0:96, :], AF.Square)
                            ps_, rps_ = pSm.next()
                            mm([r_sq, r_const], [rps_], ps_[0:96, :], ones_bf[0:96, 0:96], sqb[0:96, 0, :])
                            rstd_of([rps_], [r_rs], rsb[0:96, :], ps_[0:96, :], 96, rows=96)
                            t1, rt1 = tmpA.next()
                            tt([rt0, r_rs], [rt1], t1[0:96, :], t0[0:96, :], rsb[0:96, :], ALU.mult)
                            if not samp:
                                k.I("act", [rt1, r_par], [r_qT], lambda e: e.mul(out=qT[:, h, tsl], in_=t1[0:96, :], mul=qn[:, 0:1]))
                            else:
                                k.I("act", [rt1, r_par], [r_qT], lambda e: e.mul(out=qT[0:64, h, tsl], in_=t1[0:64, :], mul=qn[0:64, 0:1]))
                                k.I("act", [rt1, r_par], [rt1], lambda e: e.mul(out=t1[64:96, :], in_=t1[64:96, :], mul=qn[64:96, 0:1]))
                                tt([rt1, r_cs], [rt1], t1[64:96, :], t1[64:96, :], cosb[64:96, :], ALU.mult)
                                pp2, rpp2 = pA.next()
                                for kc in range(3):
                                    mm([r_uq, r_cq], [rpp2], pp2[0:96, :], uq[:, kc, h, 96:192], cqn[:, kc, :], start=(kc == 0), stop=(kc == 2), inc=(kc == 2))
                                t2, rt2 = tmpB.next()
                                tt([rpp2, r_rs], [rt2], t2[64:96, :], pp2[64:96, :], rsb[64:96, :], ALU.mult)
                                k.I("act", [rt2, r_par], [rt2], lambda e: e.mul(out=t2[64:96, :], in_=t2[64:96, :], mul=qn[64:96, 1:2]))
                                tt([rt2, r_cs], [rt2], t2[64:96, :], t2[64:96, :], sinb[64:96, :], ALU.mult)
                                tt([rt1, rt2], [r_qT], qT[64:96, h, tsl], t1[64:96, :], t2[64:96, :], ALU.add)
                        kv_side(512, ksl, NCTX // 128 + tt_ * 4, samp)
                    k.barrier()
                with ExitStack() as st_t:
                    pSc = Rot(k, 3, [128, 512], F32, st=st_t, psum=True)
                    pO = Rot(k, 2, [128, 512], F32, st=st_t, psum=True)
                    pB = Rot(k, 1, [64, 512], F32, st=st_t, psum=True)
                    Pb = Rot(k, 3, [128, 512], BF16, st=st_t)
                    rdb = Rot(k, 2, [128, 512], F32, st=st_t)
                    bcb = Rot(k, 2, [64, 512], F32, st=st_t)
                    if samp:
                        blocks = [(slice(b * 512, (b + 1) * 512), 512, list(range(nkt))) for b in range(ntile)]
                    else:
                        blocks = [(slice(s_ * 256, (s_ + 1) * 256), 256, [2 * s_, 2 * s_ + 1]) for s_ in range(nseq)]
                    for h in range(H):
                        for (qsl, n, kts) in blocks:
                            po, rpo = pO.next()
                            AH = 2
                            sq_ = []

                            def issue_s(kt_):
                                ps__, rps__ = pSc.next()
                                mm([r_kT, r_qT], [rps__], ps__[:, 0:n], kT[:, h, kt_ * 128:(kt_ + 1) * 128], qT[:, h, qsl])
                                sq_.append((ps__, rps__))
                            for kt_ in kts[:AH]:
                                issue_s(kt_)
                            for i_, kt in enumerate(kts):
                                ps_, rps_ = sq_.pop(0)
                                if i_ + AH < len(kts):
                                    issue_s(kts[i_ + AH])
                                P, rP = Pb.next()
                                act([rps_], [rP], P[:, 0:n], ps_[:, 0:n], AF.Exp, scale=SC)
                                mm([r_Vt, rP], [rpo], po[0:65, 0:n], Vt[:, kt, h, :], P[:, 0:n], start=(i_ == 0), stop=(i_ == len(kts) - 1), inc=(i_ == len(kts) - 1))
                            rd, rrd = rdb.next()
                            k.I("dve", [rpo], [rrd], lambda e: e.reciprocal(out=rd[64:65, 0:n], in_=po[64:65, 0:n]))
                            pb_, rpb = pB.next()
                            mm([rrd, r_const], [rpb], pb_[:, 0:n], ones128[64:65, :], rd[64:65, 0:n])
                            bc, rbc = bcb.next()
                            cp([rpb], [rbc], bc[:, 0:n], pb_[:, 0:n], eng="act")
                            tt([rpo, rbc], [r_mixA], mixA[:, h, qsl], po[0:64, 0:n], bc[:, 0:n], ALU.mult)
                    k.barrier()
            oT = k.sb([64, H, NTk], F32, st=st_mem)
            k.I("pool", [], [r_oT], lambda e: e.memset(oT[:], 0.0))
            with ExitStack() as st_b:
                qg = k.sb([64, H, NTk], BF16, st=st_b)
                kg = k.sb([64, H, NTk], BF16, st=st_b)
                Vtm = k.sb([64, nch, H, 64], BF16, st=st_b)
                Gtm = k.sb([64, nch, 4 * H], F32, st=st_b)
                r_qg, r_kg, r_Ktm, r_Vtm, r_G = R(), R(), R(), R(), R()
                with ExitStack() as st_p:
                    get_h = make_get_h(st_p)
                    pfm = Rot(k, 2, [64, 512], F32, st=st_p, psum=True)
                    ptm = Rot(k, 2, [64, 64], F32, st=st_p, psum=True)
                    xsb = Rot(k, 2, [64, 512], F32, st=st_p)
                    wstr = WStream(st_p, [128, 8, 256])
                    for b0 in range(GQ, WE, 256):
                        b1 = min(b0 + 256, WE)
                        if b0 >= GZ and b1 <= GA:
                            continue
                        if True:
                            wb, rwb = wstr.load(jb["w_in"][j, :, :, b0:b1], sub=lambda t_: t_[:, :, 0:b1 - b0])
                            for tt_ in range(ntile):
                                hts, rhts = get_h(tt_)
                                for T_ in range(3):
                                    for h in range(H):
                                        c0 = GQ + (T_ * H + h) * 64
                                        if not (b0 <= c0 < b1):
                                            continue
                                        pp, rpp = pfm.next()
                                        for kc in range(8):
                                            mm([rwb, rhts], [rpp], pp[:], wb[:, kc, c0 - b0:c0 - b0 + 64], hts[:, kc, :], start=(kc == 0), stop=(kc == 7), inc=(kc == 7))
                                        xs, rxs = xsb.next()
                                        cp([rpp], [rxs], xs[:], pp[:], eng="act")
                                        if samp:
                                            k.dma("sp", xpre[T_ * H + h][:, 0, 1 + tt_ * 512:1 + (tt_ + 1) * 512], xs[:], [rxs], [r_xpre])
                                        else:
                                            k.dma("sp", xpre[T_ * H + h][:, :, 1:257], xs[:].rearrange("p (s t) -> p s t", s=2), [rxs], [r_xpre])
                                lo, hi = max(GA, b0), min(GA + 4 * H, b1)
                                if lo < hi:
                                    for cc in range(8):
                                        c = tt_ * 8 + cc
                                        pp, rpp = ptm.next()
                                        for kc in range(8):
                                            mm([rwb, rhts], [rpp], pp[:, 0:hi - lo], hts[:, kc, cc * 64:(cc + 1) * 64], wb[:, kc, lo - b0:hi - b0], start=(kc == 0), stop=(kc == 7), inc=(kc == 7))
                                        cp([rpp], [r_G], Gtm[:, c, lo - GA:hi - GA], pp[:, 0:hi - lo])
                    k.barrier()
                with ExitStack() as st_c:
                    xin = k.sb([64, nseq, Ls + 2], F32, st=st_c)
                    yb = k.sb([64, nseq, Ls], F32, st=st_c)
                    r_xin, r_y = R(), R()
                    pss = Rot(k, 2, [64, 512], F32, st=st_c, psum=True)
                    ptr = Rot(k, 2, [64, 8, 64], F32, st=st_c, psum=True)
                    sqc = Rot(k, 2, [64, 512], BF16, st=st_c)
                    rsc = Rot(k, 2, [64, 512], F32, st=st_c)
                    ynf = Rot(k, 2, [64, 512], F32, st=st_c)
                    yfl = yb[:].rearrange("p s t -> p (s t)")
                    for T_ in range(3):
                        for h in range(H):
                            k.dma("sp", xin[:], xpre[T_ * H + h], [r_xpre], [r_xin])
                            k.I("dve", [r_xin, r_par], [r_y], lambda e: e.tensor_scalar_mul(out=yb[:], in0=xin[:, :, 1:Ls + 1], scalar1=convw[:, T_, h, 1:2]))
                            stt([r_xin, r_par, r_y], [r_y], yb[:], xin[:, :, 0:Ls], convw[:, T_, h, 0:1], yb[:], ALU.mult, ALU.add)
                            stt([r_xin, r_par, r_y], [r_y], yb[:], xin[:, :, 2:Ls + 2], convw[:, T_, h, 2:3], yb[:], ALU.mult, ALU.add)
                            act([r_y], [r_y], yb[:], yb[:], AF.Silu)
                            for tt_ in range(ntile):
                                tsl = slice(tt_ * 512, (tt_ + 1) * 512)
                                if T_ < 2:
                                    sq, rsq = sqc.next()
                                    act([r_y], [rsq], sq[:], yfl[:, tsl], AF.Square)
                                    ps_, rps_ = pss.next()
                                    mm([rsq, r_const], [rps_], ps_[:], ones_bf[0:64, 0:64], sq[:])
                                    rs, rrs = rsc.next()
                                    rstd_of([rps_], [rrs], rs[:], ps_[:], 1.0, rows=64)
                                    yn, ryn = ynf.next()
                                    tt([r_y, rrs], [ryn], yn[:], yfl[:, tsl], rs[:], ALU.mult)
                                    if T_ == 0:
                                        k.I("act", [ryn], [r_qg], lambda e: e.mul(out=qg[:, h, tsl], in_=yn[:], mul=0.125))
                                        continue
                                    cp([ryn], [r_kg], kg[:, h, tsl], yn[:], eng="act")
                                    continue
                                else:
                                    src_t, rsrc = None, r_y
                                    dst_t, rdst = Vtm, r_Vtm
                                pt_, rpt = ptr.next()
                                for cc in range(8):
                                    src = (src_t[:, cc * 64:(cc + 1) * 64] if src_t is not None else yfl[:, tt_ * 512 + cc * 64:tt_ * 512 + (cc + 1) * 64])
                                    k.I("pe", [rsrc, r_const], [rpt], lambda e: e.transpose(pt_[:, cc, :], src, ident[0:64, 0:64]), inc=(cc == 7))
                                cp([rpt], [rdst], dst_t[:, tt_ * 8:(tt_ + 1) * 8, h, :], pt_[:])
                    k.barrier()
                Gg = k.sb([64, 4, nch, H], F32, st=st_b)
                GCs = k.sb([64, 2, nch, H], F32, st=st_b)
                BET = k.sb([64, 2, nch, H], F32, st=st_b)
                KBG = k.sb([64, 2, nch, H], F32, st=st_b)
                KDE = k.sb([64, 2, nch, H], F32, st=st_b)
                GLb = k.sb([64, 2, nch, H], F32, st=st_b)
                r_gt = R()
                with ExitStack() as st_g:
                    pg = k.ps([64, 2, nch, H], F32, st=st_g)
                    pt2 = k.ps([64, 2, nch, H], F32, st=st_g)
                    r_pg, r_pt2 = R(), R()
                    cp([r_G], [r_gt], Gg[:], Gtm[:].rearrange("p c (t h) -> p t c h", t=4))
                    tt([r_gt, r_par], [r_gt], Gg[:, 0:2], Gg[:, 0:2], dtb[:].rearrange("p (d h) -> p d h", d=2).unsqueeze(2).to_broadcast([64, 2, nch, H]), ALU.add)
                    act([r_gt], [r_gt], Gg[:, 0:2], Gg[:, 0:2], AF.Exp)
                    act([r_gt], [r_gt], Gg[:, 0:2], Gg[:, 0:2], AF.Ln, bias=1.0)
                    tt([r_gt, r_par], [r_gt], Gg[:, 0:2], Gg[:, 0:2], alog[:].rearrange("p (d h) -> p d h", d=2).unsqueeze(2).to_broadcast([64, 2, nch, H]), ALU.mult)
                    act([r_gt], [r_gt], BET[:], Gg[:, 2:4], AF.Sigmoid)
                    fl3 = lambda t_: t_.rearrange("p c h -> p (c h)")
                    mm([r_gt, r_const], [r_pg], fl3(pg[:, 0]), maskU[:], fl3(Gg[:, 0]))
                    mm([r_gt, r_const], [r_pg], fl3(pg[:, 1]), maskL[:], fl3(Gg[:, 1]))
                    cp([r_pg], [r_gt], GCs[:], pg[:])
                    mm([r_gt, r_const], [r_pt2], pt2[:].rearrange("p d c h -> p (d c h)"), ones_f[:], Gg[:, 0:2].rearrange("p d c h -> p (d c h)"))
                    act([r_pt2], [r_gt], GLb[:], pt2[:], AF.Exp)
                    tt([r_pt2, r_gt], [r_gt], KDE[:], pt2[:], GCs[:], ALU.subtract)
                    act([r_gt], [r_gt], KDE[:], KDE[:], AF.Exp)
                    act([r_gt], [r_gt], KBG[:], GCs[:], AF.Exp)
                    tt([r_gt], [r_gt], KBG[:], KBG[:], BET[:], ALU.mult)
                    k.barrier()
                with ExitStack() as st_l:
                    NU = NU_GDN
                    pR = Rot(k, 2, [64, 8, 64], F32, st=st_l, psum=True)
                    pM = Rot(k, 6, [64, 8, 64], F32, st=st_l, psum=True)
                    B3 = lambda: k.sb([64, 8, 64], F32, st=st_l)
                    bufsets = []
                    for _u in range(NU):
                        bufsets.append(dict(b=[B3() for _ in range(9)], kbT=k.sb([64, 8, 64], BF16, st=st_l),
                                            vns=Rot(k, 2, [64, 64], F32, st=st_l), r=R()))
                    nsp = nch // 8
                    fl = lambda t_: t_[:].rearrange("p c i -> p (c i)")
                    bcI = ident[0:64, 0:64].unsqueeze(1).to_broadcast([64, 8, 64])
                    r_Sdh = [[R() for _ in range(H)] for _ in range(2)]
                    r_oTh = [R() for _ in range(H)]
                    for d_ in range(2):
                        for h_ in range(H):
                            r_Sdh[d_][h_].w = r_S.w
                    for h_ in range(H):
                        r_oTh[h_].w = r_oT.w

                    def unit_gen(d, h, bs):
                        A0, A1, A2, A3, qd, NTb, attnT, Nb, Xb = bs["b"]
                        Dg, Ma, vb = A0, A0, A0
                        DmTs, Mb, kbg = A1, A1, A1
                        EgR, MTa, kd = A2, A2, A2
                        DmT, MTb = A3, A3
                        WTb, Ub = NTb, Nb
                        kbT, vns, r_sp = bs["kbT"], bs["vns"], bs["r"]
                        r_Su, r_oTu = r_Sdh[d][h], r_oTh[h]
                        mI, mS = (maskU, maskUs) if d == 0 else (maskL, maskLs)
                        for si in range(nsp):
                            sp_ = si if d == 0 else nsp - 1 - si
                            c0 = sp_ * 8
                            tsl = slice(c0 * 64, c0 * 64 + 512)
                            colb = lambda t_: t_[:, d, c0:c0 + 8, h:h + 1].to_broadcast([64, 8, 64])
                            tt([r_gt, r_const], [r_sp], Dg[:], bcI, colb(GCs), ALU.mult, eng="pool")
                            yield
                            p1, rp1 = pR.next()
                            mm([r_sp, r_const], [rp1], fl(p1), ones_f[:], fl(Dg))
                            yield
                            tt([rp1, r_gt], [r_sp], DmT[:], p1[:], colb(GCs), ALU.subtract)
                            k.I("dve", [r_sp], [r_sp], lambda e: e.tensor_scalar_min(out=DmT[:], in0=DmT[:], scalar1=0.0))
                            yield
                            act([r_sp], [r_sp], DmT[:], DmT[:], AF.Exp)
                            act([rp1], [r_sp], EgR[:], p1[:], AF.Exp)
                            yield
                            tt([r_sp, r_const], [r_sp], DmTs[:], DmT[:], mS[:].unsqueeze(1).to_broadcast([64, 8, 64]), ALU.mult, eng="pool")
                            tt([r_sp, r_const], [r_sp], DmT[:], DmT[:], mI[:].unsqueeze(1).to_broadcast([64, 8, 64]), ALU.mult, eng="pool")
                            tt([r_qg, r_sp], [r_sp], fl(qd), qg[:, h, tsl], fl(EgR), ALU.mult)
                            yield
                            tt([r_gt, r_const], [r_sp], Dg[:], bcI, colb(BET), ALU.mult, eng="pool")
                            yield
                            p2, rp2 = pR.next()
                            mm([r_sp, r_const], [rp2], fl(p2), ones_f[:], fl(Dg))
                            yield
                            tt([r_kg, rp2], [r_sp], fl(kbT), kg[:, h, tsl], fl(p2), ALU.mult)
                            yield
                            pa, rpa = pM.next()
                            pq_, rpq = pM.next()
                            for cc in range(8):
                                csl = slice(c0 * 64 + cc * 64, c0 * 64 + (cc + 1) * 64)
                                mm([r_kg, r_sp], [rpa], pa[:, cc, :], kg[:, h, csl], kbT[:, cc, :], inc=(cc == 7))
                            for cc in range(8):
                                csl = slice(c0 * 64 + cc * 64, c0 * 64 + (cc + 1) * 64)
                                mm([r_kg, r_qg], [rpq], pq_[:, cc, :], kg[:, h, csl], qg[:, h, csl], inc=(cc == 7))
                            yield
                            stt([rpa, r_sp], [r_sp], NTb[:], pa[:], -1.0, DmTs[:], ALU.mult, ALU.mult)
                            tt([rpq, r_sp], [r_sp], attnT[:], pq_[:], DmT[:], ALU.mult)
                            yield
                            pn, rpn = pM.next()
                            for cc in range(8):
                                k.I("pe", [r_sp, r_const], [rpn], lambda e: e.transpose(pn[:, cc, :], NTb[:, cc, :], ident[0:64, 0:64]), inc=(cc == 7))
                            yield
                            cp([rpn], [r_sp], Nb[:], pn[:], eng="act")
                            tt([r_sp, r_const], [r_sp], Xb[:], NTb[:], bcI, ALU.add, eng="pool")
                            yield
                            M_, MT_ = Nb, NTb
                            bufs = [(Ma, MTa), (Mb, MTb)]
                            for rnd in range(5):
                                Mn, MTn = bufs[rnd % 2]
                                pm_, rpm_ = pM.next()
                                for cc in range(8):
                                    mm([r_sp], [rpm_], pm_[:, cc, :], MT_[:, cc, :], M_[:, cc, :], inc=(cc == 7))
                                if rnd < 4:
                                    pmt, rpmt = pM.next()
                                    for cc in range(8):
                                        mm([r_sp], [rpmt], pmt[:, cc, :], M_[:, cc, :], MT_[:, cc, :], inc=(cc == 7))
                                yield
                                cp([rpm_], [r_sp], Mn[:], pm_[:], eng="act")
                                if rnd < 4:
                                    cp([rpmt], [r_sp], MTn[:], pmt[:], eng="dve")
                                yield
                                px_, rpx = pM.next()
                                for cc in range(8):
                                    mm([r_sp], [rpx], px_[:, cc, :], Mn[:, cc, :], Xb[:, cc, :], inc=(cc == 7))
                                yield
                                tt([rpx, r_sp], [r_sp], Xb[:], Xb[:], px_[:], ALU.add)
                                yield
                                M_, MT_ = Mn, MTn
                            pkt_, rpk = pM.next()
                            pk_ = pkt_[:].bitcast(BF16)[:, :, 0:64]
                            for cc in range(8):
                                csl = slice(c0 * 64 + cc * 64, c0 * 64 + (cc + 1) * 64)
                                k.I("pe", [r_kg, r_const], [rpk], lambda e: e.transpose(pk_[:, cc, :], kg[:, h, csl], ident_bf[:]), inc=(cc == 7))
                            yield
                            tt([rpk, r_gt], [r_sp], kbg[:], pk_, colb(KBG), ALU.mult)
                            tt([rpk, r_gt], [r_sp], kd[:], pk_, colb(KDE), ALU.mult)
                            tt([r_Vtm, r_gt], [r_sp], vb[:], Vtm[:, c0:c0 + 8, h, :], colb(BET), ALU.mult, eng="pool")
                            yield
                            pu_, rpu = pM.next()
                            for cc in range(8):
                                mm([r_sp], [rpu], pu_[:, cc, :], Xb[:, cc, :], vb[:, cc, :], inc=(cc == 7))
                            pw_, rpw = pM.next()
                            for cc in range(8):
                                mm([r_sp], [rpw], pw_[:, cc, :], kbg[:, cc, :], Xb[:, cc, :], inc=(cc == 7))
                            yield
                            cp([rpu], [r_sp], Ub[:], pu_[:], eng="act")
                            cp([rpw], [r_sp], WTb[:], pw_[:], eng="dve")
                            yield
                            order = list(range(8)) if d == 0 else list(range(7, -1, -1))
                            for cc in order:
                                c = c0 + cc
                                s_ = c // nchs
                                csl = slice(c * 64, (c + 1) * 64)
                                S_ = Sst[:, d, h, s_, :]
                                if cc == order[0]:
                                    ps3, rps3 = pR.next()
                                mm([r_sp, r_Su], [rps3], ps3[:, 0, :], WTb[:, cc, :], S_)
                                yield
                                vn, rvn = vns.next()
                                tt([r_sp, rps3], [rvn], vn[:], Ub[:, cc, :], ps3[:, 0, :], ALU.subtract)
                                yield
                                mm([r_Su, r_sp], [rps3], ps3[:, 1, :], S_, qd[:, cc, :], start=True, stop=False, inc=False)
                                mm([rvn, r_sp], [rps3], ps3[:, 1, :], vn[:], attnT[:, cc, :], start=False, stop=True)
                                mm([r_sp, rvn], [rps3], ps3[:, 2, :], kd[:, cc, :], vn[:])
                                yield
                                tt([rps3, r_oTu], [r_oTu], oT[:, h, csl], oT[:, h, csl], ps3[:, 1, :], ALU.add)
                                stt([r_Su, r_gt, rps3], [r_Su], S_, S_, GLb[:, d, c, h:h + 1], ps3[:, 2, :], ALU.mult, ALU.add)
                                yield

                    units = [(d, h) for d in range(2) for h in range(H)]
                    for g0 in range(0, len(units), NU):
                        gens = [unit_gen(d, h, bufsets[i_]) for i_, (d, h) in enumerate(units[g0:g0 + NU])]
                        while gens:
                            for g_ in list(gens):
                                try:
                                    next(g_)
                                except StopIteration:
                                    gens.remove(g_)
                    for rr in [x for row in r_Sdh for x in row] + r_oTh + [b_["r"] for b_ in bufsets]:
                        pass
                    k.barrier()
                    r_S.w, r_S.rd = None, {}
                    r_oT.w, r_oT.rd = None, {}
                    if not samp:
                        k.dma("sp", gdn_out[j], Sst[:], [r_S], [r_fin])
                    k.barrier()
            with ExitStack() as st_f:
                get_h = make_get_h(st_f)
                out_sink = make_sink(st_f)
                ps1 = Rot(k, 2, [64, 512], F32, st=st_f, psum=True)
                ps2 = Rot(k, 2, [64, 512], F32, st=st_f, psum=True)
                ps3 = Rot(k, 2, [128, 512], F32, st=st_f, psum=True)
                sqs = Rot(k, 2, [64, 512], BF16, st=st_f)
                rss = Rot(k, 2, [64, 512], F32, st=st_f)
                zs = Rot(k, 2, [64, 512], F32, st=st_f)
                mixD = k.sb([64, H, 512], BF16, st=st_f)
                r_mixD = R()
                wz_res = wres(st_f, jb["w_in"][j, :, :, GZ:GZ + 64 * H], [128, 8, 64 * H]) if H == 2 else None
                wz_str = WStream(st_f, [128, 8, 64]) if H != 2 else None
                wout_str = WStream(st_f, [64, 2, H, 128])
                for tt_ in range(ntile):
                    hts, rhts = get_h(tt_)
                    tsl = slice(tt_ * 512, (tt_ + 1) * 512)
                    for h in range(H):
                        if wz_res is not None:
                            (wz, rwz), c0, stw = wz_res, h * 64, None
                        else:
                            stw = None
                            wz, rwz = wz_str.load(jb["w_in"][j, :, :, GZ + h * 64:GZ + (h + 1) * 64])
                            c0 = 0
                        sq, rsq = sqs.next()
                        act([r_oT], [rsq], sq[:], oT[:, h, tsl], AF.Square)
                        p1, rp1 = ps1.next()
                        mm([rsq, r_const], [rp1], p1[:], ones_bf[0:64, 0:64], sq[:])
                        rs, rrs = rss.next()
                        rstd_of([rp1], [rrs], rs[:], p1[:], 64, rows=64)
                        p2, rp2 = ps2.next()
                        for kc in range(8):
                            mm([rwz, rhts], [rp2], p2[:], wz[:, kc, c0:c0 + 64], hts[:, kc, :], start=(kc == 0), stop=(kc == 7), inc=(kc == 7))
                        z, rz = zs.next()
                        act([rp2], [rz], z[:], p2[:], AF.Silu)
                        tt([r_oT, rrs], [rrs], rs[:], oT[:, h, tsl], rs[:], ALU.mult)
                        stt([rrs, r_par, rz], [r_mixD], mixD[:, h, :], rs[:], onorm[:, 0:1], z[:], ALU.mult, ALU.mult)
                        if stw is not None:
                            k.barrier()
                            stw.close()
                    for c in range(8):
                        if True:
                            wout, rwout = wout_str.load(jb["w_out"][j, :, :, :, c * 128:(c + 1) * 128])
                            p3, rp3 = ps3.next()
                            for h in range(H):
                                mm([rwout, r_mixA], [rp3], p3[:], wout[:, 0, h, :], mixA[:, h, tsl], start=(h == 0), stop=False, inc=False)
                            for h in range(H):
                                mm([rwout, r_mixD], [rp3], p3[:], wout[:, 1, h, :], mixD[:, h, :], start=False, stop=(h == H - 1), inc=(h == H - 1))
                            out_sink(tt_, c, p3, rp3)
                k.barrier()

        def mixer(l):
            j = l // 2
            job_fn = odd_job if l % 2 == 1 else even_job
            r_agin, r_agout, r_rsin, r_rsout = R(), R(), R(), R()
            with ExitStack() as st_h:
                hTp = k.sb([128, 8, 512], BF16, st=st_h)
                rhp = R()
                with ExitStack() as st_n:
                    ph = {"sq": Rot(k, 1, [128, 8, 512], BF16, st=st_n),
                          "rs": Rot(k, 2, [128, 512], F32, st=st_n),
                          "tmp": Rot(k, 2, [128, 512], F32, st=st_n)}
                    ps_ss = k.ps([128, 512], F32, st=st_n)
                    rps = R()
                    hTs = k.sb([128, 8, 1024], BF16, st=st_n)
                    rhs_ = R()
                    for t in (1, 2):
                        norm_mod(ph, l, 0, t, (lambda c, t=t: hTs[:, c, (t - 1) * 512:t * 512]), rhs_, ps_ss, rps)
                    norm_mod(ph, l, 0, 0, (lambda c: hTp[:, c, :]), rhp, ps_ss, rps)
                    for q_ in range(4):
                        k.dma("sp", ag_in.ap()[q_].bitcast(BF16).rearrange("(c p) t -> p c t", p=128),
                              hTs[:, 2 * q_:2 * q_ + 2, :], [rhs_], [r_agin])
                    for q_ in range(4):
                        k.cc("AllGather", ALU.bypass, GROUPS, ag_in.ap()[q_], ag_out.ap()[q_], [r_agin], [r_agout])
                    k.barrier()

                def sink_p(tt_, c, ps, rps_):
                    k.I("dve", [rps_, r_mod, rx[0]], [rx[0]], lambda e: e.scalar_tensor_tensor(
                        out=xT[:, c, 0:512], in0=ps[:], scalar=mvec(l, 2, c, 0), in1=xT[:, c, 0:512], op0=ALU.mult, op1=ALU.add))
                with ExitStack() as st_mem:
                    job_fn(l, "p", lambda st_: (lambda tt_: (hTp, rhp)), st_mem, lambda st_: sink_p)
                    k.barrier()
            with ExitStack() as st_s:
                agv = [ag_out.ap()[q_].bitcast(BF16).rearrange("(r c p) t -> r p c t", r=4, c=2) for q_ in range(4)]

                def make_get_hs(st_):
                    hbuf = Rot(k, 2, [128, 8, 512], BF16, st=st_)

                    def get_hs(tt_):
                        hb, rhb = hbuf.next()
                        for q_ in range(4):
                            k.dma("sp", hb[:, 2 * q_:2 * q_ + 2, :], agv[q_][tt_ // 2][:, :, (tt_ % 2) * 512:(tt_ % 2 + 1) * 512], [r_agout], [rhb])
                        return hb, rhb
                    return get_hs
                rsv = rs_in.ap().rearrange("(r c p) t -> r c p t", r=4, c=8)

                def make_sink_s(st_):
                    osb = Rot(k, 2, [128, 512], F32, st=st_)

                    def sink_s(tt_, c, ps, rps_):
                        ob, rob = osb.next()
                        cp([rps_], [rob], ob[:], ps[:], eng="act")
                        k.dma("sp", rsv[tt_ // 2, c][:, (tt_ % 2) * 512:(tt_ % 2 + 1) * 512], ob[:], [rob], [r_rsin])
                    return sink_s
                with ExitStack() as st_mem:
                    job_fn(l, "s", make_get_hs, st_mem, make_sink_s)
                    k.barrier()
                k.cc("ReduceScatter", ALU.add, GROUPS, rs_in.ap(), rs_out.ap(), [r_rsin], [r_rsout])
                rso = rs_out.ap().rearrange("(c p) t -> p c t", p=128)
                stg = Rot(k, 2, [128, 8, 512], F32, st=st_s)
                for t in (1, 2):
                    sg_, rsg_ = stg.next()
                    k.dma("sp", sg_[:], rso[:, :, (t - 1) * 512:t * 512], [r_rsout], [rsg_])
                    for c in range(8):
                        k.I("dve", [rsg_, r_mod, rx[t]], [rx[t]], lambda e: e.scalar_tensor_tensor(
                            out=xT[:, c, t * 512:(t + 1) * 512], in0=sg_[:, c, :], scalar=mvec(l, 2, c, 1),
                            in1=xT[:, c, t * 512:(t + 1) * 512], op0=ALU.mult, op1=ALU.add))
                k.barrier()

        r_fin = R()
        for l in ([] if ONLY_ADA else (LAYERS if LAYERS is not None else range(NLAYERS))):
            if MIXERS:
                mixer(l)
            if FFN:
                ffn(l)

        r_out = R()
        for t in range(NT):
            k.dma("sp", yT_d[:, :, t * 512:(t + 1) * 512], xT[:, :, t * 512:(t + 1) * 512], [rx[t]], [r_out])
            k.finish([r_out])
        k.finish(list(dbg.values()) + [r_fin])
        print("n_ins", k.n_ins)
    return nc


def to_fm(x):
    n = x.shape[0]
    return np.ascontiguousarray(x.reshape(n, 8, 128).transpose(2, 1, 0))


def from_fm(y):
    n = y.shape[2]
    return np.ascontiguousarray(y.transpose(2, 1, 0).reshape(n, 1024))


def prep_inputs(inp):
    f = lambda a: np.asarray(a, dtype=np.float32)
    xp, xs = f(inp["x_prompt"]), f(inp["x_sample"])
    shared = {}
    shared["w_ada"] = np.ascontiguousarray(f(inp["w_ada"]).reshape(DEPTH, 8, 128, 6 * D).transpose(0, 2, 1, 3))
    shared["b_adaT"] = np.ascontiguousarray(f(inp["b_ada"]).reshape(DEPTH, 48, 128).transpose(2, 0, 1))
    shared["nmT"] = np.ascontiguousarray(f(inp["norm_mix"]).reshape(DEPTH, 8, 128).transpose(2, 0, 1))
    shared["nfT"] = np.ascontiguousarray(f(inp["norm_ffn"]).reshape(DEPTH, 8, 128).transpose(2, 0, 1))
    wfi = f(inp["w_ffn_in"])
    g = wfi[:, :, :DFF].reshape(DEPTH, 8, 128, NJ, 128)
    u = wfi[:, :, DFF:].reshape(DEPTH, 8, 128, NJ, 128)
    gu = np.concatenate([g, u], axis=-1)
    shared["w_ffn_in"] = np.ascontiguousarray(gu.transpose(0, 2, 3, 1, 4))
    shared["w_ffn_out"] = np.ascontiguousarray(f(inp["w_ffn_out"]).reshape(DEPTH, NJ, 128, D).transpose(0, 2, 1, 3))
    maps = []
    for c in range(NCORES):
        g_, r_ = c // 4, c % 4
        toks = np.concatenate([xp[2 * c], xp[2 * c + 1], xs[g_, r_ * 1024:(r_ + 1) * 1024]], axis=0)
        m = dict(shared)
        m["xT"] = to_fm(toks)
        cnd = np.stack([f(inp["c_ctx"]), f(inp["c"])[g_]], axis=-1)
        m["condT"] = np.ascontiguousarray(cnd.reshape(8, 128, 2).transpose(1, 0, 2))
        maps.append(m)
    return maps


def _pm(w):
    return np.ascontiguousarray(w.reshape(8, 128, -1).transpose(1, 0, 2))


def prep_odd(inp, maps):
    f = lambda a: np.asarray(a, dtype=np.float32)
    w_in, w_out = f(inp["w_odd_in"]), f(inp["w_odd_out"])
    gbias, onorm = f(inp["mlstm_gate_bias"]), f(inp["mlstm_out_norm"])
    C0, n0, m0 = f(inp["state_mlstm_C"]), f(inp["state_mlstm_n"]), f(inp["state_mlstm_m"])

    def pack_in(j, heads):
        w = w_in[j]
        q = np.concatenate([w[:, h * 64:(h + 1) * 64] for h in heads], 1)
        kk = np.concatenate([w[:, 512 + h * 64:512 + (h + 1) * 64] for h in heads], 1)
        v = np.concatenate([w[:, 1024 + h * 128:1024 + (h + 1) * 128] for h in heads], 1)
        o = np.concatenate([w[:, 2048 + h * 128:2048 + (h + 1) * 128] for h in heads], 1)
        g = np.concatenate([w[:, 3072 + t * 8 + h:3072 + t * 8 + h + 1] for t in range(4) for h in heads], 1)
        return _pm(np.concatenate([q, kk, v, o, g], 1))

    def pack_out(j, heads):
        return np.ascontiguousarray(np.stack([w_out[j, h * 128:(h + 1) * 128, :] for h in heads], 1))

    def pack_gb(j, heads):
        row = np.concatenate([gbias[j, t, heads] for t in range(4)])
        return np.ascontiguousarray(np.tile(row[None, :], (64, 1)))

    def pack_on(j, heads):
        return np.ascontiguousarray(np.stack([onorm[j, h * 128:(h + 1) * 128] for h in heads], 1))

    allh = list(range(8))
    shared = {
        "wo_in_p": np.stack([pack_in(j, allh) for j in range(2)]),
        "wo_out_p": np.stack([pack_out(j, allh) for j in range(2)]),
        "ogb_p": np.stack([pack_gb(j, allh) for j in range(2)]),
        "onorm_p": np.stack([pack_on(j, allh) for j in range(2)]),
    }
    per_r = {}
    for r in range(4):
        hs = [2 * r, 2 * r + 1]
        per_r[r] = {
            "wo_in_s": np.stack([pack_in(j, hs) for j in range(2)]),
            "wo_out_s": np.stack([pack_out(j, hs) for j in range(2)]),
            "ogb_s": np.stack([pack_gb(j, hs) for j in range(2)]),
            "onorm_s": np.stack([pack_on(j, hs) for j in range(2)]),
        }
    for c in range(NCORES):
        g_, r_ = c // 4, c % 4
        hs = [2 * r_, 2 * r_ + 1]
        m = maps[c]
        m.update(shared)
        m.update(per_r[r_])
        Cs = C0[g_][:, :, hs]
        ns = n0[g_][:, :, hs]
        aug = np.concatenate([Cs, ns[..., None]], -1)
        m["oC0_s"] = np.ascontiguousarray(aug.transpose(0, 3, 1, 2, 4)[:, :, :, :, None, :])
        m["om0_s"] = np.ascontiguousarray(m0[g_][:, :, hs][:, None, :, :, None])

_ROPE_PERM = np.array([(r // 16) * 16 + ((r % 16) + 8) % 16 for r in range(32)])
_ROPE_SIGN = np.array([-1.0 if (r % 16) < 8 else 1.0 for r in range(32)], np.float32)


def _rope_tables():
    t = np.arange(4096)
    row, col = (t // 64).astype(np.float32), (t % 64).astype(np.float32)
    inv = (1.0 / (10000.0 ** (np.arange(0, 16, 2, dtype=np.float32) / 16))).astype(np.float32)
    ar, ac = row[:, None] * inv, col[:, None] * inv
    ang = np.concatenate([ar, ar, ac, ac], -1)
    cos = np.zeros((96, 4096), np.float32)
    sin = np.zeros((96, 4096), np.float32)
    cos[64:] = np.cos(ang).T
    sin[64:] = (np.sin(ang) * _ROPE_SIGN[None, :]).T
    return cos, sin


def prep_even(inp, maps):
    f = lambda a: np.asarray(a, dtype=np.float32)
    w_in, w_out = f(inp["w_even_in"]), f(inp["w_even_out"])
    w_uq, w_ukv = f(inp["w_mla_uq"]), f(inp["w_mla_ukv"])
    conv, alog, dtb = f(inp["gdn_conv"]), f(inp["gdn_a_log"]), f(inp["gdn_dt_bias"])
    qn, kn = f(inp["mla_q_norm"]), f(inp["mla_k_norm"])

    def pack_in(j, heads):
        w = w_in[j]
        z64 = np.zeros((1024, 64), np.float32)
        kr = w[:, 640:672]
        parts = [w[:, 0:384], w[:, 384:640], z64, kr, z64, kr[:, _ROPE_PERM]]
        for T_ in range(4):
            parts += [w[:, 672 + T_ * 512 + h * 64:672 + T_ * 512 + (h + 1) * 64] for h in heads]
        for gbase in (2720, 2736):
            parts += [w[:, gbase + d * 8 + h:gbase + d * 8 + h + 1] for d in range(2) for h in heads]
        return _pm(np.concatenate(parts, 1))

    def pack_uq(j, heads):
        out = np.zeros((384, len(heads), 192), np.float32)
        for i, h in enumerate(heads):
            out[:, i, 0:96] = w_uq[j][:, h * 96:(h + 1) * 96]
            out[:, i, 160:192] = w_uq[j][:, h * 96 + 64 + _ROPE_PERM]
        return np.ascontiguousarray(out.reshape(3, 128, len(heads), 192).transpose(1, 0, 2, 3))

    def pack_uk(j, heads):
        out = np.stack([w_ukv[j][:, h * 128:h * 128 + 64] for h in heads], 1)
        return np.ascontiguousarray(out.reshape(2, 128, len(heads), 64).transpose(1, 0, 2, 3))

    def pack_uv(j, heads):
        out = np.concatenate([w_ukv[j][:, h * 128 + 64:(h + 1) * 128] for h in heads], 1)
        return np.ascontiguousarray(out.reshape(2, 128, -1).transpose(1, 0, 2))

    def pack_out(j, heads):
        a = np.stack([w_out[j][h * 64:(h + 1) * 64] for h in heads], 1)
        dlt = np.stack([w_out[j][512 + h * 64:512 + (h + 1) * 64] for h in heads], 1)
        return np.ascontiguousarray(np.stack([a, dlt], 1))

    def pack_conv(j, heads):
        out = np.zeros((64, 3, len(heads), 3), np.float32)
        for T_ in range(3):
            for i, h in enumerate(heads):
                out[:, T_, i, :] = conv[j][:, T_ * 512 + h * 64:T_ * 512 + (h + 1) * 64].T
        return out

    def rep(v, heads):
        row = np.concatenate([v[d, heads] for d in range(2)])
        return np.ascontiguousarray(np.tile(row[None, :], (64, 1)))

    def packs(sfx, heads):
        return {
            "we_in_" + sfx: np.stack([pack_in(j, heads) for j in range(2)]),
            "we_uq_" + sfx: np.stack([pack_uq(j, heads) for j in range(2)]),
            "we_uk_" + sfx: np.stack([pack_uk(j, heads) for j in range(2)]),
            "we_uv_" + sfx: np.stack([pack_uv(j, heads) for j in range(2)]),
            "we_out_" + sfx: np.stack([pack_out(j, heads) for j in range(2)]),
            "we_conv_" + sfx: np.stack([pack_conv(j, heads) for j in range(2)]),
            "we_alog_" + sfx: np.stack([rep(alog[j], heads) for j in range(2)]),
            "we_dtb_" + sfx: np.stack([rep(dtb[j], heads) for j in range(2)]),
        }
    shared = packs("p", list(range(8)))
    shared["we_qan"] = np.ascontiguousarray(f(inp["mla_q_a_norm"]).reshape(2, 3, 128).transpose(0, 2, 1))
    shared["we_kvan"] = np.ascontiguousarray(f(inp["mla_kv_a_norm"]).reshape(2, 2, 128).transpose(0, 2, 1))

    def npk(g):
        out = np.zeros((2, 96, 2), np.float32)
        out[:, :, 0] = g
        out[:, 64:, 1] = g[:, 64 + _ROPE_PERM]
        return out
    shared["we_qn"], shared["we_kn"] = npk(qn), npk(kn)
    shared["we_onorm"] = np.ascontiguousarray(f(inp["gdn_out_norm"])[:, :, None])
    cos, sin = _rope_tables()
    shared["rope_cos"], shared["rope_sin"] = cos, sin
    per_r = {r: packs("s", [2 * r, 2 * r + 1]) for r in range(4)}
    cckv, ckr, sg = f(inp["cache_mla_ckv"]), f(inp["cache_mla_krope"]), f(inp["state_gdn"])
    for c in range(NCORES):
        g_, r_ = c // 4, c % 4
        hs = [2 * r_, 2 * r_ + 1]
        m = maps[c]
        m.update(shared)
        m.update(per_r[r_])
        m["we_ckvctx"] = np.ascontiguousarray(cckv[g_].transpose(0, 2, 1).reshape(2, 2, 128, 256).transpose(0, 2, 1, 3))
        kc_ = np.zeros((2, 96, 256), np.float32)
        kc_[:, 64:, :] = ckr[g_].transpose(0, 2, 1)
        m["we_krctx"] = kc_
        S = sg[g_][:, :, hs]
        m["we_S0"] = np.ascontiguousarray(S.transpose(0, 3, 1, 2, 4)[:, :, :, :, None, :])


_NC = None


def kernel(**inputs):
    global _NC
    maps = prep_inputs(inputs)
    prep_odd(inputs, maps)
    prep_even(inputs, maps)
    if _NC is None:
        _NC = build_program()
    res = run_bass_kernel_spmd(_NC, maps, core_ids=list(range(NCORES)))
    if DEBUG:
        kernel.res = res
    yp = np.zeros((16, 256, D), np.float32)
    ys = np.zeros((2, 4096, D), np.float32)
    for c in range(NCORES):
        y = from_fm(res.results[c]["yT"])
        yp[2 * c] = y[0:256]
        yp[2 * c + 1] = y[256:512]
        ys[c // 4, (c % 4) * 1024:(c % 4 + 1) * 1024] = y[512:1536]
    z = lambda *sh: np.zeros(sh, np.float32)
    mC, mn, mm_ = z(16, 2, 2, 8, 64, 128), z(16, 2, 2, 8, 64), z(16, 2, 2, 8)
    for c in range(NCORES):
        oc = res.results[c]["oC_out"]
        om = res.results[c]["om_out"]
        for s_ in range(2):
            a = oc[:, :, :, :, s_, :].transpose(0, 2, 3, 1, 4)
            mC[2 * c + s_] = a[..., :128]
            mn[2 * c + s_] = a[..., 128]
            mm_[2 * c + s_] = om[:, 0, :, :, s_]
    ockv, okr, ogdn = z(16, 2, 256, 256), z(16, 2, 256, 32), z(16, 2, 2, 8, 64, 64)
    for c in range(NCORES):
        ck = res.results[c]["ckv_out"]
        kr = res.results[c]["kr_out"]
        gd = res.results[c]["gdn_out"]
        for s_ in range(2):
            ockv[2 * c + s_] = ck[:, :, :, s_ * 256:(s_ + 1) * 256].transpose(0, 3, 2, 1).reshape(2, 256, 256)
            okr[2 * c + s_] = kr[:, :, s_ * 256:(s_ + 1) * 256].transpose(0, 2, 1)
            ogdn[2 * c + s_] = gd[:, :, :, :, s_, :].transpose(0, 2, 3, 1, 4)
    return (yp, ys, ockv, okr, ogdn, mC, mn, mm_)
```
